# Optimizing a Trainium2 kernel written in Bass

```python
import jax, jax.numpy as jnp
from jax import lax
import numpy as np

D_MODEL = 2048
BATCH = 2
SEQ = 8192
DEPTH = 1
DEC_BATCH = 2
DEC_SEQ = 4096
PAST_LEN = 128

MIX_WIDTH = D_MODEL
CONV_WIDTH = MIX_WIDTH // 2
CONV_KERNEL = 31
CONV_PAD = CONV_KERNEL // 2
HG_WIDTH = MIX_WIDTH - CONV_WIDTH
HG_HEADS = 8
HG_DK = HG_WIDTH // HG_HEADS
HG_DV = HG_WIDTH // HG_HEADS
D_FF = 5632
CHUNK = 64
NORM_EPS = 1e-6
SPLIT_SIZES = (CONV_WIDTH, CONV_WIDTH, HG_WIDTH, HG_WIDTH, HG_WIDTH, HG_WIDTH, HG_WIDTH)
SPLIT_IDX = tuple(int(v) for v in np.cumsum(SPLIT_SIZES)[:-1])
IN_COLS = int(sum(SPLIT_SIZES))

kernel_name = "hymba_conformer_hgrn2_macaron_encoder"


def rms_norm(x, w):
    xf = x.astype(jnp.float32)
    y = xf * lax.rsqrt(jnp.mean(xf * xf, axis=-1, keepdims=True) + NORM_EPS)
    return (y * w.astype(jnp.float32)).astype(x.dtype)


def layer_norm(x, w, b):
    xf = x.astype(jnp.float32)
    mu = jnp.mean(xf, axis=-1, keepdims=True)
    xc = xf - mu
    y = xc * lax.rsqrt(jnp.mean(xc * xc, axis=-1, keepdims=True) + NORM_EPS)
    return (y * w.astype(jnp.float32) + b.astype(jnp.float32)).astype(x.dtype)


def swiglu(x, w1, w3, w2):
    return (jax.nn.silu(x @ w1) * (x @ w3)) @ w2


def depthwise_conv(x, w, b):
    c = x.shape[-1]
    y = lax.conv_general_dilated(
        x, w[:, None, :].astype(x.dtype), window_strides=(1,), padding=[(CONV_PAD, CONV_PAD)],
        dimension_numbers=('NWC', 'WIO', 'NWC'), feature_group_count=c)
    return y + b.astype(y.dtype)


def gla_scan(q, k, v, g):
    B, L, H, K = q.shape
    V = v.shape[-1]
    n = L // CHUNK

    def to_chunks(t):
        return t.reshape(B, n, CHUNK, H, t.shape[-1]).transpose(1, 0, 3, 2, 4)

    lower = jnp.tril(jnp.ones((CHUNK, CHUNK), dtype=bool))[:, :, None]

    def step(S, inp):
        qc, kc, vc, gc = inp
        b = jnp.cumsum(gc, axis=2)
        o_inter = jnp.einsum('bhtk,bhkv->bhtv', qc * jnp.exp(b), S)
        diff = jnp.where(lower, b[:, :, :, None, :] - b[:, :, None, :, :], -jnp.inf)
        scores = jnp.einsum('bhtk,bhtsk,bhsk->bhts', qc, jnp.exp(diff), kc)
        o_intra = jnp.einsum('bhts,bhsv->bhtv', scores, vc)
        b_last = b[:, :, -1, :]
        S = S * jnp.exp(b_last)[..., None] + jnp.einsum(
            'bhsk,bhsv->bhkv', kc * jnp.exp(b_last[:, :, None, :] - b), vc)
        return S, o_inter + o_intra

    S0 = jnp.zeros((B, H, K, V), jnp.float32)
    _, o = lax.scan(step, S0, (to_chunks(q), to_chunks(k), to_chunks(v), to_chunks(g)))
    return o.transpose(1, 0, 3, 2, 4).reshape(B, L, H, V)


def hgrn2_bidirectional(q_raw, i_raw, zf_fwd, zf_bwd, lb_f, lb_b):
    B, L, _ = q_raw.shape
    q = jax.nn.silu(q_raw.astype(jnp.float32)).reshape(B, L, HG_HEADS, HG_DK)
    v = i_raw.astype(jnp.float32).reshape(B, L, HG_HEADS, HG_DV)

    def gates(z, lb):
        f = lb + (1.0 - lb) * jax.nn.sigmoid(z.astype(jnp.float32))
        return ((1.0 - f).reshape(B, L, HG_HEADS, HG_DK),
                jnp.log(f).reshape(B, L, HG_HEADS, HG_DK))

    k_f, g_f = gates(zf_fwd, lb_f)
    k_b, g_b = gates(zf_bwd, lb_b)
    o_f = gla_scan(q, k_f, v, g_f)
    flip = lambda t: t[:, ::-1]
    o_b = flip(gla_scan(flip(q), flip(k_b), flip(v), flip(g_b)))
    return o_f + o_b


def encoder(x, ffn1_norm, ffn1_w1, ffn1_w3, ffn1_w2, mix_norm, w_in, conv_w, conv_b,
            conv_ln_w, conv_ln_b, lb_fwd, lb_bwd, hg_norm, w_out,
            ffn2_norm, ffn2_w1, ffn2_w3, ffn2_w2, final_norm):
    B, L, _ = x.shape
    lbf_all = jnp.cumsum(jax.nn.softmax(lb_fwd.astype(jnp.float32), axis=0), axis=0)
    lbb_all = jnp.cumsum(jax.nn.softmax(lb_bwd.astype(jnp.float32), axis=0), axis=0)
    for l in range(DEPTH):
        x = x + 0.5 * swiglu(rms_norm(x, ffn1_norm[l]), ffn1_w1[l], ffn1_w3[l], ffn1_w2[l])

        h = rms_norm(x, mix_norm[l])
        proj = h @ w_in[l]
        a, ga, q, zf, zb, iv, og = jnp.split(proj, SPLIT_IDX, axis=-1)

        u = a * jax.nn.sigmoid(ga)
        u = depthwise_conv(u, conv_w[l], conv_b[l])
        u = jax.nn.silu(layer_norm(u, conv_ln_w[l], conv_ln_b[l]))

        o = hgrn2_bidirectional(q, iv, zf, zb, lbf_all[l], lbb_all[l])
        o = o * lax.rsqrt(jnp.mean(o * o, axis=-1, keepdims=True) + NORM_EPS) * hg_norm[l].astype(jnp.float32)
        o = (o.reshape(B, L, HG_WIDTH) * jax.nn.silu(og.astype(jnp.float32))).astype(x.dtype)

        x = x + jnp.concatenate([u.astype(x.dtype), o], axis=-1) @ w_out[l]

        x = x + 0.5 * swiglu(rms_norm(x, ffn2_norm[l]), ffn2_w1[l], ffn2_w3[l], ffn2_w2[l])
    return rms_norm(x, final_norm)


def setup_inputs(seed: int = 0) -> dict:
    key = jax.random.key(seed)
    ks = jax.random.split(key, 24)
    f32 = jnp.float32
    nrm = lambda k, shape, scale: jax.random.normal(k, shape, f32) * scale
    gain = lambda k, shape: 1.0 + 0.02 * jax.random.normal(k, shape, f32)
    return {
        "x_prompt": nrm(ks[0], (BATCH, SEQ, D_MODEL), 1.0),
        "x_sample": nrm(ks[1], (DEC_BATCH, DEC_SEQ, D_MODEL), 1.0),
        "ffn1_norm": gain(ks[2], (DEPTH, D_MODEL)),
        "ffn1_w1": nrm(ks[3], (DEPTH, D_MODEL, D_FF), D_MODEL ** -0.5),
        "ffn1_w3": nrm(ks[4], (DEPTH, D_MODEL, D_FF), D_MODEL ** -0.5),
        "ffn1_w2": nrm(ks[5], (DEPTH, D_FF, D_MODEL), D_FF ** -0.5),
        "mix_norm": gain(ks[6], (DEPTH, D_MODEL)),
        "w_in": nrm(ks[7], (DEPTH, D_MODEL, IN_COLS), D_MODEL ** -0.5),
        "conv_w": nrm(ks[8], (DEPTH, CONV_KERNEL, CONV_WIDTH), CONV_KERNEL ** -0.5),
        "conv_b": nrm(ks[9], (DEPTH, CONV_WIDTH), 0.02),
        "conv_ln_w": gain(ks[10], (DEPTH, CONV_WIDTH)),
        "conv_ln_b": nrm(ks[11], (DEPTH, CONV_WIDTH), 0.02),
        "lb_fwd": nrm(ks[12], (DEPTH + 1, HG_WIDTH), 0.1),
        "lb_bwd": nrm(ks[13], (DEPTH + 1, HG_WIDTH), 0.1),
        "hg_norm": gain(ks[14], (DEPTH, HG_DV)),
        "w_out": nrm(ks[15], (DEPTH, MIX_WIDTH, D_MODEL), MIX_WIDTH ** -0.5),
        "ffn2_norm": gain(ks[16], (DEPTH, D_MODEL)),
        "ffn2_w1": nrm(ks[17], (DEPTH, D_MODEL, D_FF), D_MODEL ** -0.5),
        "ffn2_w3": nrm(ks[18], (DEPTH, D_MODEL, D_FF), D_MODEL ** -0.5),
        "ffn2_w2": nrm(ks[19], (DEPTH, D_FF, D_MODEL), D_FF ** -0.5),
        "final_norm": gain(ks[20], (D_MODEL,)),
    }


def reference(x_prompt, x_sample, ffn1_norm, ffn1_w1, ffn1_w3, ffn1_w2, mix_norm, w_in, conv_w, conv_b,
              conv_ln_w, conv_ln_b, lb_fwd, lb_bwd, hg_norm, w_out,
              ffn2_norm, ffn2_w1, ffn2_w3, ffn2_w2, final_norm):
    y_prompt = encoder(x_prompt, ffn1_norm, ffn1_w1, ffn1_w3, ffn1_w2, mix_norm, w_in, conv_w, conv_b,
                       conv_ln_w, conv_ln_b, lb_fwd, lb_bwd, hg_norm, w_out,
                       ffn2_norm, ffn2_w1, ffn2_w3, ffn2_w2, final_norm)
    y_sample = encoder(x_sample, ffn1_norm, ffn1_w1, ffn1_w3, ffn1_w2, mix_norm, w_in, conv_w, conv_b,
                       conv_ln_w, conv_ln_b, lb_fwd, lb_bwd, hg_norm, w_out,
                       ffn2_norm, ffn2_w1, ffn2_w3, ffn2_w2, final_norm)
    return (y_prompt, y_sample)
```

```python
import numpy as np
import concourse.bass as bass
import concourse.mybir as mybir
from concourse.bass_utils import run_bass_kernel_spmd

F32 = mybir.dt.float32
BF16 = mybir.dt.bfloat16
AF = mybir.ActivationFunctionType
ALU = mybir.AluOpType

SAME_ENGINE_SYNC = True
NDMASEM = 14
EPS = 1e-6
D = 2048
DFF = 5632
T = 512


class Buf:
    __slots__ = ("name", "w", "r")

    def __init__(self, name):
        self.name = name
        self.w = None
        self.r = []


class Op:
    __slots__ = ("eng", "emit", "deps", "dma", "idx", "sig", "cnt", "sem", "semval", "prevdma", "inc")


class Prog:
    ENGS = ("pe", "act", "dve", "pool", "sp")

    def __init__(self, nc, dry=False):
        self.nc = nc
        self.dry = dry
        self.ops = []
        self.by_eng = {e: [] for e in self.ENGS}
        self.ndma = {e: 0 for e in self.ENGS}
        self.dma_ops = {e: [] for e in self.ENGS}
        self.bufs = {}

    def buf(self, name):
        b = self.bufs.get(name)
        if b is None:
            b = self.bufs[name] = Buf(name)
        return b

    def handoff(self, old, new):
        if self.dry:
            return
        ops = []
        for n in old:
            b = self.buf(n)
            if b.w is not None:
                ops.append(b.w)
            ops.extend(b.r)
            b.w = None
            b.r = []
        best = {}
        keep = []
        for o in ops:
            if o.dma:
                keep.append(o)
            else:
                c = best.get(o.eng)
                if c is None or o.idx > c.idx:
                    best[o.eng] = o
        keep.extend(best.values())
        for n in new:
            self.buf(n).r.extend(keep)

    def op(self, eng, emit, reads=(), writes=(), dma=False, inc=16):
        if self.dry:
            return None
        o = Op()
        o.eng = eng
        o.emit = emit
        o.dma = dma
        o.idx = len(self.ops)
        o.sig = False
        o.cnt = None
        o.sem = None
        o.semval = None
        o.prevdma = None
        o.inc = inc
        deps = set()
        for b in reads:
            b = self.buf(b)
            if b.w is not None:
                deps.add(b.w)
            b.r.append(o)
        for b in writes:
            b = self.buf(b)
            if b.w is not None:
                deps.add(b.w)
            for r in b.r:
                if r is not o:
                    deps.add(r)
            b.w = o
            b.r = []
        deps.discard(o)
        best = {}
        keep = []
        for d in deps:
            if d.dma:
                keep.append(d)
            else:
                if d.eng == eng and not dma and (eng == "pe" or not SAME_ENGINE_SYNC):
                    continue
                cur = best.get(d.eng)
                if cur is None or d.idx > cur.idx:
                    best[d.eng] = d
        keep.extend(best.values())
        o.deps = keep
        for d in keep:
            d.sig = True
        if dma:
            k = self.ndma[eng]
            self.ndma[eng] += 1
            lst = self.dma_ops[eng]
            o.sem = k % NDMASEM
            prev = lst[k - NDMASEM].semval if k >= NDMASEM else 0
            o.semval = prev + inc
            if k >= NDMASEM:
                o.prevdma = lst[k - NDMASEM]
            lst.append(o)
        self.ops.append(o)
        self.by_eng[eng].append(o)
        return o

    def finalize(self, final_waits=()):
        nc = self.nc
        for e in self.ENGS:
            c = 0
            for o in self.by_eng[e]:
                if o.sig and not o.dma:
                    c += 1
                    o.cnt = c
        esem = {e: nc.alloc_semaphore(name=f"es_{e}") for e in self.ENGS}
        dsem = {e: [nc.alloc_semaphore(name=f"ds_{e}_{i}") for i in range(NDMASEM)]
                for e in ("sp", "pool", "act") if self.ndma[e] > 0}

        def run(e, engobj):
            waited = {}
            dwaited = {}
            for o in self.by_eng[e]:
                deps = list(o.deps)
                if o.prevdma is not None:
                    deps.append(o.prevdma)
                for d in deps:
                    if d.dma:
                        key = (d.eng, d.sem)
                        if dwaited.get(key, 0) >= d.semval:
                            continue
                        dwaited[key] = d.semval
                        engobj.wait_ge(dsem[d.eng][d.sem], d.semval)
                    else:
                        if waited.get(d.eng, 0) >= d.cnt:
                            continue
                        waited[d.eng] = d.cnt
                        engobj.wait_ge(esem[d.eng], d.cnt)
                ins = o.emit(engobj)
                if o.dma:
                    ins.then_inc(dsem[e][o.sem], o.inc)
                elif o.sig:
                    ins.then_inc(esem[e], 1)
            if e == "sp":
                for d in final_waits:
                    key = (d.eng, d.sem)
                    if dwaited.get(key, 0) >= d.semval:
                        continue
                    dwaited[key] = d.semval
                    engobj.wait_ge(dsem[d.eng][d.sem], d.semval)

        with nc.Block() as block:
            @block.tensor
            def _(eng):
                run("pe", eng)

            @block.scalar
            def _(eng):
                run("act", eng)

            @block.vector
            def _(eng):
                run("dve", eng)

            @block.gpsimd
            def _(eng):
                run("pool", eng)

            @block.sync
            def _(eng):
                run("sp", eng)


class WeightRing:
    NSLOT = 4
    LOOKAHEAD = 2

    def __init__(self, P, slots, order):
        self.P = P
        self.slots = slots
        self.order = order
        self.rec = []
        self.next_load = 0
        self.cur = 0

    def get(self, key, src, shape):
        if self.order is None:
            self.rec.append((key, src, shape))
            k = len(self.rec) - 1
        else:
            k = self.cur
            assert self.order[k][0] == key, (self.order[k][0], key)
            self.cur += 1
            lim = min(len(self.order), k + self.LOOKAHEAD + 1)
            while self.next_load < lim:
                self._load(self.next_load)
                self.next_load += 1
        s = k % self.NSLOT
        n = 1
        for d in shape[1:]:
            n *= d
        v = self.slots[s][:, 0:n]
        if len(shape) == 3:
            v = v.rearrange("p (a b) -> p a b", b=shape[2])
        return v, f"wslot{s}"

    def _load(self, k):
        key, src, shape = self.order[k]
        s = k % self.NSLOT
        n = 1
        for d in shape[1:]:
            n *= d
        v = self.slots[s][:, 0:n]
        if len(shape) == 3:
            v = v.rearrange("p (a b) -> p a b", b=shape[2])
        self.P.op("pool", lambda e, v=v, src=src: e.dma_start(out=v, in_=src), writes=[f"wslot{s}"], dma=True)


WNAMES = [
    ("ffn1_norm", [1, D]), ("ffn1_w1", [1, D, DFF]), ("ffn1_w3", [1, D, DFF]), ("ffn1_w2", [1, DFF, D]),
    ("mix_norm", [1, D]), ("w_in", [1, D, 7168]), ("conv_w", [1, 31, 1024]), ("conv_b", [1, 1024]),
    ("conv_ln_w", [1, 1024]), ("conv_ln_b", [1, 1024]), ("lb_fwd", [2, 1024]), ("lb_bwd", [2, 1024]),
    ("hg_norm", [1, 128]), ("w_out", [1, D, D]), ("ffn2_norm", [1, D]), ("ffn2_w1", [1, D, DFF]),
    ("ffn2_w3", [1, D, DFF]), ("ffn2_w2", [1, DFF, D]), ("final_norm", [D]),
]


def build(NT, NTu=0):
    nc = bass.Bass("TRN2", target_bir_lowering=False)
    Lc = NT * T
    x_d = nc.dram_tensor("x", [Lc, D], F32, kind="ExternalInput").ap()
    xu_d = nc.dram_tensor("xu", [NTu * T, D], F32, kind="ExternalInput").ap() if NTu else None
    W = {n: nc.dram_tensor(n, s, F32, kind="ExternalInput").ap() for n, s in WNAMES}
    y_d = nc.dram_tensor("y", [Lc, D], F32, kind="ExternalOutput").ap()
    x1_d = nc.dram_tensor("x1_d", [Lc, D], F32).ap()
    upre_d = nc.dram_tensor("upre_d", [1024, Lc + 32], BF16).ap()
    q_d = nc.dram_tensor("q_d", [1024, Lc], F32).ap()
    zb_d = nc.dram_tensor("zb_d", [1024, Lc], F32).ap()
    og_d = nc.dram_tensor("og_d", [1024, Lc], F32).ap()
    of_d = nc.dram_tensor("of_d", [1024, Lc], F32).ap()
    v_d = nc.dram_tensor("v_d", [Lc, 1024], BF16).ap()

    sb = nc.alloc_sbuf_tensor
    xt = sb("xt", [128, 4, D], F32)
    hT = sb("hT", [128, 16, T], BF16)
    hid = sb("hid", [128, 44, T], BF16)
    slots = [sb(f"wslot{i}", [128, 8192], BF16) for i in range(WeightRing.NSLOT)]
    sg = [sb(f"sg{i}", [128, T], F32) for i in range(2)]
    vtok = sb("vtok", [128, 4, 1024], BF16)
    stage = sb("stage", [128, 128], F32)
    cols = sb("cols", [128, 128], F32)
    identF = sb("identF", [128, 128], F32)
    identB = sb("identB", [128, 128], BF16)
    onesF = sb("onesF", [128, 128], F32)
    cwrow = sb("cwrow", [31, 1024], F32)
    cw = sb("cw", [128, 8, 31], F32)
    lbv = sb("lbv", [128, 16], F32)
    oml = sb("oml", [128, 16], F32)
    noml = sb("noml", [128, 16], F32)
    lbd = sb("lbd", [128, 16], F32)
    maskf = sb("maskf", [128, T], F32)
    maskb = sb("maskb", [128, T], F32)
    Mf = sb("Mf", [128, T], F32)
    Mb = sb("Mb", [128, T], F32)
    fwb = sb("fwb", [128, D], F32)
    zpad = sb("zpad", [128, 8, 16], BF16)
    ss = sb("ss", [128, 4], F32)
    epsc = sb("epsc", [128, 1], F32)
    rstd = sb("rstd", [128, 4], F32)
    S32 = [sb(f"S32_{d}", [128, 8, 128], F32) for d in range(2)]
    Sbf = [sb(f"Sbf_{d}", [128, 8, 128], BF16) for d in range(2)]
    Ssc = sb("Ssc", [128, 4, 128], F32)
    ebe = sb("ebe", [128, 8, 8], F32)
    B = [nc.alloc_psum_tensor(f"B{i}", [128, T], F32) for i in range(8)]

    hidflat = hid[:].rearrange("p a b -> p (a b)")

    def arena_f32(off_bytes, n):
        return hidflat[:, off_bytes // 2: off_bytes // 2 + 2 * n].bitcast(F32)

    def arena_bf16(off_bytes, n):
        return hidflat[:, off_bytes // 2: off_bytes // 2 + n]

    xn = hid[:, 0:16, :].rearrange("p (t a) f -> p t (a f)", t=4)
    KB = 1024
    TS = [[arena_f32((s * 7 + i) * 2 * KB, T) for i in range(7)] for s in range(2)]
    qt = [arena_bf16(28 * KB + i * KB, T) for i in range(4)]
    ktok = [arena_bf16(32 * KB + i * KB, T).rearrange("p (g k) -> p g k", k=128) for i in range(4)]
    scT = [arena_bf16(36 * KB + i * KB, T) for i in range(2)]
    ktb = [arena_bf16(38 * KB + i * KB, T) for i in range(2)]
    osb = [arena_f32(40 * KB + i * 2 * KB, T) for i in range(2)]
    upad = arena_bf16(0, 8 * 542).rearrange("p (c t) -> p c t", t=542)
    uc = arena_f32(8704, 8 * T).rearrange("p (c t) -> p c t", t=T)
    cst = [arena_f32(25088 + i * 2048, T) for i in range(5)]
    dg = [arena_bf16(35328 + i * 256, 128) for i in range(4)]
    ARENA_XN = ["xn0", "xn1", "xn2", "xn3"]
    ARENA_HID = [f"hid{j}" for j in range(44)]
    ARENA_SCAN = [f"TS{s}_{i}" for s in range(2) for i in range(7)] + [f"qt{i}" for i in range(4)] + \
                 [f"ktok{i}" for i in range(4)] + ["scT0", "scT1", "ktb0", "ktb1", "osb0", "osb1"]
    ARENA_CONV = ["upad"] + [f"uc{c}" for c in range(8)] + [f"cst{i}" for i in range(5)] + [f"dg{i}" for i in range(4)]

    Bt = [b[:, 0:256].bitcast(BF16) for b in B]

    def make(P, order):
        Wr = WeightRing(P, slots, order)
        op = P.op
        cnt = [0]

        def alt():
            cnt[0] += 1
            return "act" if cnt[0] % 2 else "dve"

        def scale_copy(eng, out, in_, scal, reads, writes):
            if eng == "act":
                op("act", lambda e: e.activation(out=out, in_=in_, func=AF.Copy, scale=scal), reads, writes)
            else:
                op("dve", lambda e: e.tensor_scalar(out=out, in0=in_, scalar1=scal, scalar2=None, op0=ALU.mult),
                   reads, writes)

        def copy(eng, out, in_, reads, writes):
            if eng == "act":
                op("act", lambda e: e.activation(out=out, in_=in_, func=AF.Copy), reads, writes)
            else:
                op("dve", lambda e: e.tensor_copy(out=out, in_=in_), reads, writes)

        def setup():
            op("dve", lambda e: e.memset(stage[:], 0.0), writes=["stage"])
            op("dve", lambda e: e.memset(epsc[:], EPS), writes=["epsc"])
            rows = [("ffn1_norm", 0, 16), ("mix_norm", 16, 16), ("ffn2_norm", 32, 16), ("conv_b", 48, 8),
                    ("conv_ln_w", 56, 8), ("conv_ln_b", 64, 8)]
            for n, r0, nr in rows:
                src = W[n][0].rearrange("(k p) -> k p", p=128)
                op("sp", lambda e, src=src, r0=r0, nr=nr: e.dma_start(out=stage[r0:r0 + nr, :], in_=src),
                   reads=[], writes=["stage"], dma=True)
            for n, r0 in (("lb_fwd", 72), ("lb_bwd", 88)):
                for s_ in range(2):
                    src = W[n][s_].rearrange("(k p) -> k p", p=128)
                    op("sp", lambda e, src=src, r=r0 + 8 * s_: e.dma_start(out=stage[r:r + 8, :], in_=src),
                       writes=["stage"], dma=True)
            op("sp", lambda e: e.dma_start(out=stage[104:105, :], in_=W["hg_norm"]), writes=["stage"], dma=True)
            op("sp", lambda e: e.dma_start(out=cwrow[:], in_=W["conv_w"][0]), writes=["cwrow"], dma=True)
            op("sp", lambda e: e.dma_start(out=fwb[:], in_=W["final_norm"].partition_broadcast(128)),
               writes=["fwb"], dma=True)
            op("pool", lambda e: e.memset(identF[:], 0.0), writes=["identF"])
            op("pool", lambda e: e.affine_select(out=identF[:], in_=identF[:], pattern=[[-1, 128]],
                                                 compare_op=ALU.not_equal, fill=1.0, base=0, channel_multiplier=1),
               reads=["identF"], writes=["identF"])
            op("dve", lambda e: e.tensor_copy(out=identB[:], in_=identF[:]), reads=["identF"], writes=["identB"])
            op("dve", lambda e: e.memset(onesF[:], 1.0), writes=["onesF"])
            op("dve", lambda e: e.memset(maskf[:], 1.0), writes=["maskf"])
            op("dve", lambda e: e.memset(maskf[:].rearrange("p (c t) -> p c t", t=64)[:, :, 0:1], 0.0),
               writes=["maskf"])
            op("dve", lambda e: e.memset(maskb[:], 1.0), writes=["maskb"])
            op("dve", lambda e: e.memset(maskb[:].rearrange("p (c t) -> p c t", t=64)[:, :, 63:64], 0.0),
               writes=["maskb"])
            op("pool", lambda e: e.memset(Mf[:], 1.0), writes=["Mf"])
            op("pool", lambda e: e.affine_select(out=Mf[:, 0:128], in_=Mf[:, 0:128], pattern=[[1, 128]],
                                                 compare_op=ALU.is_ge, fill=0.0, base=0, channel_multiplier=-1),
               reads=["Mf"], writes=["Mf"])
            op("pool", lambda e: e.memset(Mf[0:64, 64:128], 0.0), reads=["Mf"], writes=["Mf"])
            op("pool", lambda e: e.memset(Mb[:], 1.0), writes=["Mb"])
            op("pool", lambda e: e.affine_select(out=Mb[:, 0:128], in_=Mb[:, 0:128], pattern=[[-1, 128]],
                                                 compare_op=ALU.is_ge, fill=0.0, base=0, channel_multiplier=1),
               reads=["Mb"], writes=["Mb"])
            op("pool", lambda e: e.memset(Mb[64:128, 0:64], 0.0), reads=["Mb"], writes=["Mb"])
            for M_, nm in ((Mf, "Mf"), (Mb, "Mb")):
                for g in range(1, 4):
                    op("pool", lambda e, M_=M_, g=g: e.tensor_copy(out=M_[:, g * 128:(g + 1) * 128], in_=M_[:, 0:128]),
                       reads=[nm], writes=[nm])
            op("pe", lambda e: e.transpose(out=B[0][:, 0:128], in_=stage[:], identity=identF[:]),
               reads=["stage", "identF"], writes=["B0"])
            op("act", lambda e: e.activation(out=cols[:], in_=B[0][:, 0:128], func=AF.Copy), reads=["B0"], writes=["cols"])
            for c in range(8):
                op("pe", lambda e, c=c: e.transpose(out=B[1][:, 0:31], in_=cwrow[0:31, c * 128:(c + 1) * 128],
                                                    identity=identF[0:31, 0:31]),
                   reads=["cwrow", "identF"], writes=["B1"])
                op("dve", lambda e, c=c: e.tensor_copy(out=cw[:, c, :], in_=B[1][:, 0:31]), reads=["B1"], writes=["cw"])
            op("dve", lambda e: e.tensor_tensor(out=lbd[:, 0:8], in0=cols[:, 72:80], in1=cols[:, 80:88], op=ALU.subtract),
               reads=["cols"], writes=["lbd"])
            op("dve", lambda e: e.tensor_tensor(out=lbd[:, 8:16], in0=cols[:, 88:96], in1=cols[:, 96:104], op=ALU.subtract),
               reads=["cols"], writes=["lbd"])
            op("act", lambda e: e.activation(out=lbv[:], in_=lbd[:], func=AF.Sigmoid), reads=["lbd"], writes=["lbv"])
            op("dve", lambda e: e.tensor_scalar(out=oml[:], in0=lbv[:], scalar1=-1.0, scalar2=1.0, op0=ALU.mult, op1=ALU.add),
               reads=["lbv"], writes=["oml"])
            op("dve", lambda e: e.tensor_scalar(out=noml[:], in0=lbv[:], scalar1=-1.0, scalar2=None, op0=ALU.add),
               reads=["lbv"], writes=["noml"])
            op("dve", lambda e: e.memset(zpad[:], 0.0), writes=["zpad"])
            ur = upre_d.rearrange("(c p) l -> p c l", p=128)
            op("sp", lambda e: e.dma_start(out=ur[:, :, 0:16], in_=zpad[:]), reads=["zpad"], writes=["upre_pad0"], dma=True)
            if not NTu:
                op("sp", lambda e: e.dma_start(out=ur[:, :, 16 + Lc:32 + Lc], in_=zpad[:]), reads=["zpad"],
                   writes=["upre_pad1"], dma=True)
            for d in range(2):
                op("dve", lambda e, d=d: e.memset(S32[d][:], 0.0), writes=[f"S32_{d}_{h}" for h in range(8)])
                op("dve", lambda e, d=d: e.memset(Sbf[d][:], 0.0), writes=[f"Sbf_{d}_{h}" for h in range(8)])

        G1 = cols[:, 0:16]
        G2 = cols[:, 16:32]
        G3 = cols[:, 32:48]
        CB = cols[:, 48:56]
        LNW = cols[:, 56:64]
        LNB = cols[:, 64:72]
        HGW = cols[:, 104:105]

        def rmsnorm_hT(gcols):
            XT = [f"xt{tb}" for tb in range(4)]
            op("dve", lambda e: e.memset(ss[:], 0.0), writes=["ss"])
            for tb in range(4):
                op("act", lambda e, tb=tb: e.activation(out=xn[:, tb, :], in_=xt[:, tb, :], func=AF.Square,
                                                        accum_out=ss[:, tb:tb + 1]),
                   reads=[XT[tb], "ss"], writes=[f"xn{tb}", "ss"])
            op("act", lambda e: e.activation(out=rstd[:], in_=ss[:], func=AF.Sqrt, scale=1.0 / D, bias=epsc[:, 0:1]),
               reads=["ss", "epsc"], writes=["rstd"])
            op("dve", lambda e: e.reciprocal(out=rstd[:], in_=rstd[:]), reads=["rstd"], writes=["rstd"])
            for tb in range(4):
                scale_copy(alt(), xn[:, tb, :], xt[:, tb, :], rstd[:, tb:tb + 1], [XT[tb], "rstd"], [f"xn{tb}"])
            for kc in range(16):
                bk = kc % 2
                for tb in range(4):
                    op("pe", lambda e, kc=kc, tb=tb, bk=bk: e.transpose(
                        out=Bt[bk][:, tb * 128:(tb + 1) * 128], in_=xn[:, tb, kc * 128:(kc + 1) * 128], identity=identB[:]),
                       reads=[f"xn{tb}", "identB"], writes=[f"B{bk}"])
                scale_copy(alt(), hT[:, kc, :], Bt[bk][:, :], gcols[:, kc:kc + 1], [f"B{bk}", "cols"], [f"hT{kc}"])

        HT = [f"hT{kc}" for kc in range(16)]

        def ffn(pre):
            w1 = W[pre + "_w1"][0].rearrange("(kc p) f -> p kc f", p=128)
            w3 = W[pre + "_w3"][0].rearrange("(kc p) f -> p kc f", p=128)
            w2 = W[pre + "_w2"][0].rearrange("(kc p) f -> p kc f", p=128)
            P.handoff(ARENA_XN + ARENA_SCAN + ARENA_CONV, ARENA_HID)
            for blk in range(11):
                s1, n1 = Wr.get((pre, "w1", blk), w1[:, :, blk * 512:(blk + 1) * 512], [128, 16, 512])
                s3, n3 = Wr.get((pre, "w3", blk), w3[:, :, blk * 512:(blk + 1) * 512], [128, 16, 512])
                for j in range(4):
                    jj = blk * 4 + j
                    gb = 2 * (jj % 2)
                    ub = gb + 1
                    for kc in range(16):
                        op("pe", lambda e, kc=kc, j=j, gb=gb, s1=s1: e.matmul(
                            B[gb][:], s1[:, kc, j * 128:(j + 1) * 128], hT[:, kc, :], start=(kc == 0), stop=(kc == 15)),
                           reads=[n1, HT[kc]], writes=[f"B{gb}"])
                    for kc in range(16):
                        op("pe", lambda e, kc=kc, j=j, ub=ub, s3=s3: e.matmul(
                            B[ub][:], s3[:, kc, j * 128:(j + 1) * 128], hT[:, kc, :], start=(kc == 0), stop=(kc == 15)),
                           reads=[n3, HT[kc]], writes=[f"B{ub}"])
                    sgi = jj % 2
                    op("act", lambda e, gb=gb, sgi=sgi: e.activation(out=sg[sgi][:], in_=B[gb][:], func=AF.Silu),
                       reads=[f"B{gb}"], writes=[f"sg{sgi}"])
                    op("dve", lambda e, ub=ub, sgi=sgi, jj=jj: e.tensor_tensor(
                        out=hid[:, jj, :], in0=sg[sgi][:], in1=B[ub][:], op=ALU.mult),
                       reads=[f"sg{sgi}", f"B{ub}"], writes=[f"hid{jj}"])
            for n in range(4):
                for q in range(4):
                    s2, n2 = Wr.get((pre, "w2", n, q), w2[:, q * 11:(q + 1) * 11, n * 512:(n + 1) * 512], [128, 11, 512])
                    for tb in range(4):
                        for kc in range(11):
                            op("pe", lambda e, tb=tb, kc=kc, q=q, s2=s2: e.matmul(
                                B[4 + tb][:], hid[:, q * 11 + kc, tb * 128:(tb + 1) * 128], s2[:, kc, :],
                                start=(q == 0 and kc == 0), stop=(q == 3 and kc == 10)),
                               reads=[n2, f"hid{q * 11 + kc}"], writes=[f"B{4 + tb}"])
                for tb in range(4):
                    op("dve", lambda e, tb=tb, n=n: e.scalar_tensor_tensor(
                        out=xt[:, tb, n * 512:(n + 1) * 512], in0=B[4 + tb][:], scalar=0.5,
                        in1=xt[:, tb, n * 512:(n + 1) * 512], op0=ALU.mult, op1=ALU.add),
                       reads=[f"B{4 + tb}", f"xt{tb}"], writes=[f"xt{tb}"])

        def scan_group(d, hgp, zsrc, qsrc, finish, pidx=None, state_only=False):
            p = d if pidx is None else pidx
            Md, Mn = (Mf, "Mf") if d == 0 else (Mb, "Mb")
            mk, mkn = (maskf, "maskf") if d == 0 else (maskb, "maskb")
            srcs = {}
            for hi in range(4):
                h = hgp * 4 + hi
                st = hi % 2
                s_, g_, b_, eb_ = TS[st][0], TS[st][1], TS[st][2], TS[st][3]
                sN, gN, bN, ebN = (f"TS{st}_{i}" for i in range(4))
                if hi == 0:
                    srcs[0] = (zsrc(0), None if state_only else qsrc(0))
                (z, zN), qq = srcs[hi]
                if not state_only:
                    q_, qN = qq
                lc = p * 8 + h
                op("act", lambda e, s_=s_, z=z: e.activation(out=s_, in_=z, func=AF.Sigmoid), reads=[zN], writes=[sN])
                if hi + 1 < 4:
                    srcs[hi + 1] = (zsrc(hi + 1), None if state_only else qsrc(hi + 1))
                op("act", lambda e, s_=s_, g_=g_, lc=lc: e.activation(
                    out=g_, in_=s_, func=AF.Ln, scale=oml[:, lc:lc + 1], bias=lbv[:, lc:lc + 1]),
                   reads=[sN, "oml", "lbv"], writes=[gN])
                op("dve", lambda e, s_=s_, lc=lc: e.tensor_scalar(
                    out=s_, in0=s_, scalar1=noml[:, lc:lc + 1], scalar2=oml[:, lc:lc + 1], op0=ALU.mult, op1=ALU.add),
                   reads=[sN, "noml", "oml"], writes=[sN])
                if d == 0:
                    op("dve", lambda e, b_=b_, g_=g_: e.tensor_tensor_scan(
                        out=b_, data0=mk[:], data1=g_, initial=0.0, op0=ALU.mult, op1=ALU.add),
                       reads=[gN, mkn], writes=[bN])
                else:
                    op("dve", lambda e, b_=b_, g_=g_: e.tensor_tensor_scan(
                        out=b_[:, ::-1], data0=mk[:, ::-1], data1=g_[:, ::-1], initial=0.0, op0=ALU.mult, op1=ALU.add),
                       reads=[gN, mkn], writes=[bN])
                op("act", lambda e, b_=b_, eb_=eb_: e.activation(out=eb_, in_=b_, func=AF.Exp), reads=[bN], writes=[ebN])
                op("act", lambda e, b_=b_: e.activation(out=b_, in_=b_, func=AF.Exp, scale=-1.0), reads=[bN], writes=[bN])
                ecol = 63 if d == 0 else 0
                op("dve", lambda e, eb_=eb_, h=h, ecol=ecol: e.tensor_copy(
                    out=ebe[:, h, :], in_=eb_.rearrange("p (c t) -> p c t", t=64)[:, :, ecol]),
                   reads=[ebN], writes=[f"ebe{h}"])
                if not state_only:
                    op("dve", lambda e, hi=hi, q_=q_, eb_=eb_: e.tensor_tensor(out=qt[hi], in0=q_, in1=eb_, op=ALU.mult),
                       reads=[qN, ebN], writes=[f"qt{hi}"])
                op("dve", lambda e, st=st, s_=s_, b_=b_: e.tensor_tensor(out=ktb[st], in0=s_, in1=b_, op=ALU.mult),
                   reads=[sN, bN], writes=[f"ktb{st}"])
                if not state_only:
                    for g in range(4):
                        op("pe", lambda e, g=g, st=st, hi=hi: e.matmul(
                            B[2][:, g * 128:(g + 1) * 128], ktb[st][:, g * 128:(g + 1) * 128],
                            qt[hi][:, g * 128:(g + 1) * 128], start=True, stop=True),
                           reads=[f"ktb{st}", f"qt{hi}"], writes=["B2"])
                    op("dve", lambda e, st=st: e.tensor_tensor(out=scT[st], in0=B[2][:], in1=Md[:], op=ALU.mult),
                       reads=["B2", Mn], writes=[f"scT{st}"])
                for g in range(4):
                    op("pe", lambda e, g=g, st=st: e.transpose(
                        out=Bt[1][:, g * 128:(g + 1) * 128], in_=ktb[st][:, g * 128:(g + 1) * 128], identity=identB[:]),
                       reads=[f"ktb{st}", "identB"], writes=["B1"])
                op("act", lambda e, hi=hi: e.activation(out=ktok[hi].rearrange("p g k -> p (g k)"), in_=Bt[1][:, :],
                                                        func=AF.Copy), reads=["B1"], writes=[f"ktok{hi}"])
                if not state_only:
                    for g in range(4):
                        op("pe", lambda e, g=g, st=st, hi=hi, h=h: e.matmul(
                            B[4 + hi][:, g * 128:(g + 1) * 128], vtok[:, g, h * 128:(h + 1) * 128],
                            scT[st][:, g * 128:(g + 1) * 128], start=(g == 0), stop=False),
                           reads=["vtok", f"scT{st}"], writes=[f"B{4 + hi}"])
            corder = list(range(8)) if d == 0 else list(range(7, -1, -1))
            for c in corder:
                g = c // 2
                r0 = (c % 2) * 64
                for hi in range(4):
                    h = hgp * 4 + hi
                    op("pe", lambda e, hi=hi, h=h, g=g, r0=r0: e.matmul(
                        B[3][:, hi * 128:(hi + 1) * 128], ktok[hi][r0:r0 + 64, g, :], vtok[r0:r0 + 64, g, h * 128:(h + 1) * 128],
                        start=True, stop=True), reads=[f"ktok{hi}", "vtok"], writes=["B3"])
                for hi in range(4):
                    h = hgp * 4 + hi
                    if state_only:
                        break
                    op("pe", lambda e, c=c, hi=hi, h=h: e.matmul(
                        B[4 + hi][:, c * 64:(c + 1) * 64], Sbf[p][:, h, :], qt[hi][:, c * 64:(c + 1) * 64],
                        start=False, stop=True), reads=[f"Sbf_{p}_{h}", f"qt{hi}"], writes=[f"B{4 + hi}"])
                h0 = hgp * 4
                SN = [f"S32_{p}_{h0 + k}" for k in range(4)]
                SBN = [f"Sbf_{p}_{h0 + k}" for k in range(4)]
                EN = [f"ebe{h0 + k}" for k in range(4)]
                op("dve", lambda e, h0=h0: e.tensor_tensor(
                    out=Ssc[:], in0=B[3][:].rearrange("p (a b) -> p a b", b=128), in1=S32[p][:, h0:h0 + 4, :], op=ALU.add),
                   reads=["B3"] + SN, writes=["Ssc"])
                op("dve", lambda e, h0=h0, c=c: e.tensor_tensor(
                    out=S32[p][:, h0:h0 + 4, :], in0=Ssc[:], in1=ebe[:, h0:h0 + 4, c:c + 1].to_broadcast([128, 4, 128]),
                    op=ALU.mult), reads=["Ssc"] + EN, writes=SN)
                op("act", lambda e, h0=h0: e.activation(out=Sbf[p][:, h0:h0 + 4, :], in_=S32[p][:, h0:h0 + 4, :], func=AF.Copy),
                   reads=SN, writes=SBN)
            if not state_only:
                for hi in range(4):
                    finish(hi, hgp * 4 + hi, B[4 + hi], f"B{4 + hi}")

        def proj_chunk(bank, sv, sn, off):
            for kc in range(16):
                op("pe", lambda e, kc=kc: e.matmul(B[bank][:], sv[:, kc, off:off + 128], hT[:, kc, :],
                                                   start=(kc == 0), stop=(kc == 15)),
                   reads=[sn, HT[kc]], writes=[f"B{bank}"])

        win = W["w_in"][0].rearrange("(kc p) f -> p kc f", p=128)

        def wpiece(i):
            return Wr.get(("w_in", i), win[:, :, i * 512:(i + 1) * 512], [128, 16, 512])

        def fm(dram, h, tile):
            return dram[h * 128:(h + 1) * 128, tile * T:(tile + 1) * T]

        def phase_a(i):
            rows = slice(i * T, (i + 1) * T)
            for tb in range(4):
                op("sp", lambda e, tb=tb: e.dma_start(out=xt[:, tb, :], in_=x_d[i * T + tb * 128:i * T + (tb + 1) * 128, :]),
                   writes=[f"xt{tb}"], dma=True)
            P.handoff(ARENA_HID + ARENA_SCAN + ARENA_CONV, ARENA_XN)
            rmsnorm_hT(G1)
            ffn("ffn1")
            for tb in range(4):
                op("sp", lambda e, tb=tb: e.dma_start(out=x1_d[i * T + tb * 128:i * T + (tb + 1) * 128, :], in_=xt[:, tb, :]),
                   reads=[f"xt{tb}"], writes=[f"x1_d{i}_{tb}"], dma=True)
            P.handoff(ARENA_HID + ARENA_SCAN + ARENA_CONV, ARENA_XN)
            rmsnorm_hT(G2)
            P.handoff(ARENA_XN + ARENA_HID + ARENA_CONV, ARENA_SCAN)
            for pc in range(2):
                sv, sn = wpiece(10 + pc)
                for tb in range(4):
                    bank = tb % 2
                    for kc in range(16):
                        op("pe", lambda e, kc=kc, tb=tb, bank=bank, sv=sv: e.matmul(
                            B[bank][:], hT[:, kc, tb * 128:(tb + 1) * 128], sv[:, kc, :], start=(kc == 0), stop=(kc == 15)),
                           reads=[sn, HT[kc]], writes=[f"B{bank}"])
                    copy(alt(), vtok[:, tb, pc * 512:(pc + 1) * 512], B[bank][:], [f"B{bank}"], ["vtok"])
            op("sp", lambda e: e.dma_start(out=v_d[rows, :].rearrange("(t p) f -> p t f", p=128), in_=vtok[:]),
               reads=["vtok"], writes=[f"v_d{i}"], dma=True)
            for half in range(2):
                sa, na = wpiece(0 + half)
                sga, nga = wpiece(2 + half)
                for cc in range(4):
                    c = half * 4 + cc
                    st = c % 2
                    proj_chunk(0, sa, na, cc * 128)
                    proj_chunk(1, sga, nga, cc * 128)
                    t0 = TS[st][0]
                    op("act", lambda e, t0=t0: e.activation(out=t0, in_=B[1][:], func=AF.Sigmoid),
                       reads=["B1"], writes=[f"TS{st}_0"])
                    op("dve", lambda e, t0=t0, st=st: e.tensor_tensor(out=ktb[st], in0=t0, in1=B[0][:], op=ALU.mult),
                       reads=[f"TS{st}_0", "B0"], writes=[f"ktb{st}"])
                    op("sp", lambda e, st=st, c=c: e.dma_start(
                        out=upre_d[c * 128:(c + 1) * 128, 16 + i * T:16 + (i + 1) * T], in_=ktb[st]),
                       reads=[f"ktb{st}"], writes=[f"upre_d{i}_{c}"], dma=True)
            for grp, dram, fn, nm in ((12, og_d, AF.Silu, "og"), (8, zb_d, AF.Copy, "zb")):
                for half in range(2):
                    sv, sn = wpiece(grp + half)
                    for cc in range(4):
                        h = half * 4 + cc
                        st = h % 2
                        proj_chunk(h % 2, sv, sn, cc * 128)
                        t2 = TS[st][2]
                        op("act", lambda e, t2=t2, bk=h % 2, fn=fn: e.activation(out=t2, in_=B[bk][:], func=fn),
                           reads=[f"B{h % 2}"], writes=[f"TS{st}_2"])
                        op("sp", lambda e, t2=t2, h=h, dram=dram: e.dma_start(out=fm(dram, h, i), in_=t2),
                           reads=[f"TS{st}_2"], writes=[f"{nm}_d{i}_{h}"], dma=True)
            for hgp in range(2):
                sq_, nq = wpiece(4 + hgp)
                sz, nz = wpiece(6 + hgp)
                qbufs = {}

                def zsrc(hi, sz=sz, nz=nz):
                    proj_chunk(0, sz, nz, hi * 128)
                    return B[0][:], "B0"

                def qsrc(hi, sq_=sq_, nq=nq, hgp=hgp):
                    h = hgp * 4 + hi
                    st = hi % 2
                    proj_chunk(1, sq_, nq, hi * 128)
                    t4 = TS[st][4]
                    op("act", lambda e, t4=t4: e.activation(out=t4, in_=B[1][:], func=AF.Silu),
                       reads=["B1"], writes=[f"TS{st}_4"])
                    op("sp", lambda e, t4=t4, h=h: e.dma_start(out=fm(q_d, h, i), in_=t4),
                       reads=[f"TS{st}_4"], writes=[f"q_d{i}_{h}"], dma=True)
                    return t4, f"TS{st}_4"

                def finish(hi, h, ps, psn):
                    st = hi % 2
                    copy(alt(), osb[st], ps[:], [psn], [f"osb{st}"])
                    op("sp", lambda e, st=st, h=h: e.dma_start(out=fm(of_d, h, i), in_=osb[st]),
                       reads=[f"osb{st}"], writes=[f"of_d{i}_{h}"], dma=True)

                scan_group(0, hgp, zsrc, qsrc, finish)

        def upstream(t):
            rows = slice(t * T, (t + 1) * T)
            for tb in range(4):
                op("sp", lambda e, tb=tb: e.dma_start(out=xt[:, tb, :], in_=xu_d[t * T + tb * 128:t * T + (tb + 1) * 128, :]),
                   writes=[f"xt{tb}"], dma=True)
            P.handoff(ARENA_HID + ARENA_SCAN + ARENA_CONV, ARENA_XN)
            rmsnorm_hT(G1)
            ffn("ffn1")
            P.handoff(ARENA_HID + ARENA_SCAN + ARENA_CONV, ARENA_XN)
            rmsnorm_hT(G2)
            P.handoff(ARENA_XN + ARENA_HID + ARENA_CONV, ARENA_SCAN)
            for pc in range(2):
                sv, sn = wpiece(10 + pc)
                for tb in range(4):
                    bank = tb % 2
                    for kc in range(16):
                        op("pe", lambda e, kc=kc, tb=tb, bank=bank, sv=sv: e.matmul(
                            B[bank][:], hT[:, kc, tb * 128:(tb + 1) * 128], sv[:, kc, :], start=(kc == 0), stop=(kc == 15)),
                           reads=[sn, HT[kc]], writes=[f"B{bank}"])
                    copy(alt(), vtok[:, tb, pc * 512:(pc + 1) * 512], B[bank][:], [f"B{bank}"], ["vtok"])
            if t == NTu - 1:
                for half in range(2):
                    sa, na = wpiece(0 + half)
                    sga, nga = wpiece(2 + half)
                    for cc in range(4):
                        c = half * 4 + cc
                        st = c % 2
                        proj_chunk(0, sa, na, cc * 128)
                        proj_chunk(1, sga, nga, cc * 128)
                        t0 = TS[st][0]
                        op("act", lambda e, t0=t0: e.activation(out=t0, in_=B[1][:], func=AF.Sigmoid),
                           reads=["B1"], writes=[f"TS{st}_0"])
                        t1 = TS[st][1]
                        op("dve", lambda e, t0=t0, t1=t1: e.tensor_tensor(out=t1, in0=t0, in1=B[0][:], op=ALU.mult),
                           reads=[f"TS{st}_0", "B0"], writes=[f"TS{st}_1"])
                        op("dve", lambda e, t1=t1, st=st: e.tensor_copy(out=ktb[st][:, 0:16], in_=t1[:, ::-1][:, 0:16]),
                           reads=[f"TS{st}_1"], writes=[f"ktb{st}"])
                        op("sp", lambda e, st=st, c=c: e.dma_start(
                            out=upre_d[c * 128:(c + 1) * 128, 16 + Lc:32 + Lc], in_=ktb[st][:, 0:16]),
                           reads=[f"ktb{st}"], writes=["upre_pad1"], dma=True)
            for hgp in range(2):
                sz, nz = wpiece(8 + hgp)

                def zsrc(hi, sz=sz, nz=nz):
                    proj_chunk(0, sz, nz, hi * 128)
                    return B[0][:], "B0"

                scan_group(0, hgp, zsrc, None, None, pidx=1, state_only=True)

        def phase_b(i, outs):
            rows = slice(i * T, (i + 1) * T)
            P.handoff(ARENA_XN + ARENA_HID + ARENA_SCAN, ARENA_CONV)
            ur = upre_d.rearrange("(c p) l -> p c l", p=128)
            deps = [f"upre_d{j}_{c}" for j in (i - 1, i, i + 1) if 0 <= j < NT for c in range(8)] + ["upre_pad0", "upre_pad1"]
            op("sp", lambda e: e.dma_start(out=upad, in_=ur[:, :, 1 + i * T:1 + i * T + 542]),
               reads=deps, writes=["upad"], dma=True)
            def ln_stats(c):
                sq = cst[c % 2]
                op("pe", lambda e, c=c: e.matmul(B[0][:], onesF[:], uc[:, c, :], start=(c == 0), stop=(c == 7)),
                   reads=["onesF", f"uc{c}"], writes=["B0"])
                op("act", lambda e, c=c, sq=sq: e.activation(out=sq, in_=uc[:, c, :], func=AF.Square),
                   reads=[f"uc{c}"], writes=[f"cst{c % 2}"])
                op("pe", lambda e, c=c, sq=sq: e.matmul(B[1][:], onesF[:], sq, start=(c == 0), stop=(c == 7)),
                   reads=["onesF", f"cst{c % 2}"], writes=["B1"])

            for c in range(8):
                bank = 2 + c % 2
                for j in range(31):
                    k = (c * 31 + j) % 4
                    op("act", lambda e, c=c, j=j, k=k: e.activation(out=dg[k], in_=identB[:], func=AF.Copy,
                                                                   scale=cw[:, c, j:j + 1]),
                       reads=["identB", "cw"], writes=[f"dg{k}"])
                    op("pe", lambda e, c=c, j=j, k=k, bank=bank: e.matmul(
                        B[bank][:], dg[k], upad[:, c, j:j + T], start=(j == 0), stop=(j == 30)),
                       reads=[f"dg{k}", "upad"], writes=[f"B{bank}"])
                    if j == 8 and c > 0:
                        ln_stats(c - 1)
                op("act", lambda e, c=c, bank=bank: e.activation(out=uc[:, c, :], in_=B[bank][:], func=AF.Identity,
                                                                 scale=1.0, bias=CB[:, c:c + 1]),
                   reads=[f"B{bank}", "cols"], writes=[f"uc{c}"])
            ln_stats(7)
            mean, var, rs = cst[2], cst[3], cst[4]
            op("dve", lambda e: e.tensor_scalar(out=mean, in0=B[0][:], scalar1=1.0 / 1024, scalar2=None, op0=ALU.mult),
               reads=["B0"], writes=["cst2"])
            op("dve", lambda e: e.tensor_tensor(out=var, in0=mean, in1=mean, op=ALU.mult), reads=["cst2"], writes=["cst3"])
            op("dve", lambda e: e.scalar_tensor_tensor(out=var, in0=B[1][:], scalar=1.0 / 1024, in1=var,
                                                       op0=ALU.mult, op1=ALU.subtract),
               reads=["B1", "cst3"], writes=["cst3"])
            op("act", lambda e: e.activation(out=rs, in_=var, func=AF.Sqrt, scale=1.0, bias=epsc[:, 0:1]),
               reads=["cst3", "epsc"], writes=["cst4"])
            op("dve", lambda e: e.reciprocal(out=rs, in_=rs), reads=["cst4"], writes=["cst4"])
            for c in range(8):
                op("dve", lambda e, c=c: e.tensor_tensor(out=uc[:, c, :], in0=uc[:, c, :], in1=mean, op=ALU.subtract),
                   reads=[f"uc{c}", "cst2"], writes=[f"uc{c}"])
                op("dve", lambda e, c=c: e.tensor_tensor(out=uc[:, c, :], in0=uc[:, c, :], in1=rs, op=ALU.mult),
                   reads=[f"uc{c}", "cst4"], writes=[f"uc{c}"])
                op("act", lambda e, c=c: e.activation(out=hT[:, c, :], in_=uc[:, c, :], func=AF.Silu,
                                                      scale=LNW[:, c:c + 1], bias=LNB[:, c:c + 1]),
                   reads=[f"uc{c}", "cols"], writes=[HT[c]])
            for tb in range(4):
                op("sp", lambda e, tb=tb: e.dma_start(out=xt[:, tb, :], in_=x1_d[i * T + tb * 128:i * T + (tb + 1) * 128, :]),
                   reads=[f"x1_d{i}_{tb}"], writes=[f"xt{tb}"], dma=True)
            P.handoff(ARENA_XN + ARENA_HID + ARENA_CONV, ARENA_SCAN)
            op("sp", lambda e: e.dma_start(out=vtok[:], in_=v_d[rows, :].rearrange("(t p) f -> p t f", p=128)),
               reads=[f"v_d{i}"], writes=["vtok"], dma=True)
            for hgp in range(2):
                def zsrc(hi, hgp=hgp):
                    h = hgp * 4 + hi
                    st = hi % 2
                    t0 = TS[st][0]
                    op("sp", lambda e, t0=t0, h=h: e.dma_start(out=t0, in_=fm(zb_d, h, i)),
                       reads=[f"zb_d{i}_{h}"], writes=[f"TS{st}_0"], dma=True)
                    return t0, f"TS{st}_0"

                def qsrc(hi, hgp=hgp):
                    h = hgp * 4 + hi
                    st = hi % 2
                    t4 = TS[st][4]
                    op("sp", lambda e, t4=t4, h=h: e.dma_start(out=t4, in_=fm(q_d, h, i)),
                       reads=[f"q_d{i}_{h}"], writes=[f"TS{st}_4"], dma=True)
                    return t4, f"TS{st}_4"

                def finish(hi, h, ps, psn):
                    st = hi % 2
                    t5, t6 = TS[st][5], TS[st][6]
                    n5, n6 = f"TS{st}_5", f"TS{st}_6"
                    o_ = osb[st]
                    on = f"osb{st}"
                    op("sp", lambda e, t5=t5, h=h: e.dma_start(out=t5, in_=fm(of_d, h, i)),
                       reads=[f"of_d{i}_{h}"], writes=[n5], dma=True)
                    op("sp", lambda e, t6=t6, h=h: e.dma_start(out=t6, in_=fm(og_d, h, i)),
                       reads=[f"og_d{i}_{h}"], writes=[n6], dma=True)
                    op("dve", lambda e, o_=o_, t5=t5: e.tensor_tensor(out=o_, in0=ps[:], in1=t5, op=ALU.add),
                       reads=[psn, n5], writes=[on])
                    op("act", lambda e, o_=o_, t5=t5: e.activation(out=t5, in_=o_, func=AF.Square), reads=[on], writes=[n5])
                    op("pe", lambda e, t5=t5: e.matmul(B[0][:], onesF[:], t5, start=True, stop=True),
                       reads=["onesF", n5], writes=["B0"])
                    op("act", lambda e, t5=t5: e.activation(out=t5, in_=B[0][:], func=AF.Sqrt, scale=1.0 / 128,
                                                            bias=epsc[:, 0:1]), reads=["B0", "epsc"], writes=[n5])
                    op("dve", lambda e, t5=t5: e.reciprocal(out=t5, in_=t5), reads=[n5], writes=[n5])
                    op("dve", lambda e, o_=o_, t5=t5: e.tensor_tensor(out=o_, in0=o_, in1=t5, op=ALU.mult),
                       reads=[on, n5], writes=[on])
                    op("dve", lambda e, o_=o_, t6=t6, h=h: e.scalar_tensor_tensor(
                        out=hT[:, 8 + h, :], in0=o_, scalar=HGW[:, 0:1], in1=t6, op0=ALU.mult, op1=ALU.mult),
                       reads=[on, n6, "cols"], writes=[HT[8 + h]])

                scan_group(1, hgp, zsrc, qsrc, finish)
            wo = W["w_out"][0].rearrange("(kc p) f -> p kc f", p=128)
            for n in range(4):
                sv, sn = Wr.get(("w_out", n), wo[:, :, n * 512:(n + 1) * 512], [128, 16, 512])
                for tb in range(4):
                    bank = 4 + tb
                    for kc in range(16):
                        op("pe", lambda e, kc=kc, tb=tb, bank=bank, sv=sv: e.matmul(
                            B[bank][:], hT[:, kc, tb * 128:(tb + 1) * 128], sv[:, kc, :], start=(kc == 0), stop=(kc == 15)),
                           reads=[sn, HT[kc]], writes=[f"B{bank}"])
                    op("dve", lambda e, tb=tb, n=n, bank=bank: e.tensor_tensor(
                        out=xt[:, tb, n * 512:(n + 1) * 512], in0=B[bank][:], in1=xt[:, tb, n * 512:(n + 1) * 512], op=ALU.add),
                       reads=[f"B{bank}", f"xt{tb}"], writes=[f"xt{tb}"])
            P.handoff(ARENA_HID + ARENA_SCAN + ARENA_CONV, ARENA_XN)
            rmsnorm_hT(G3)
            ffn("ffn2")
            op("dve", lambda e: e.memset(ss[:], 0.0), writes=["ss"])
            for tb in range(4):
                op("act", lambda e, tb=tb: e.activation(
                    out=vtok[:, (tb % 2) * 2:(tb % 2) * 2 + 2, :].rearrange("p a b -> p (a b)"), in_=xt[:, tb, :],
                    func=AF.Square, accum_out=ss[:, tb:tb + 1]),
                   reads=[f"xt{tb}", "ss"], writes=["ss", "vtok"])
            op("act", lambda e: e.activation(out=rstd[:], in_=ss[:], func=AF.Sqrt, scale=1.0 / D, bias=epsc[:, 0:1]),
               reads=["ss", "epsc"], writes=["rstd"])
            op("dve", lambda e: e.reciprocal(out=rstd[:], in_=rstd[:]), reads=["rstd"], writes=["rstd"])
            for tb in range(4):
                op("dve", lambda e, tb=tb: e.scalar_tensor_tensor(
                    out=xt[:, tb, :], in0=xt[:, tb, :], scalar=rstd[:, tb:tb + 1], in1=fwb[:], op0=ALU.mult, op1=ALU.mult),
                   reads=[f"xt{tb}", "rstd", "fwb"], writes=[f"xt{tb}"])
            for tb in range(4):
                o = op("sp", lambda e, tb=tb: e.dma_start(out=y_d[i * T + tb * 128:i * T + (tb + 1) * 128, :], in_=xt[:, tb, :]),
                       reads=[f"xt{tb}"], writes=[f"y_d{i}_{tb}"], dma=True)
                outs.append(o)

        outs = []
        setup()
        for t in range(NTu):
            upstream(t)
        for i in range(NT):
            phase_a(i)
        for i in range(NT - 1, -1, -1):
            phase_b(i, outs)
        return Wr, outs

    Pd = Prog(nc, dry=True)
    Wr0, _ = make(Pd, None)
    order = Wr0.rec
    P = Prog(nc)
    Wr, outs = make(P, order)
    assert Wr.cur == len(order)
    P.finalize(final_waits=outs)
    return nc, P


_CACHE = {}


def run_cores(seqs, weights, NT, ups=None, wvariant=None):
    NTu = 0 if ups is None else ups[0].shape[0] // T
    key = (NT, NTu)
    if key not in _CACHE:
        _CACHE[key] = build(NT, NTu)[0]
    nc = _CACHE[key]
    wmap = {n: np.ascontiguousarray(np.asarray(weights[n], dtype=np.float32)) for n, _ in WNAMES}
    in_maps = []
    for c, s in enumerate(seqs):
        m = dict(wmap)
        if wvariant is not None and wvariant[c]:
            m.update(wvariant[c])
        m["x"] = np.ascontiguousarray(s, dtype=np.float32)
        if NTu:
            m["xu"] = np.ascontiguousarray(ups[c], dtype=np.float32)
        in_maps.append(m)
    res = run_bass_kernel_spmd(nc, in_maps, core_ids=list(range(8)))
    return [r["y"] for r in res.results]


def mirror_weights(weights):
    w_in = np.asarray(weights["w_in"], dtype=np.float32)
    wm = w_in.copy()
    wm[:, :, 3072:4096] = w_in[:, :, 4096:5120]
    wm[:, :, 4096:5120] = w_in[:, :, 3072:4096]
    return {
        "w_in": wm,
        "lb_fwd": np.ascontiguousarray(np.asarray(weights["lb_bwd"], dtype=np.float32)),
        "lb_bwd": np.ascontiguousarray(np.asarray(weights["lb_fwd"], dtype=np.float32)),
        "conv_w": np.ascontiguousarray(np.asarray(weights["conv_w"], dtype=np.float32)[:, ::-1, :]),
    }


def split_layout(xseqs, NT):
    No = NT * T
    owns, ups, mir = [], [], []
    for x in xseqs:
        L = x.shape[0]
        H = L // 2
        assert H <= No
        a = np.zeros((No, D), np.float32)
        a[No - H:] = x[:H]
        ua = np.zeros((No, D), np.float32)
        ua[No - H:] = x[H:][::-1]
        b = np.zeros((No, D), np.float32)
        b[No - H:] = x[H:][::-1]
        ub = np.zeros((No, D), np.float32)
        ub[No - H:] = x[:H]
        owns += [a, b]
        ups += [ua, ub]
        mir += [False, True]
    return owns, ups, mir


def run_split(xseqs, weights, NT):
    owns, ups, mir = split_layout(xseqs, NT)
    mw = mirror_weights(weights)
    ys = run_cores(owns, weights, NT, ups=ups, wvariant=[mw if m else None for m in mir])
    No = NT * T
    outs = []
    for k, x in enumerate(xseqs):
        H = x.shape[0] // 2
        ya = ys[2 * k][No - H:]
        yb = ys[2 * k + 1][No - H:][::-1]
        outs.append(np.concatenate([ya, yb], 0).astype(np.float32))
    return outs


def kernel(**inputs):
    xp = np.asarray(inputs["x_prompt"], dtype=np.float32)
    xs = np.asarray(inputs["x_sample"], dtype=np.float32)
    outs = run_split([xp[0], xp[1], xs[0], xs[1]], inputs, 8)
    y_prompt = np.stack([outs[0], outs[1]], 0)
    y_sample = np.stack([outs[2], outs[3]], 0)
    return (y_prompt, y_sample)
```

```python
import numpy as np
import concourse.bass as bass
import concourse.mybir as mybir
from concourse.bass_utils import run_bass_kernel_spmd

F32 = mybir.dt.float32
BF16 = mybir.dt.bfloat16
AF = mybir.ActivationFunctionType
ALU = mybir.AluOpType

SAME_ENGINE_SYNC = True
NDMASEM = 14
EPS = 1e-6
D = 2048
DFF = 5632
T = 512


class Buf:
    __slots__ = ("name", "w", "r")

    def __init__(self, name):
        self.name = name
        self.w = None
        self.r = []


class Op:
    __slots__ = ("eng", "emit", "deps", "dma", "idx", "sig", "cnt", "sem", "semval", "prevdma", "inc")


class Prog:
    ENGS = ("pe", "act", "dve", "pool", "sp")

    def __init__(self, nc, dry=False):
        self.nc = nc
        self.dry = dry
        self.ops = []
        self.by_eng = {e: [] for e in self.ENGS}
        self.ndma = {e: 0 for e in self.ENGS}
        self.dma_ops = {e: [] for e in self.ENGS}
        self.bufs = {}

    def buf(self, name):
        b = self.bufs.get(name)
        if b is None:
            b = self.bufs[name] = Buf(name)
        return b

    def handoff(self, old, new):
        if self.dry:
            return
        ops = []
        for n in old:
            b = self.buf(n)
            if b.w is not None:
                ops.append(b.w)
            ops.extend(b.r)
            b.w = None
            b.r = []
        best = {}
        keep = []
        for o in ops:
            if o.dma:
                keep.append(o)
            else:
                c = best.get(o.eng)
                if c is None or o.idx > c.idx:
                    best[o.eng] = o
        keep.extend(best.values())
        for n in new:
            self.buf(n).r.extend(keep)

    def op(self, eng, emit, reads=(), writes=(), dma=False, inc=16):
        if self.dry:
            return None
        o = Op()
        o.eng = eng
        o.emit = emit
        o.dma = dma
        o.idx = len(self.ops)
        o.sig = False
        o.cnt = None
        o.sem = None
        o.semval = None
        o.prevdma = None
        o.inc = inc
        deps = set()
        for b in reads:
            b = self.buf(b)
            if b.w is not None:
                deps.add(b.w)
            b.r.append(o)
        for b in writes:
            b = self.buf(b)
            if b.w is not None:
                deps.add(b.w)
            for r in b.r:
                if r is not o:
                    deps.add(r)
            b.w = o
            b.r = []
        deps.discard(o)
        best = {}
        keep = []
        for d in deps:
            if d.dma:
                keep.append(d)
            else:
                if d.eng == eng and not dma and (eng == "pe" or not SAME_ENGINE_SYNC):
                    continue
                cur = best.get(d.eng)
                if cur is None or d.idx > cur.idx:
                    best[d.eng] = d
        keep.extend(best.values())
        o.deps = keep
        for d in keep:
            d.sig = True
        if dma:
            k = self.ndma[eng]
            self.ndma[eng] += 1
            lst = self.dma_ops[eng]
            o.sem = k % NDMASEM
            prev = lst[k - NDMASEM].semval if k >= NDMASEM else 0
            o.semval = prev + inc
            if k >= NDMASEM:
                o.prevdma = lst[k - NDMASEM]
            lst.append(o)
        self.ops.append(o)
        self.by_eng[eng].append(o)
        return o

    def finalize(self, final_waits=()):
        nc = self.nc
        for e in self.ENGS:
            c = 0
            for o in self.by_eng[e]:
                if o.sig and not o.dma:
                    c += 1
                    o.cnt = c
        esem = {e: nc.alloc_semaphore(name=f"es_{e}") for e in self.ENGS}
        dsem = {e: [nc.alloc_semaphore(name=f"ds_{e}_{i}") for i in range(NDMASEM)]
                for e in ("sp", "pool", "act") if self.ndma[e] > 0}

        def run(e, engobj):
            waited = {}
            dwaited = {}
            for o in self.by_eng[e]:
                deps = list(o.deps)
                if o.prevdma is not None:
                    deps.append(o.prevdma)
                for d in deps:
                    if d.dma:
                        key = (d.eng, d.sem)
                        if dwaited.get(key, 0) >= d.semval:
                            continue
                        dwaited[key] = d.semval
                        engobj.wait_ge(dsem[d.eng][d.sem], d.semval)
                    else:
                        if waited.get(d.eng, 0) >= d.cnt:
                            continue
                        waited[d.eng] = d.cnt
                        engobj.wait_ge(esem[d.eng], d.cnt)
                ins = o.emit(engobj)
                if o.dma:
                    ins.then_inc(dsem[e][o.sem], o.inc)
                elif o.sig:
                    ins.then_inc(esem[e], 1)
            if e == "sp":
                for d in final_waits:
                    key = (d.eng, d.sem)
                    if dwaited.get(key, 0) >= d.semval:
                        continue
                    dwaited[key] = d.semval
                    engobj.wait_ge(dsem[d.eng][d.sem], d.semval)

        with nc.Block() as block:
            @block.tensor
            def _(eng):
                run("pe", eng)

            @block.scalar
            def _(eng):
                run("act", eng)

            @block.vector
            def _(eng):
                run("dve", eng)

            @block.gpsimd
            def _(eng):
                run("pool", eng)

            @block.sync
            def _(eng):
                run("sp", eng)


class WeightRing:
    NSLOT = 4
    LOOKAHEAD = 2

    def __init__(self, P, slots, order):
        self.P = P
        self.slots = slots
        self.order = order
        self.rec = []
        self.next_load = 0
        self.cur = 0

    def get(self, key, src, shape):
        if self.order is None:
            self.rec.append((key, src, shape))
            k = len(self.rec) - 1
        else:
            k = self.cur
            assert self.order[k][0] == key, (self.order[k][0], key)
            self.cur += 1
            lim = min(len(self.order), k + self.LOOKAHEAD + 1)
            while self.next_load < lim:
                self._load(self.next_load)
                self.next_load += 1
        s = k % self.NSLOT
        n = 1
        for d in shape[1:]:
            n *= d
        v = self.slots[s][:, 0:n]
        if len(shape) == 3:
            v = v.rearrange("p (a b) -> p a b", b=shape[2])
        return v, f"wslot{s}"

    def _load(self, k):
        key, src, shape = self.order[k]
        s = k % self.NSLOT
        n = 1
        for d in shape[1:]:
            n *= d
        v = self.slots[s][:, 0:n]
        if len(shape) == 3:
            v = v.rearrange("p (a b) -> p a b", b=shape[2])
        self.P.op("pool", lambda e, v=v, src=src: e.dma_start(out=v, in_=src), writes=[f"wslot{s}"], dma=True)


WNAMES = [
    ("ffn1_norm", [1, D]), ("ffn1_w1", [1, D, DFF]), ("ffn1_w3", [1, D, DFF]), ("ffn1_w2", [1, DFF, D]),
    ("mix_norm", [1, D]), ("w_in", [1, D, 7168]), ("conv_w", [1, 31, 1024]), ("conv_b", [1, 1024]),
    ("conv_ln_w", [1, 1024]), ("conv_ln_b", [1, 1024]), ("lb_fwd", [2, 1024]), ("lb_bwd", [2, 1024]),
    ("hg_norm", [1, 128]), ("w_out", [1, D, D]), ("ffn2_norm", [1, D]), ("ffn2_w1", [1, D, DFF]),
    ("ffn2_w3", [1, D, DFF]), ("ffn2_w2", [1, DFF, D]), ("final_norm", [D]),
]


def build(NT, NTu=0):
    nc = bass.Bass("TRN2", target_bir_lowering=False)
    Lc = NT * T
    x_d = nc.dram_tensor("x", [Lc, D], F32, kind="ExternalInput").ap()
    xu_d = nc.dram_tensor("xu", [NTu * T, D], F32, kind="ExternalInput").ap() if NTu else None
    W = {n: nc.dram_tensor(n, s, F32, kind="ExternalInput").ap() for n, s in WNAMES}
    y_d = nc.dram_tensor("y", [Lc, D], F32, kind="ExternalOutput").ap()
    x1_d = nc.dram_tensor("x1_d", [Lc, D], F32).ap()
    upre_d = nc.dram_tensor("upre_d", [1024, Lc + 32], BF16).ap()
    q_d = nc.dram_tensor("q_d", [1024, Lc], F32).ap()
    zb_d = nc.dram_tensor("zb_d", [1024, Lc], F32).ap()
    og_d = nc.dram_tensor("og_d", [1024, Lc], F32).ap()
    of_d = nc.dram_tensor("of_d", [1024, Lc], F32).ap()
    v_d = nc.dram_tensor("v_d", [Lc, 1024], BF16).ap()

    sb = nc.alloc_sbuf_tensor
    xt = sb("xt", [128, 4, D], F32)
    hT = sb("hT", [128, 16, T], BF16)
    hid = sb("hid", [128, 44, T], BF16)
    slots = [sb(f"wslot{i}", [128, 8192], BF16) for i in range(WeightRing.NSLOT)]
    sg = [sb(f"sg{i}", [128, T], F32) for i in range(2)]
    vtok = sb("vtok", [128, 4, 1024], BF16)
    stage = sb("stage", [128, 128], F32)
    cols = sb("cols", [128, 128], F32)
    identF = sb("identF", [128, 128], F32)
    identB = sb("identB", [128, 128], BF16)
    onesF = sb("onesF", [128, 128], F32)
    cwrow = sb("cwrow", [31, 1024], F32)
    cw = sb("cw", [128, 8, 31], F32)
    lbv = sb("lbv", [128, 16], F32)
    oml = sb("oml", [128, 16], F32)
    noml = sb("noml", [128, 16], F32)
    lbd = sb("lbd", [128, 16], F32)
    maskf = sb("maskf", [128, T], F32)
    maskb = sb("maskb", [128, T], F32)
    Mf = sb("Mf", [128, T], F32)
    Mb = sb("Mb", [128, T], F32)
    fwb = sb("fwb", [128, D], F32)
    zpad = sb("zpad", [128, 8, 16], BF16)
    ss = sb("ss", [128, 4], F32)
    epsc = sb("epsc", [128, 1], F32)
    rstd = sb("rstd", [128, 4], F32)
    S32 = [sb(f"S32_{d}", [128, 8, 128], F32) for d in range(2)]
    Sbf = [sb(f"Sbf_{d}", [128, 8, 128], BF16) for d in range(2)]
    Ssc = sb("Ssc", [128, 4, 128], F32)
    ebe = sb("ebe", [128, 8, 8], F32)
    B = [nc.alloc_psum_tensor(f"B{i}", [128, T], F32) for i in range(8)]

    hidflat = hid[:].rearrange("p a b -> p (a b)")

    def arena_f32(off_bytes, n):
        return hidflat[:, off_bytes // 2: off_bytes // 2 + 2 * n].bitcast(F32)

    def arena_bf16(off_bytes, n):
        return hidflat[:, off_bytes // 2: off_bytes // 2 + n]

    xn = hid[:, 0:16, :].rearrange("p (t a) f -> p t (a f)", t=4)
    KB = 1024
    TS = [[arena_f32((s * 7 + i) * 2 * KB, T) for i in range(7)] for s in range(2)]
    qt = [arena_bf16(28 * KB + i * KB, T) for i in range(4)]
    ktok = [arena_bf16(32 * KB + i * KB, T).rearrange("p (g k) -> p g k", k=128) for i in range(4)]
    scT = [arena_bf16(36 * KB + i * KB, T) for i in range(2)]
    ktb = [arena_bf16(38 * KB + i * KB, T) for i in range(2)]
    osb = [arena_f32(40 * KB + i * 2 * KB, T) for i in range(2)]
    upad = arena_bf16(0, 8 * 542).rearrange("p (c t) -> p c t", t=542)
    uc = arena_f32(8704, 8 * T).rearrange("p (c t) -> p c t", t=T)
    cst = [arena_f32(25088 + i * 2048, T) for i in range(5)]
    dg = [arena_bf16(35328 + i * 256, 128) for i in range(4)]
    ARENA_XN = ["xn0", "xn1", "xn2", "xn3"]
    ARENA_HID = [f"hid{j}" for j in range(44)]
    ARENA_SCAN = [f"TS{s}_{i}" for s in range(2) for i in range(7)] + [f"qt{i}" for i in range(4)] + \
                 [f"ktok{i}" for i in range(4)] + ["scT0", "scT1", "ktb0", "ktb1", "osb0", "osb1"]
    ARENA_CONV = ["upad"] + [f"uc{c}" for c in range(8)] + [f"cst{i}" for i in range(5)] + [f"dg{i}" for i in range(4)]

    Bt = [b[:, 0:256].bitcast(BF16) for b in B]

    def make(P, order):
        Wr = WeightRing(P, slots, order)
        op = P.op
        cnt = [0]

        def alt():
            cnt[0] += 1
            return "act" if cnt[0] % 2 else "dve"

        def scale_copy(eng, out, in_, scal, reads, writes):
            if eng == "act":
                op("act", lambda e: e.activation(out=out, in_=in_, func=AF.Copy, scale=scal), reads, writes)
            else:
                op("dve", lambda e: e.tensor_scalar(out=out, in0=in_, scalar1=scal, scalar2=None, op0=ALU.mult),
                   reads, writes)

        def copy(eng, out, in_, reads, writes):
            if eng == "act":
                op("act", lambda e: e.activation(out=out, in_=in_, func=AF.Copy), reads, writes)
            else:
                op("dve", lambda e: e.tensor_copy(out=out, in_=in_), reads, writes)

        def setup():
            op("dve", lambda e: e.memset(stage[:], 0.0), writes=["stage"])
            op("dve", lambda e: e.memset(epsc[:], EPS), writes=["epsc"])
            rows = [("ffn1_norm", 0, 16), ("mix_norm", 16, 16), ("ffn2_norm", 32, 16), ("conv_b", 48, 8),
                    ("conv_ln_w", 56, 8), ("conv_ln_b", 64, 8)]
            for n, r0, nr in rows:
                src = W[n][0].rearrange("(k p) -> k p", p=128)
                op("sp", lambda e, src=src, r0=r0, nr=nr: e.dma_start(out=stage[r0:r0 + nr, :], in_=src),
                   reads=[], writes=["stage"], dma=True)
            for n, r0 in (("lb_fwd", 72), ("lb_bwd", 88)):
                for s_ in range(2):
                    src = W[n][s_].rearrange("(k p) -> k p", p=128)
                    op("sp", lambda e, src=src, r=r0 + 8 * s_: e.dma_start(out=stage[r:r + 8, :], in_=src),
                       writes=["stage"], dma=True)
            op("sp", lambda e: e.dma_start(out=stage[104:105, :], in_=W["hg_norm"]), writes=["stage"], dma=True)
            op("sp", lambda e: e.dma_start(out=cwrow[:], in_=W["conv_w"][0]), writes=["cwrow"], dma=True)
            op("sp", lambda e: e.dma_start(out=fwb[:], in_=W["final_norm"].partition_broadcast(128)),
               writes=["fwb"], dma=True)
            op("pool", lambda e: e.memset(identF[:], 0.0), writes=["identF"])
            op("pool", lambda e: e.affine_select(out=identF[:], in_=identF[:], pattern=[[-1, 128]],
                                                 compare_op=ALU.not_equal, fill=1.0, base=0, channel_multiplier=1),
               reads=["identF"], writes=["identF"])
            op("dve", lambda e: e.tensor_copy(out=identB[:], in_=identF[:]), reads=["identF"], writes=["identB"])
            op("dve", lambda e: e.memset(onesF[:], 1.0), writes=["onesF"])
            op("dve", lambda e: e.memset(maskf[:], 1.0), writes=["maskf"])
            op("dve", lambda e: e.memset(maskf[:].rearrange("p (c t) -> p c t", t=64)[:, :, 0:1], 0.0),
               writes=["maskf"])
            op("dve", lambda e: e.memset(maskb[:], 1.0), writes=["maskb"])
            op("dve", lambda e: e.memset(maskb[:].rearrange("p (c t) -> p c t", t=64)[:, :, 63:64], 0.0),
               writes=["maskb"])
            op("pool", lambda e: e.memset(Mf[:], 1.0), writes=["Mf"])
            op("pool", lambda e: e.affine_select(out=Mf[:, 0:128], in_=Mf[:, 0:128], pattern=[[1, 128]],
                                                 compare_op=ALU.is_ge, fill=0.0, base=0, channel_multiplier=-1),
               reads=["Mf"], writes=["Mf"])
            op("pool", lambda e: e.memset(Mf[0:64, 64:128], 0.0), reads=["Mf"], writes=["Mf"])
            op("pool", lambda e: e.memset(Mb[:], 1.0), writes=["Mb"])
            op("pool", lambda e: e.affine_select(out=Mb[:, 0:128], in_=Mb[:, 0:128], pattern=[[-1, 128]],
                                                 compare_op=ALU.is_ge, fill=0.0, base=0, channel_multiplier=1),
               reads=["Mb"], writes=["Mb"])
            op("pool", lambda e: e.memset(Mb[64:128, 0:64], 0.0), reads=["Mb"], writes=["Mb"])
            for M_, nm in ((Mf, "Mf"), (Mb, "Mb")):
                for g in range(1, 4):
                    op("pool", lambda e, M_=M_, g=g: e.tensor_copy(out=M_[:, g * 128:(g + 1) * 128], in_=M_[:, 0:128]),
                       reads=[nm], writes=[nm])
            op("pe", lambda e: e.transpose(out=B[0][:, 0:128], in_=stage[:], identity=identF[:]),
               reads=["stage", "identF"], writes=["B0"])
            op("act", lambda e: e.activation(out=cols[:], in_=B[0][:, 0:128], func=AF.Copy), reads=["B0"], writes=["cols"])
            for c in range(8):
                op("pe", lambda e, c=c: e.transpose(out=B[1][:, 0:31], in_=cwrow[0:31, c * 128:(c + 1) * 128],
                                                    identity=identF[0:31, 0:31]),
                   reads=["cwrow", "identF"], writes=["B1"])
                op("dve", lambda e, c=c: e.tensor_copy(out=cw[:, c, :], in_=B[1][:, 0:31]), reads=["B1"], writes=["cw"])
            op("dve", lambda e: e.tensor_tensor(out=lbd[:, 0:8], in0=cols[:, 72:80], in1=cols[:, 80:88], op=ALU.subtract),
               reads=["cols"], writes=["lbd"])
            op("dve", lambda e: e.tensor_tensor(out=lbd[:, 8:16], in0=cols[:, 88:96], in1=cols[:, 96:104], op=ALU.subtract),
               reads=["cols"], writes=["lbd"])
            op("act", lambda e: e.activation(out=lbv[:], in_=lbd[:], func=AF.Sigmoid), reads=["lbd"], writes=["lbv"])
            op("dve", lambda e: e.tensor_scalar(out=oml[:], in0=lbv[:], scalar1=-1.0, scalar2=1.0, op0=ALU.mult, op1=ALU.add),
               reads=["lbv"], writes=["oml"])
            op("dve", lambda e: e.tensor_scalar(out=noml[:], in0=lbv[:], scalar1=-1.0, scalar2=None, op0=ALU.add),
               reads=["lbv"], writes=["noml"])
            op("dve", lambda e: e.memset(zpad[:], 0.0), writes=["zpad"])
            ur = upre_d.rearrange("(c p) l -> p c l", p=128)
            op("sp", lambda e: e.dma_start(out=ur[:, :, 0:16], in_=zpad[:]), reads=["zpad"], writes=["upre_pad0"], dma=True)
            if not NTu:
                op("sp", lambda e: e.dma_start(out=ur[:, :, 16 + Lc:32 + Lc], in_=zpad[:]), reads=["zpad"],
                   writes=["upre_pad1"], dma=True)
            for d in range(2):
                op("dve", lambda e, d=d: e.memset(S32[d][:], 0.0), writes=[f"S32_{d}_{h}" for h in range(8)])
                op("dve", lambda e, d=d: e.memset(Sbf[d][:], 0.0), writes=[f"Sbf_{d}_{h}" for h in range(8)])

        G1 = cols[:, 0:16]
        G2 = cols[:, 16:32]
        G3 = cols[:, 32:48]
        CB = cols[:, 48:56]
        LNW = cols[:, 56:64]
        LNB = cols[:, 64:72]
        HGW = cols[:, 104:105]

        def rmsnorm_hT(gcols):
            XT = [f"xt{tb}" for tb in range(4)]
            op("dve", lambda e: e.memset(ss[:], 0.0), writes=["ss"])
            for tb in range(4):
                op("act", lambda e, tb=tb: e.activation(out=xn[:, tb, :], in_=xt[:, tb, :], func=AF.Square,
                                                        accum_out=ss[:, tb:tb + 1]),
                   reads=[XT[tb], "ss"], writes=[f"xn{tb}", "ss"])
            op("act", lambda e: e.activation(out=rstd[:], in_=ss[:], func=AF.Sqrt, scale=1.0 / D, bias=epsc[:, 0:1]),
               reads=["ss", "epsc"], writes=["rstd"])
            op("dve", lambda e: e.reciprocal(out=rstd[:], in_=rstd[:]), reads=["rstd"], writes=["rstd"])
            for tb in range(4):
                scale_copy(alt(), xn[:, tb, :], xt[:, tb, :], rstd[:, tb:tb + 1], [XT[tb], "rstd"], [f"xn{tb}"])
            for kc in range(16):
                bk = kc % 2
                for tb in range(4):
                    op("pe", lambda e, kc=kc, tb=tb, bk=bk: e.transpose(
                        out=Bt[bk][:, tb * 128:(tb + 1) * 128], in_=xn[:, tb, kc * 128:(kc + 1) * 128], identity=identB[:]),
                       reads=[f"xn{tb}", "identB"], writes=[f"B{bk}"])
                scale_copy(alt(), hT[:, kc, :], Bt[bk][:, :], gcols[:, kc:kc + 1], [f"B{bk}", "cols"], [f"hT{kc}"])

        HT = [f"hT{kc}" for kc in range(16)]

        def ffn(pre):
            w1 = W[pre + "_w1"][0].rearrange("(kc p) f -> p kc f", p=128)
            w3 = W[pre + "_w3"][0].rearrange("(kc p) f -> p kc f", p=128)
            w2 = W[pre + "_w2"][0].rearrange("(kc p) f -> p kc f", p=128)
            P.handoff(ARENA_XN + ARENA_SCAN + ARENA_CONV, ARENA_HID)
            for blk in range(11):
                s1, n1 = Wr.get((pre, "w1", blk), w1[:, :, blk * 512:(blk + 1) * 512], [128, 16, 512])
                s3, n3 = Wr.get((pre, "w3", blk), w3[:, :, blk * 512:(blk + 1) * 512], [128, 16, 512])
                for j in range(4):
                    jj = blk * 4 + j
                    gb = 2 * (jj % 2)
                    ub = gb + 1
                    for kc in range(16):
                        op("pe", lambda e, kc=kc, j=j, gb=gb, s1=s1: e.matmul(
                            B[gb][:], s1[:, kc, j * 128:(j + 1) * 128], hT[:, kc, :], start=(kc == 0), stop=(kc == 15)),
                           reads=[n1, HT[kc]], writes=[f"B{gb}"])
                    for kc in range(16):
                        op("pe", lambda e, kc=kc, j=j, ub=ub, s3=s3: e.matmul(
                            B[ub][:], s3[:, kc, j * 128:(j + 1) * 128], hT[:, kc, :], start=(kc == 0), stop=(kc == 15)),
                           reads=[n3, HT[kc]], writes=[f"B{ub}"])
                    sgi = jj % 2
                    op("act", lambda e, gb=gb, sgi=sgi: e.activation(out=sg[sgi][:], in_=B[gb][:], func=AF.Silu),
                       reads=[f"B{gb}"], writes=[f"sg{sgi}"])
                    op("dve", lambda e, ub=ub, sgi=sgi, jj=jj: e.tensor_tensor(
                        out=hid[:, jj, :], in0=sg[sgi][:], in1=B[ub][:], op=ALU.mult),
                       reads=[f"sg{sgi}", f"B{ub}"], writes=[f"hid{jj}"])
            for n in range(4):
                for q in range(4):
                    s2, n2 = Wr.get((pre, "w2", n, q), w2[:, q * 11:(q + 1) * 11, n * 512:(n + 1) * 512], [128, 11, 512])
                    for tb in range(4):
                        for kc in range(11):
                            op("pe", lambda e, tb=tb, kc=kc, q=q, s2=s2: e.matmul(
                                B[4 + tb][:], hid[:, q * 11 + kc, tb * 128:(tb + 1) * 128], s2[:, kc, :],
                                start=(q == 0 and kc == 0), stop=(q == 3 and kc == 10)),
                               reads=[n2, f"hid{q * 11 + kc}"], writes=[f"B{4 + tb}"])
                for tb in range(4):
                    op("dve", lambda e, tb=tb, n=n: e.scalar_tensor_tensor(
                        out=xt[:, tb, n * 512:(n + 1) * 512], in0=B[4 + tb][:], scalar=0.5,
                        in1=xt[:, tb, n * 512:(n + 1) * 512], op0=ALU.mult, op1=ALU.add),
                       reads=[f"B{4 + tb}", f"xt{tb}"], writes=[f"xt{tb}"])

        def scan_group(d, hgp, zsrc, qsrc, finish, pidx=None, state_only=False):
            p = d if pidx is None else pidx
            Md, Mn = (Mf, "Mf") if d == 0 else (Mb, "Mb")
            mk, mkn = (maskf, "maskf") if d == 0 else (maskb, "maskb")
            srcs = {}
            for hi in range(4):
                h = hgp * 4 + hi
                st = hi % 2
                s_, g_, b_, eb_ = TS[st][0], TS[st][1], TS[st][2], TS[st][3]
                sN, gN, bN, ebN = (f"TS{st}_{i}" for i in range(4))
                if hi == 0:
                    srcs[0] = (zsrc(0), None if state_only else qsrc(0))
                (z, zN), qq = srcs[hi]
                if not state_only:
                    q_, qN = qq
                lc = p * 8 + h
                op("act", lambda e, s_=s_, z=z: e.activation(out=s_, in_=z, func=AF.Sigmoid), reads=[zN], writes=[sN])
                if hi + 1 < 4:
                    znext = zsrc(hi + 1)
                op("act", lambda e, s_=s_, g_=g_, lc=lc: e.activation(
                    out=g_, in_=s_, func=AF.Ln, scale=oml[:, lc:lc + 1], bias=lbv[:, lc:lc + 1]),
                   reads=[sN, "oml", "lbv"], writes=[gN])
                op("dve", lambda e, s_=s_, lc=lc: e.tensor_scalar(
                    out=s_, in0=s_, scalar1=noml[:, lc:lc + 1], scalar2=oml[:, lc:lc + 1], op0=ALU.mult, op1=ALU.add),
                   reads=[sN, "noml", "oml"], writes=[sN])
                if d == 0:
                    op("dve", lambda e, b_=b_, g_=g_: e.tensor_tensor_scan(
                        out=b_, data0=mk[:], data1=g_, initial=0.0, op0=ALU.mult, op1=ALU.add),
                       reads=[gN, mkn], writes=[bN])
                else:
                    op("dve", lambda e, b_=b_, g_=g_: e.tensor_tensor_scan(
                        out=b_[:, ::-1], data0=mk[:, ::-1], data1=g_[:, ::-1], initial=0.0, op0=ALU.mult, op1=ALU.add),
                       reads=[gN, mkn], writes=[bN])
                op("act", lambda e, b_=b_, eb_=eb_: e.activation(out=eb_, in_=b_, func=AF.Exp), reads=[bN], writes=[ebN])
                op("act", lambda e, b_=b_: e.activation(out=b_, in_=b_, func=AF.Exp, scale=-1.0), reads=[bN], writes=[bN])
                if hi + 1 < 4:
                    srcs[hi + 1] = (znext, None if state_only else qsrc(hi + 1))
                ecol = 63 if d == 0 else 0
                op("dve", lambda e, eb_=eb_, h=h, ecol=ecol: e.tensor_copy(
                    out=ebe[:, h, :], in_=eb_.rearrange("p (c t) -> p c t", t=64)[:, :, ecol]),
                   reads=[ebN], writes=[f"ebe{h}"])
                if not state_only:
                    op("dve", lambda e, hi=hi, q_=q_, eb_=eb_: e.tensor_tensor(out=qt[hi], in0=q_, in1=eb_, op=ALU.mult),
                       reads=[qN, ebN], writes=[f"qt{hi}"])
                op("dve", lambda e, st=st, s_=s_, b_=b_: e.tensor_tensor(out=ktb[st], in0=s_, in1=b_, op=ALU.mult),
                   reads=[sN, bN], writes=[f"ktb{st}"])
                if not state_only:
                    for g in range(4):
                        op("pe", lambda e, g=g, st=st, hi=hi: e.matmul(
                            B[2][:, g * 128:(g + 1) * 128], ktb[st][:, g * 128:(g + 1) * 128],
                            qt[hi][:, g * 128:(g + 1) * 128], start=True, stop=True),
                           reads=[f"ktb{st}", f"qt{hi}"], writes=["B2"])
                    op("dve", lambda e, st=st: e.tensor_tensor(out=scT[st], in0=B[2][:], in1=Md[:], op=ALU.mult),
                       reads=["B2", Mn], writes=[f"scT{st}"])
                for g in range(4):
                    op("pe", lambda e, g=g, st=st: e.transpose(
                        out=Bt[1][:, g * 128:(g + 1) * 128], in_=ktb[st][:, g * 128:(g + 1) * 128], identity=identB[:]),
                       reads=[f"ktb{st}", "identB"], writes=["B1"])
                op("act", lambda e, hi=hi: e.activation(out=ktok[hi].rearrange("p g k -> p (g k)"), in_=Bt[1][:, :],
                                                        func=AF.Copy), reads=["B1"], writes=[f"ktok{hi}"])
                if not state_only:
                    for g in range(4):
                        op("pe", lambda e, g=g, st=st, hi=hi, h=h: e.matmul(
                            B[4 + hi][:, g * 128:(g + 1) * 128], vtok[:, g, h * 128:(h + 1) * 128],
                            scT[st][:, g * 128:(g + 1) * 128], start=(g == 0), stop=False),
                           reads=["vtok", f"scT{st}"], writes=[f"B{4 + hi}"])
            corder = list(range(8)) if d == 0 else list(range(7, -1, -1))
            for c in corder:
                g = c // 2
                r0 = (c % 2) * 64
                for hi in range(4):
                    h = hgp * 4 + hi
                    op("pe", lambda e, hi=hi, h=h, g=g, r0=r0: e.matmul(
                        B[3][:, hi * 128:(hi + 1) * 128], ktok[hi][r0:r0 + 64, g, :], vtok[r0:r0 + 64, g, h * 128:(h + 1) * 128],
                        start=True, stop=True), reads=[f"ktok{hi}", "vtok"], writes=["B3"])
                for hi in range(4):
                    h = hgp * 4 + hi
                    if state_only:
                        break
                    op("pe", lambda e, c=c, hi=hi, h=h: e.matmul(
                        B[4 + hi][:, c * 64:(c + 1) * 64], Sbf[p][:, h, :], qt[hi][:, c * 64:(c + 1) * 64],
                        start=False, stop=True), reads=[f"Sbf_{p}_{h}", f"qt{hi}"], writes=[f"B{4 + hi}"])
                h0 = hgp * 4
                SN = [f"S32_{p}_{h0 + k}" for k in range(4)]
                SBN = [f"Sbf_{p}_{h0 + k}" for k in range(4)]
                EN = [f"ebe{h0 + k}" for k in range(4)]
                op("dve", lambda e, h0=h0: e.tensor_tensor(
                    out=Ssc[:], in0=B[3][:].rearrange("p (a b) -> p a b", b=128), in1=S32[p][:, h0:h0 + 4, :], op=ALU.add),
                   reads=["B3"] + SN, writes=["Ssc"])
                op("dve", lambda e, h0=h0, c=c: e.tensor_tensor(
                    out=S32[p][:, h0:h0 + 4, :], in0=Ssc[:], in1=ebe[:, h0:h0 + 4, c:c + 1].to_broadcast([128, 4, 128]),
                    op=ALU.mult), reads=["Ssc"] + EN, writes=SN)
                op("act", lambda e, h0=h0: e.activation(out=Sbf[p][:, h0:h0 + 4, :], in_=S32[p][:, h0:h0 + 4, :], func=AF.Copy),
                   reads=SN, writes=SBN)
            if not state_only:
                for hi in range(4):
                    finish(hi, hgp * 4 + hi, B[4 + hi], f"B{4 + hi}")

        def proj_chunk(bank, sv, sn, off):
            for kc in range(16):
                op("pe", lambda e, kc=kc: e.matmul(B[bank][:], sv[:, kc, off:off + 128], hT[:, kc, :],
                                                   start=(kc == 0), stop=(kc == 15)),
                   reads=[sn, HT[kc]], writes=[f"B{bank}"])

        win = W["w_in"][0].rearrange("(kc p) f -> p kc f", p=128)

        def wpiece(i):
            return Wr.get(("w_in", i), win[:, :, i * 512:(i + 1) * 512], [128, 16, 512])

        def fm(dram, h, tile):
            return dram[h * 128:(h + 1) * 128, tile * T:(tile + 1) * T]

        def phase_a(i):
            rows = slice(i * T, (i + 1) * T)
            for tb in range(4):
                op("sp", lambda e, tb=tb: e.dma_start(out=xt[:, tb, :], in_=x_d[i * T + tb * 128:i * T + (tb + 1) * 128, :]),
                   writes=[f"xt{tb}"], dma=True)
            P.handoff(ARENA_HID + ARENA_SCAN + ARENA_CONV, ARENA_XN)
            rmsnorm_hT(G1)
            ffn("ffn1")
            for tb in range(4):
                op("sp", lambda e, tb=tb: e.dma_start(out=x1_d[i * T + tb * 128:i * T + (tb + 1) * 128, :], in_=xt[:, tb, :]),
                   reads=[f"xt{tb}"], writes=[f"x1_d{i}_{tb}"], dma=True)
            P.handoff(ARENA_HID + ARENA_SCAN + ARENA_CONV, ARENA_XN)
            rmsnorm_hT(G2)
            P.handoff(ARENA_XN + ARENA_HID + ARENA_CONV, ARENA_SCAN)
            for pc in range(2):
                sv, sn = wpiece(10 + pc)
                for tb in range(4):
                    bank = tb % 2
                    for kc in range(16):
                        op("pe", lambda e, kc=kc, tb=tb, bank=bank, sv=sv: e.matmul(
                            B[bank][:], hT[:, kc, tb * 128:(tb + 1) * 128], sv[:, kc, :], start=(kc == 0), stop=(kc == 15)),
                           reads=[sn, HT[kc]], writes=[f"B{bank}"])
                    copy(alt(), vtok[:, tb, pc * 512:(pc + 1) * 512], B[bank][:], [f"B{bank}"], ["vtok"])
            op("sp", lambda e: e.dma_start(out=v_d[rows, :].rearrange("(t p) f -> p t f", p=128), in_=vtok[:]),
               reads=["vtok"], writes=[f"v_d{i}"], dma=True)
            for half in range(2):
                sa, na = wpiece(0 + half)
                sga, nga = wpiece(2 + half)
                for cc in range(4):
                    c = half * 4 + cc
                    st = c % 2
                    proj_chunk(0, sa, na, cc * 128)
                    proj_chunk(1, sga, nga, cc * 128)
                    t0 = TS[st][0]
                    op("act", lambda e, t0=t0: e.activation(out=t0, in_=B[1][:], func=AF.Sigmoid),
                       reads=["B1"], writes=[f"TS{st}_0"])
                    op("dve", lambda e, t0=t0, st=st: e.tensor_tensor(out=ktb[st], in0=t0, in1=B[0][:], op=ALU.mult),
                       reads=[f"TS{st}_0", "B0"], writes=[f"ktb{st}"])
                    op("sp", lambda e, st=st, c=c: e.dma_start(
                        out=upre_d[c * 128:(c + 1) * 128, 16 + i * T:16 + (i + 1) * T], in_=ktb[st]),
                       reads=[f"ktb{st}"], writes=[f"upre_d{i}_{c}"], dma=True)
            for grp, dram, fn, nm in ((12, og_d, AF.Silu, "og"), (8, zb_d, AF.Copy, "zb")):
                for half in range(2):
                    sv, sn = wpiece(grp + half)
                    for cc in range(4):
                        h = half * 4 + cc
                        st = h % 2
                        proj_chunk(h % 2, sv, sn, cc * 128)
                        t2 = TS[st][2]
                        op("act", lambda e, t2=t2, bk=h % 2, fn=fn: e.activation(out=t2, in_=B[bk][:], func=fn),
                           reads=[f"B{h % 2}"], writes=[f"TS{st}_2"])
                        op("sp", lambda e, t2=t2, h=h, dram=dram: e.dma_start(out=fm(dram, h, i), in_=t2),
                           reads=[f"TS{st}_2"], writes=[f"{nm}_d{i}_{h}"], dma=True)
            for hgp in range(2):
                sq_, nq = wpiece(4 + hgp)
                sz, nz = wpiece(6 + hgp)
                qbufs = {}

                def zsrc(hi, sz=sz, nz=nz):
                    proj_chunk(0, sz, nz, hi * 128)
                    return B[0][:], "B0"

                def qsrc(hi, sq_=sq_, nq=nq, hgp=hgp):
                    h = hgp * 4 + hi
                    st = hi % 2
                    proj_chunk(1, sq_, nq, hi * 128)
                    t4 = TS[st][4]
                    op("act", lambda e, t4=t4: e.activation(out=t4, in_=B[1][:], func=AF.Silu),
                       reads=["B1"], writes=[f"TS{st}_4"])
                    op("sp", lambda e, t4=t4, h=h: e.dma_start(out=fm(q_d, h, i), in_=t4),
                       reads=[f"TS{st}_4"], writes=[f"q_d{i}_{h}"], dma=True)
                    return t4, f"TS{st}_4"

                def finish(hi, h, ps, psn):
                    st = hi % 2
                    copy(alt(), osb[st], ps[:], [psn], [f"osb{st}"])
                    op("sp", lambda e, st=st, h=h: e.dma_start(out=fm(of_d, h, i), in_=osb[st]),
                       reads=[f"osb{st}"], writes=[f"of_d{i}_{h}"], dma=True)

                scan_group(0, hgp, zsrc, qsrc, finish)

        def upstream(t):
            rows = slice(t * T, (t + 1) * T)
            for tb in range(4):
                op("sp", lambda e, tb=tb: e.dma_start(out=xt[:, tb, :], in_=xu_d[t * T + tb * 128:t * T + (tb + 1) * 128, :]),
                   writes=[f"xt{tb}"], dma=True)
            P.handoff(ARENA_HID + ARENA_SCAN + ARENA_CONV, ARENA_XN)
            rmsnorm_hT(G1)
            ffn("ffn1")
            P.handoff(ARENA_HID + ARENA_SCAN + ARENA_CONV, ARENA_XN)
            rmsnorm_hT(G2)
            P.handoff(ARENA_XN + ARENA_HID + ARENA_CONV, ARENA_SCAN)
            for pc in range(2):
                sv, sn = wpiece(10 + pc)
                for tb in range(4):
                    bank = tb % 2
                    for kc in range(16):
                        op("pe", lambda e, kc=kc, tb=tb, bank=bank, sv=sv: e.matmul(
                            B[bank][:], hT[:, kc, tb * 128:(tb + 1) * 128], sv[:, kc, :], start=(kc == 0), stop=(kc == 15)),
                           reads=[sn, HT[kc]], writes=[f"B{bank}"])
                    copy(alt(), vtok[:, tb, pc * 512:(pc + 1) * 512], B[bank][:], [f"B{bank}"], ["vtok"])
            if t == NTu - 1:
                for half in range(2):
                    sa, na = wpiece(0 + half)
                    sga, nga = wpiece(2 + half)
                    for cc in range(4):
                        c = half * 4 + cc
                        st = c % 2
                        proj_chunk(0, sa, na, cc * 128)
                        proj_chunk(1, sga, nga, cc * 128)
                        t0 = TS[st][0]
                        op("act", lambda e, t0=t0: e.activation(out=t0, in_=B[1][:], func=AF.Sigmoid),
                           reads=["B1"], writes=[f"TS{st}_0"])
                        t1 = TS[st][1]
                        op("dve", lambda e, t0=t0, t1=t1: e.tensor_tensor(out=t1, in0=t0, in1=B[0][:], op=ALU.mult),
                           reads=[f"TS{st}_0", "B0"], writes=[f"TS{st}_1"])
                        op("dve", lambda e, t1=t1, st=st: e.tensor_copy(out=ktb[st][:, 0:16], in_=t1[:, ::-1][:, 0:16]),
                           reads=[f"TS{st}_1"], writes=[f"ktb{st}"])
                        op("sp", lambda e, st=st, c=c: e.dma_start(
                            out=upre_d[c * 128:(c + 1) * 128, 16 + Lc:32 + Lc], in_=ktb[st][:, 0:16]),
                           reads=[f"ktb{st}"], writes=["upre_pad1"], dma=True)
            for hgp in range(2):
                sz, nz = wpiece(8 + hgp)

                def zsrc(hi, sz=sz, nz=nz):
                    proj_chunk(0, sz, nz, hi * 128)
                    return B[0][:], "B0"

                scan_group(0, hgp, zsrc, None, None, pidx=1, state_only=True)

        def phase_b(i, outs):
            rows = slice(i * T, (i + 1) * T)
            P.handoff(ARENA_XN + ARENA_HID + ARENA_SCAN, ARENA_CONV)
            ur = upre_d.rearrange("(c p) l -> p c l", p=128)
            deps = [f"upre_d{j}_{c}" for j in (i - 1, i, i + 1) if 0 <= j < NT for c in range(8)] + ["upre_pad0", "upre_pad1"]
            op("sp", lambda e: e.dma_start(out=upad, in_=ur[:, :, 1 + i * T:1 + i * T + 542]),
               reads=deps, writes=["upad"], dma=True)
            def ln_stats(c):
                sq = cst[c % 2]
                op("pe", lambda e, c=c: e.matmul(B[0][:], onesF[:], uc[:, c, :], start=(c == 0), stop=(c == 7)),
                   reads=["onesF", f"uc{c}"], writes=["B0"])
                op("act", lambda e, c=c, sq=sq: e.activation(out=sq, in_=uc[:, c, :], func=AF.Square),
                   reads=[f"uc{c}"], writes=[f"cst{c % 2}"])
                op("pe", lambda e, c=c, sq=sq: e.matmul(B[1][:], onesF[:], sq, start=(c == 0), stop=(c == 7)),
                   reads=["onesF", f"cst{c % 2}"], writes=["B1"])

            for c in range(8):
                bank = 2 + c % 2
                for j in range(31):
                    k = (c * 31 + j) % 4
                    scale_copy("act" if (c * 31 + j) % 2 else "dve", dg[k], identB[:], cw[:, c, j:j + 1],
                               ["identB", "cw"], [f"dg{k}"])
                    op("pe", lambda e, c=c, j=j, k=k, bank=bank: e.matmul(
                        B[bank][:], dg[k], upad[:, c, j:j + T], start=(j == 0), stop=(j == 30)),
                       reads=[f"dg{k}", "upad"], writes=[f"B{bank}"])
                    if j == 8 and c > 0:
                        ln_stats(c - 1)
                op("act", lambda e, c=c, bank=bank: e.activation(out=uc[:, c, :], in_=B[bank][:], func=AF.Identity,
                                                                 scale=1.0, bias=CB[:, c:c + 1]),
                   reads=[f"B{bank}", "cols"], writes=[f"uc{c}"])
            ln_stats(7)
            mean, var, rs = cst[2], cst[3], cst[4]
            op("dve", lambda e: e.tensor_scalar(out=mean, in0=B[0][:], scalar1=1.0 / 1024, scalar2=None, op0=ALU.mult),
               reads=["B0"], writes=["cst2"])
            op("dve", lambda e: e.tensor_tensor(out=var, in0=mean, in1=mean, op=ALU.mult), reads=["cst2"], writes=["cst3"])
            op("dve", lambda e: e.scalar_tensor_tensor(out=var, in0=B[1][:], scalar=1.0 / 1024, in1=var,
                                                       op0=ALU.mult, op1=ALU.subtract),
               reads=["B1", "cst3"], writes=["cst3"])
            op("act", lambda e: e.activation(out=rs, in_=var, func=AF.Sqrt, scale=1.0, bias=epsc[:, 0:1]),
               reads=["cst3", "epsc"], writes=["cst4"])
            op("dve", lambda e: e.reciprocal(out=rs, in_=rs), reads=["cst4"], writes=["cst4"])
            for c in range(8):
                op("dve", lambda e, c=c: e.tensor_tensor(out=uc[:, c, :], in0=uc[:, c, :], in1=mean, op=ALU.subtract),
                   reads=[f"uc{c}", "cst2"], writes=[f"uc{c}"])
                op("dve", lambda e, c=c: e.tensor_tensor(out=uc[:, c, :], in0=uc[:, c, :], in1=rs, op=ALU.mult),
                   reads=[f"uc{c}", "cst4"], writes=[f"uc{c}"])
                op("act", lambda e, c=c: e.activation(out=hT[:, c, :], in_=uc[:, c, :], func=AF.Silu,
                                                      scale=LNW[:, c:c + 1], bias=LNB[:, c:c + 1]),
                   reads=[f"uc{c}", "cols"], writes=[HT[c]])
            for tb in range(4):
                op("sp", lambda e, tb=tb: e.dma_start(out=xt[:, tb, :], in_=x1_d[i * T + tb * 128:i * T + (tb + 1) * 128, :]),
                   reads=[f"x1_d{i}_{tb}"], writes=[f"xt{tb}"], dma=True)
            P.handoff(ARENA_XN + ARENA_HID + ARENA_CONV, ARENA_SCAN)
            op("sp", lambda e: e.dma_start(out=vtok[:], in_=v_d[rows, :].rearrange("(t p) f -> p t f", p=128)),
               reads=[f"v_d{i}"], writes=["vtok"], dma=True)
            for hgp in range(2):
                def zsrc(hi, hgp=hgp):
                    h = hgp * 4 + hi
                    st = hi % 2
                    t0 = TS[st][0]
                    op("sp", lambda e, t0=t0, h=h: e.dma_start(out=t0, in_=fm(zb_d, h, i)),
                       reads=[f"zb_d{i}_{h}"], writes=[f"TS{st}_0"], dma=True)
                    return t0, f"TS{st}_0"

                def qsrc(hi, hgp=hgp):
                    h = hgp * 4 + hi
                    st = hi % 2
                    t4 = TS[st][4]
                    op("sp", lambda e, t4=t4, h=h: e.dma_start(out=t4, in_=fm(q_d, h, i)),
                       reads=[f"q_d{i}_{h}"], writes=[f"TS{st}_4"], dma=True)
                    return t4, f"TS{st}_4"

                def finish(hi, h, ps, psn):
                    st = hi % 2
                    t5, t6 = TS[st][5], TS[st][6]
                    n5, n6 = f"TS{st}_5", f"TS{st}_6"
                    o_ = osb[st]
                    on = f"osb{st}"
                    op("sp", lambda e, t5=t5, h=h: e.dma_start(out=t5, in_=fm(of_d, h, i)),
                       reads=[f"of_d{i}_{h}"], writes=[n5], dma=True)
                    op("sp", lambda e, t6=t6, h=h: e.dma_start(out=t6, in_=fm(og_d, h, i)),
                       reads=[f"og_d{i}_{h}"], writes=[n6], dma=True)
                    op("dve", lambda e, o_=o_, t5=t5: e.tensor_tensor(out=o_, in0=ps[:], in1=t5, op=ALU.add),
                       reads=[psn, n5], writes=[on])
                    op("act", lambda e, o_=o_, t5=t5: e.activation(out=t5, in_=o_, func=AF.Square), reads=[on], writes=[n5])
                    op("pe", lambda e, t5=t5: e.matmul(B[0][:], onesF[:], t5, start=True, stop=True),
                       reads=["onesF", n5], writes=["B0"])
                    op("act", lambda e, t5=t5: e.activation(out=t5, in_=B[0][:], func=AF.Sqrt, scale=1.0 / 128,
                                                            bias=epsc[:, 0:1]), reads=["B0", "epsc"], writes=[n5])
                    op("dve", lambda e, t5=t5: e.reciprocal(out=t5, in_=t5), reads=[n5], writes=[n5])
                    op("dve", lambda e, o_=o_, t5=t5: e.tensor_tensor(out=o_, in0=o_, in1=t5, op=ALU.mult),
                       reads=[on, n5], writes=[on])
                    op("dve", lambda e, o_=o_, t6=t6, h=h: e.scalar_tensor_tensor(
                        out=hT[:, 8 + h, :], in0=o_, scalar=HGW[:, 0:1], in1=t6, op0=ALU.mult, op1=ALU.mult),
                       reads=[on, n6, "cols"], writes=[HT[8 + h]])

                scan_group(1, hgp, zsrc, qsrc, finish)
            wo = W["w_out"][0].rearrange("(kc p) f -> p kc f", p=128)
            for n in range(4):
                sv, sn = Wr.get(("w_out", n), wo[:, :, n * 512:(n + 1) * 512], [128, 16, 512])
                for tb in range(4):
                    bank = 4 + tb
                    for kc in range(16):
                        op("pe", lambda e, kc=kc, tb=tb, bank=bank, sv=sv: e.matmul(
                            B[bank][:], hT[:, kc, tb * 128:(tb + 1) * 128], sv[:, kc, :], start=(kc == 0), stop=(kc == 15)),
                           reads=[sn, HT[kc]], writes=[f"B{bank}"])
                    op("dve", lambda e, tb=tb, n=n, bank=bank: e.tensor_tensor(
                        out=xt[:, tb, n * 512:(n + 1) * 512], in0=B[bank][:], in1=xt[:, tb, n * 512:(n + 1) * 512], op=ALU.add),
                       reads=[f"B{bank}", f"xt{tb}"], writes=[f"xt{tb}"])
            P.handoff(ARENA_HID + ARENA_SCAN + ARENA_CONV, ARENA_XN)
            rmsnorm_hT(G3)
            ffn("ffn2")
            op("dve", lambda e: e.memset(ss[:], 0.0), writes=["ss"])
            for tb in range(4):
                op("act", lambda e, tb=tb: e.activation(
                    out=vtok[:, (tb % 2) * 2:(tb % 2) * 2 + 2, :].rearrange("p a b -> p (a b)"), in_=xt[:, tb, :],
                    func=AF.Square, accum_out=ss[:, tb:tb + 1]),
                   reads=[f"xt{tb}", "ss"], writes=["ss", "vtok"])
            op("act", lambda e: e.activation(out=rstd[:], in_=ss[:], func=AF.Sqrt, scale=1.0 / D, bias=epsc[:, 0:1]),
               reads=["ss", "epsc"], writes=["rstd"])
            op("dve", lambda e: e.reciprocal(out=rstd[:], in_=rstd[:]), reads=["rstd"], writes=["rstd"])
            for tb in range(4):
                op("dve", lambda e, tb=tb: e.scalar_tensor_tensor(
                    out=xt[:, tb, :], in0=xt[:, tb, :], scalar=rstd[:, tb:tb + 1], in1=fwb[:], op0=ALU.mult, op1=ALU.mult),
                   reads=[f"xt{tb}", "rstd", "fwb"], writes=[f"xt{tb}"])
            for tb in range(4):
                o = op("sp", lambda e, tb=tb: e.dma_start(out=y_d[i * T + tb * 128:i * T + (tb + 1) * 128, :], in_=xt[:, tb, :]),
                       reads=[f"xt{tb}"], writes=[f"y_d{i}_{tb}"], dma=True)
                outs.append(o)

        outs = []
        setup()
        for t in range(NTu):
            upstream(t)
        for i in range(NT):
            phase_a(i)
        for i in range(NT - 1, -1, -1):
            phase_b(i, outs)
        return Wr, outs

    Pd = Prog(nc, dry=True)
    Wr0, _ = make(Pd, None)
    order = Wr0.rec
    P = Prog(nc)
    Wr, outs = make(P, order)
    assert Wr.cur == len(order)
    P.finalize(final_waits=outs)
    return nc, P


_CACHE = {}


def run_cores(seqs, weights, NT, ups=None, wvariant=None):
    NTu = 0 if ups is None else ups[0].shape[0] // T
    key = (NT, NTu)
    if key not in _CACHE:
        _CACHE[key] = build(NT, NTu)[0]
    nc = _CACHE[key]
    wmap = {n: np.ascontiguousarray(np.asarray(weights[n], dtype=np.float32)) for n, _ in WNAMES}
    in_maps = []
    for c, s in enumerate(seqs):
        m = dict(wmap)
        if wvariant is not None and wvariant[c]:
            m.update(wvariant[c])
        m["x"] = np.ascontiguousarray(s, dtype=np.float32)
        if NTu:
            m["xu"] = np.ascontiguousarray(ups[c], dtype=np.float32)
        in_maps.append(m)
    res = run_bass_kernel_spmd(nc, in_maps, core_ids=list(range(8)))
    return [r["y"] for r in res.results]


def mirror_weights(weights):
    w_in = np.asarray(weights["w_in"], dtype=np.float32)
    wm = w_in.copy()
    wm[:, :, 3072:4096] = w_in[:, :, 4096:5120]
    wm[:, :, 4096:5120] = w_in[:, :, 3072:4096]
    return {
        "w_in": wm,
        "lb_fwd": np.ascontiguousarray(np.asarray(weights["lb_bwd"], dtype=np.float32)),
        "lb_bwd": np.ascontiguousarray(np.asarray(weights["lb_fwd"], dtype=np.float32)),
        "conv_w": np.ascontiguousarray(np.asarray(weights["conv_w"], dtype=np.float32)[:, ::-1, :]),
    }


def split_layout(xseqs, NT):
    No = NT * T
    owns, ups, mir = [], [], []
    for x in xseqs:
        L = x.shape[0]
        H = L // 2
        assert H <= No
        a = np.zeros((No, D), np.float32)
        a[No - H:] = x[:H]
        ua = np.zeros((No, D), np.float32)
        ua[No - H:] = x[H:][::-1]
        b = np.zeros((No, D), np.float32)
        b[No - H:] = x[H:][::-1]
        ub = np.zeros((No, D), np.float32)
        ub[No - H:] = x[:H]
        owns += [a, b]
        ups += [ua, ub]
        mir += [False, True]
    return owns, ups, mir


def run_split(xseqs, weights, NT):
    owns, ups, mir = split_layout(xseqs, NT)
    mw = mirror_weights(weights)
    ys = run_cores(owns, weights, NT, ups=ups, wvariant=[mw if m else None for m in mir])
    No = NT * T
    outs = []
    for k, x in enumerate(xseqs):
        H = x.shape[0] // 2
        ya = ys[2 * k][No - H:]
        yb = ys[2 * k + 1][No - H:][::-1]
        outs.append(np.concatenate([ya, yb], 0).astype(np.float32))
    return outs


def kernel(**inputs):
    xp = np.asarray(inputs["x_prompt"], dtype=np.float32)
    xs = np.asarray(inputs["x_sample"], dtype=np.float32)
    outs = run_split([xp[0], xp[1], xs[0], xs[1]], inputs, 8)
    y_prompt = np.stack([outs[0], outs[1]], 0)
    y_sample = np.stack([outs[2], outs[3]], 0)
    return (y_prompt, y_sample)
```

```python
import numpy as np
import concourse.bass as bass
import concourse.mybir as mybir
from concourse.bass_utils import run_bass_kernel_spmd

F32 = mybir.dt.float32
BF16 = mybir.dt.bfloat16
AF = mybir.ActivationFunctionType
ALU = mybir.AluOpType

SAME_ENGINE_SYNC = True
NDMASEM = 14
EPS = 1e-6
D = 2048
DFF = 5632
T = 512


class Buf:
    __slots__ = ("name", "w", "r")

    def __init__(self, name):
        self.name = name
        self.w = None
        self.r = []


class Op:
    __slots__ = ("eng", "emit", "deps", "dma", "idx", "sig", "cnt", "sem", "semval", "prevdma", "inc")


class Prog:
    ENGS = ("pe", "act", "dve", "pool", "sp")

    def __init__(self, nc, dry=False):
        self.nc = nc
        self.dry = dry
        self.ops = []
        self.by_eng = {e: [] for e in self.ENGS}
        self.ndma = {e: 0 for e in self.ENGS}
        self.dma_ops = {e: [] for e in self.ENGS}
        self.bufs = {}

    def buf(self, name):
        b = self.bufs.get(name)
        if b is None:
            b = self.bufs[name] = Buf(name)
        return b

    def handoff(self, old, new):
        if self.dry:
            return
        ops = []
        for n in old:
            b = self.buf(n)
            if b.w is not None:
                ops.append(b.w)
            ops.extend(b.r)
            b.w = None
            b.r = []
        best = {}
        keep = []
        for o in ops:
            if o.dma:
                keep.append(o)
            else:
                c = best.get(o.eng)
                if c is None or o.idx > c.idx:
                    best[o.eng] = o
        keep.extend(best.values())
        for n in new:
            self.buf(n).r.extend(keep)

    def op(self, eng, emit, reads=(), writes=(), dma=False, inc=16):
        if self.dry:
            return None
        o = Op()
        o.eng = eng
        o.emit = emit
        o.dma = dma
        o.idx = len(self.ops)
        o.sig = False
        o.cnt = None
        o.sem = None
        o.semval = None
        o.prevdma = None
        o.inc = inc
        deps = set()
        for b in reads:
            b = self.buf(b)
            if b.w is not None:
                deps.add(b.w)
            b.r.append(o)
        for b in writes:
            b = self.buf(b)
            if b.w is not None:
                deps.add(b.w)
            for r in b.r:
                if r is not o:
                    deps.add(r)
            b.w = o
            b.r = []
        deps.discard(o)
        best = {}
        keep = []
        for d in deps:
            if d.dma:
                keep.append(d)
            else:
                if d.eng == eng and not dma and (eng == "pe" or not SAME_ENGINE_SYNC):
                    continue
                cur = best.get(d.eng)
                if cur is None or d.idx > cur.idx:
                    best[d.eng] = d
        keep.extend(best.values())
        o.deps = keep
        for d in keep:
            d.sig = True
        if dma:
            k = self.ndma[eng]
            self.ndma[eng] += 1
            lst = self.dma_ops[eng]
            o.sem = k % NDMASEM
            prev = lst[k - NDMASEM].semval if k >= NDMASEM else 0
            o.semval = prev + inc
            if k >= NDMASEM:
                o.prevdma = lst[k - NDMASEM]
            lst.append(o)
        self.ops.append(o)
        self.by_eng[eng].append(o)
        return o

    def finalize(self, final_waits=()):
        nc = self.nc
        for e in self.ENGS:
            c = 0
            for o in self.by_eng[e]:
                if o.sig and not o.dma:
                    c += 1
                    o.cnt = c
        esem = {e: nc.alloc_semaphore(name=f"es_{e}") for e in self.ENGS}
        dsem = {e: [nc.alloc_semaphore(name=f"ds_{e}_{i}") for i in range(NDMASEM)]
                for e in ("sp", "pool", "act") if self.ndma[e] > 0}

        def run(e, engobj):
            waited = {}
            dwaited = {}
            for o in self.by_eng[e]:
                deps = list(o.deps)
                if o.prevdma is not None:
                    deps.append(o.prevdma)
                for d in deps:
                    if d.dma:
                        key = (d.eng, d.sem)
                        if dwaited.get(key, 0) >= d.semval:
                            continue
                        dwaited[key] = d.semval
                        engobj.wait_ge(dsem[d.eng][d.sem], d.semval)
                    else:
                        if waited.get(d.eng, 0) >= d.cnt:
                            continue
                        waited[d.eng] = d.cnt
                        engobj.wait_ge(esem[d.eng], d.cnt)
                ins = o.emit(engobj)
                if o.dma:
                    ins.then_inc(dsem[e][o.sem], o.inc)
                elif o.sig:
                    ins.then_inc(esem[e], 1)
            if e == "sp":
                for d in final_waits:
                    key = (d.eng, d.sem)
                    if dwaited.get(key, 0) >= d.semval:
                        continue
                    dwaited[key] = d.semval
                    engobj.wait_ge(dsem[d.eng][d.sem], d.semval)

        with nc.Block() as block:
            @block.tensor
            def _(eng):
                run("pe", eng)

            @block.scalar
            def _(eng):
                run("act", eng)

            @block.vector
            def _(eng):
                run("dve", eng)

            @block.gpsimd
            def _(eng):
                run("pool", eng)

            @block.sync
            def _(eng):
                run("sp", eng)


class WeightRing:
    NSLOT = 4
    LOOKAHEAD = 2

    def __init__(self, P, slots, order):
        self.P = P
        self.slots = slots
        self.order = order
        self.rec = []
        self.next_load = 0
        self.cur = 0

    def get(self, key, src, shape):
        if self.order is None:
            self.rec.append((key, src, shape))
            k = len(self.rec) - 1
        else:
            k = self.cur
            assert self.order[k][0] == key, (self.order[k][0], key)
            self.cur += 1
            lim = min(len(self.order), k + self.LOOKAHEAD + 1)
            while self.next_load < lim:
                self._load(self.next_load)
                self.next_load += 1
        s = k % self.NSLOT
        n = 1
        for d in shape[1:]:
            n *= d
        v = self.slots[s][:, 0:n]
        if len(shape) == 3:
            v = v.rearrange("p (a b) -> p a b", b=shape[2])
        return v, f"wslot{s}"

    def _load(self, k):
        key, src, shape = self.order[k]
        s = k % self.NSLOT
        n = 1
        for d in shape[1:]:
            n *= d
        v = self.slots[s][:, 0:n]
        if len(shape) == 3:
            v = v.rearrange("p (a b) -> p a b", b=shape[2])
        self.P.op("pool", lambda e, v=v, src=src: e.dma_start(out=v, in_=src), writes=[f"wslot{s}"], dma=True)


WNAMES = [
    ("ffn1_norm", [1, D]), ("ffn1_w1", [1, D, DFF]), ("ffn1_w3", [1, D, DFF]), ("ffn1_w2", [1, DFF, D]),
    ("mix_norm", [1, D]), ("w_in", [1, D, 7168]), ("conv_w", [1, 31, 1024]), ("conv_b", [1, 1024]),
    ("conv_ln_w", [1, 1024]), ("conv_ln_b", [1, 1024]), ("lb_fwd", [2, 1024]), ("lb_bwd", [2, 1024]),
    ("hg_norm", [1, 128]), ("w_out", [1, D, D]), ("ffn2_norm", [1, D]), ("ffn2_w1", [1, D, DFF]),
    ("ffn2_w3", [1, D, DFF]), ("ffn2_w2", [1, DFF, D]), ("final_norm", [D]),
]


def build(NT, NTu=0):
    nc = bass.Bass("TRN2", target_bir_lowering=False)
    Lc = NT * T
    x_d = nc.dram_tensor("x", [Lc, D], F32, kind="ExternalInput").ap()
    xu_d = nc.dram_tensor("xu", [NTu * T, D], F32, kind="ExternalInput").ap() if NTu else None
    W = {n: nc.dram_tensor(n, s, F32, kind="ExternalInput").ap() for n, s in WNAMES}
    y_d = nc.dram_tensor("y", [Lc, D], F32, kind="ExternalOutput").ap()
    x1_d = nc.dram_tensor("x1_d", [Lc, D], F32).ap()
    upre_d = nc.dram_tensor("upre_d", [1024, Lc + 32], BF16).ap()
    q_d = nc.dram_tensor("q_d", [1024, Lc], F32).ap()
    zb_d = nc.dram_tensor("zb_d", [1024, Lc], F32).ap()
    og_d = nc.dram_tensor("og_d", [1024, Lc], F32).ap()
    of_d = nc.dram_tensor("of_d", [1024, Lc], F32).ap()
    v_d = nc.dram_tensor("v_d", [Lc, 1024], BF16).ap()

    sb = nc.alloc_sbuf_tensor
    xt = sb("xt", [128, 4, D], F32)
    hT = sb("hT", [128, 16, T], BF16)
    hid = sb("hid", [128, 44, T], BF16)
    slots = [sb(f"wslot{i}", [128, 8192], BF16) for i in range(WeightRing.NSLOT)]
    sg = [sb(f"sg{i}", [128, T], F32) for i in range(2)]
    vtok = sb("vtok", [128, 4, 1024], BF16)
    stage = sb("stage", [128, 128], F32)
    cols = sb("cols", [128, 128], F32)
    identF = sb("identF", [128, 128], F32)
    identB = sb("identB", [128, 128], BF16)
    onesF = sb("onesF", [128, 128], F32)
    cwrow = sb("cwrow", [31, 1024], F32)
    cw = sb("cw", [128, 8, 31], F32)
    lbv = sb("lbv", [128, 16], F32)
    oml = sb("oml", [128, 16], F32)
    noml = sb("noml", [128, 16], F32)
    lbd = sb("lbd", [128, 16], F32)
    maskf = sb("maskf", [128, T], F32)
    maskb = sb("maskb", [128, T], F32)
    Mf = sb("Mf", [128, T], F32)
    Mb = sb("Mb", [128, T], F32)
    fwb = sb("fwb", [128, D], F32)
    zpad = sb("zpad", [128, 8, 16], BF16)
    ss = sb("ss", [128, 4], F32)
    epsc = sb("epsc", [128, 1], F32)
    rstd = sb("rstd", [128, 4], F32)
    S32 = [sb(f"S32_{d}", [128, 8, 128], F32) for d in range(2)]
    Sbf = [sb(f"Sbf_{d}", [128, 8, 128], BF16) for d in range(2)]
    Ssc = sb("Ssc", [128, 4, 128], F32)
    ebe = sb("ebe", [128, 8, 8], F32)
    B = [nc.alloc_psum_tensor(f"B{i}", [128, T], F32) for i in range(8)]

    hidflat = hid[:].rearrange("p a b -> p (a b)")

    def arena_f32(off_bytes, n):
        return hidflat[:, off_bytes // 2: off_bytes // 2 + 2 * n].bitcast(F32)

    def arena_bf16(off_bytes, n):
        return hidflat[:, off_bytes // 2: off_bytes // 2 + n]

    xn = hid[:, 0:16, :].rearrange("p (t a) f -> p t (a f)", t=4)
    KB = 1024
    TS = [[arena_f32((s * 7 + i) * 2 * KB, T) for i in range(7)] for s in range(2)]
    qt = [arena_bf16(28 * KB + i * KB, T) for i in range(4)]
    ktok = [arena_bf16(32 * KB + i * KB, T).rearrange("p (g k) -> p g k", k=128) for i in range(4)]
    scT = [arena_bf16(36 * KB + i * KB, T) for i in range(2)]
    ktb = [arena_bf16(38 * KB + i * KB, T) for i in range(2)]
    osb = [arena_f32(40 * KB + i * 2 * KB, T) for i in range(2)]
    upad = arena_bf16(0, 8 * 542).rearrange("p (c t) -> p c t", t=542)
    uc = arena_f32(8704, 8 * T).rearrange("p (c t) -> p c t", t=T)
    cst = [arena_f32(25088 + i * 2048, T) for i in range(5)]
    dg = [arena_bf16(35328 + i * 256, 128) for i in range(4)]
    ARENA_XN = ["xn0", "xn1", "xn2", "xn3"]
    ARENA_HID = [f"hid{j}" for j in range(44)]
    ARENA_SCAN = [f"TS{s}_{i}" for s in range(2) for i in range(7)] + [f"qt{i}" for i in range(4)] + \
                 [f"ktok{i}" for i in range(4)] + ["scT0", "scT1", "ktb0", "ktb1", "osb0", "osb1"]
    ARENA_CONV = ["upad"] + [f"uc{c}" for c in range(8)] + [f"cst{i}" for i in range(5)] + [f"dg{i}" for i in range(4)]

    Bt = [b[:, 0:256].bitcast(BF16) for b in B]

    def make(P, order):
        Wr = WeightRing(P, slots, order)
        op = P.op
        cnt = [0]

        def alt():
            cnt[0] += 1
            return "act" if cnt[0] % 2 else "dve"

        def scale_copy(eng, out, in_, scal, reads, writes):
            if eng == "act":
                op("act", lambda e: e.activation(out=out, in_=in_, func=AF.Copy, scale=scal), reads, writes)
            else:
                op("dve", lambda e: e.tensor_scalar(out=out, in0=in_, scalar1=scal, scalar2=None, op0=ALU.mult),
                   reads, writes)

        def copy(eng, out, in_, reads, writes):
            if eng == "act":
                op("act", lambda e: e.activation(out=out, in_=in_, func=AF.Copy), reads, writes)
            else:
                op("dve", lambda e: e.tensor_copy(out=out, in_=in_), reads, writes)

        def setup():
            op("dve", lambda e: e.memset(stage[:], 0.0), writes=["stage"])
            op("dve", lambda e: e.memset(epsc[:], EPS), writes=["epsc"])
            rows = [("ffn1_norm", 0, 16), ("mix_norm", 16, 16), ("ffn2_norm", 32, 16), ("conv_b", 48, 8),
                    ("conv_ln_w", 56, 8), ("conv_ln_b", 64, 8)]
            for n, r0, nr in rows:
                src = W[n][0].rearrange("(k p) -> k p", p=128)
                op("sp", lambda e, src=src, r0=r0, nr=nr: e.dma_start(out=stage[r0:r0 + nr, :], in_=src),
                   reads=[], writes=["stage"], dma=True)
            for n, r0 in (("lb_fwd", 72), ("lb_bwd", 88)):
                for s_ in range(2):
                    src = W[n][s_].rearrange("(k p) -> k p", p=128)
                    op("sp", lambda e, src=src, r=r0 + 8 * s_: e.dma_start(out=stage[r:r + 8, :], in_=src),
                       writes=["stage"], dma=True)
            op("sp", lambda e: e.dma_start(out=stage[104:105, :], in_=W["hg_norm"]), writes=["stage"], dma=True)
            op("sp", lambda e: e.dma_start(out=cwrow[:], in_=W["conv_w"][0]), writes=["cwrow"], dma=True)
            op("sp", lambda e: e.dma_start(out=fwb[:], in_=W["final_norm"].partition_broadcast(128)),
               writes=["fwb"], dma=True)
            op("pool", lambda e: e.memset(identF[:], 0.0), writes=["identF"])
            op("pool", lambda e: e.affine_select(out=identF[:], in_=identF[:], pattern=[[-1, 128]],
                                                 compare_op=ALU.not_equal, fill=1.0, base=0, channel_multiplier=1),
               reads=["identF"], writes=["identF"])
            op("dve", lambda e: e.tensor_copy(out=identB[:], in_=identF[:]), reads=["identF"], writes=["identB"])
            op("dve", lambda e: e.memset(onesF[:], 1.0), writes=["onesF"])
            op("dve", lambda e: e.memset(maskf[:], 1.0), writes=["maskf"])
            op("dve", lambda e: e.memset(maskf[:].rearrange("p (c t) -> p c t", t=64)[:, :, 0:1], 0.0),
               writes=["maskf"])
            op("dve", lambda e: e.memset(maskb[:], 1.0), writes=["maskb"])
            op("dve", lambda e: e.memset(maskb[:].rearrange("p (c t) -> p c t", t=64)[:, :, 63:64], 0.0),
               writes=["maskb"])
            op("pool", lambda e: e.memset(Mf[:], 1.0), writes=["Mf"])
            op("pool", lambda e: e.affine_select(out=Mf[:, 0:128], in_=Mf[:, 0:128], pattern=[[1, 128]],
                                                 compare_op=ALU.is_ge, fill=0.0, base=0, channel_multiplier=-1),
               reads=["Mf"], writes=["Mf"])
            op("pool", lambda e: e.memset(Mf[0:64, 64:128], 0.0), reads=["Mf"], writes=["Mf"])
            op("pool", lambda e: e.memset(Mb[:], 1.0), writes=["Mb"])
            op("pool", lambda e: e.affine_select(out=Mb[:, 0:128], in_=Mb[:, 0:128], pattern=[[-1, 128]],
                                                 compare_op=ALU.is_ge, fill=0.0, base=0, channel_multiplier=1),
               reads=["Mb"], writes=["Mb"])
            op("pool", lambda e: e.memset(Mb[64:128, 0:64], 0.0), reads=["Mb"], writes=["Mb"])
            for M_, nm in ((Mf, "Mf"), (Mb, "Mb")):
                for g in range(1, 4):
                    op("pool", lambda e, M_=M_, g=g: e.tensor_copy(out=M_[:, g * 128:(g + 1) * 128], in_=M_[:, 0:128]),
                       reads=[nm], writes=[nm])
            op("pe", lambda e: e.transpose(out=B[0][:, 0:128], in_=stage[:], identity=identF[:]),
               reads=["stage", "identF"], writes=["B0"])
            op("act", lambda e: e.activation(out=cols[:], in_=B[0][:, 0:128], func=AF.Copy), reads=["B0"], writes=["cols"])
            for c in range(8):
                op("pe", lambda e, c=c: e.transpose(out=B[1][:, 0:31], in_=cwrow[0:31, c * 128:(c + 1) * 128],
                                                    identity=identF[0:31, 0:31]),
                   reads=["cwrow", "identF"], writes=["B1"])
                op("dve", lambda e, c=c: e.tensor_copy(out=cw[:, c, :], in_=B[1][:, 0:31]), reads=["B1"], writes=["cw"])
            op("dve", lambda e: e.tensor_tensor(out=lbd[:, 0:8], in0=cols[:, 72:80], in1=cols[:, 80:88], op=ALU.subtract),
               reads=["cols"], writes=["lbd"])
            op("dve", lambda e: e.tensor_tensor(out=lbd[:, 8:16], in0=cols[:, 88:96], in1=cols[:, 96:104], op=ALU.subtract),
               reads=["cols"], writes=["lbd"])
            op("act", lambda e: e.activation(out=lbv[:], in_=lbd[:], func=AF.Sigmoid), reads=["lbd"], writes=["lbv"])
            op("dve", lambda e: e.tensor_scalar(out=oml[:], in0=lbv[:], scalar1=-1.0, scalar2=1.0, op0=ALU.mult, op1=ALU.add),
               reads=["lbv"], writes=["oml"])
            op("dve", lambda e: e.tensor_scalar(out=noml[:], in0=lbv[:], scalar1=-1.0, scalar2=None, op0=ALU.add),
               reads=["lbv"], writes=["noml"])
            op("dve", lambda e: e.memset(zpad[:], 0.0), writes=["zpad"])
            ur = upre_d.rearrange("(c p) l -> p c l", p=128)
            op("sp", lambda e: e.dma_start(out=ur[:, :, 0:16], in_=zpad[:]), reads=["zpad"], writes=["upre_pad0"], dma=True)
            if not NTu:
                op("sp", lambda e: e.dma_start(out=ur[:, :, 16 + Lc:32 + Lc], in_=zpad[:]), reads=["zpad"],
                   writes=["upre_pad1"], dma=True)
            for d in range(2):
                op("dve", lambda e, d=d: e.memset(S32[d][:], 0.0), writes=[f"S32_{d}_{h}" for h in range(8)])
                op("dve", lambda e, d=d: e.memset(Sbf[d][:], 0.0), writes=[f"Sbf_{d}_{h}" for h in range(8)])

        G1 = cols[:, 0:16]
        G2 = cols[:, 16:32]
        G3 = cols[:, 32:48]
        CB = cols[:, 48:56]
        LNW = cols[:, 56:64]
        LNB = cols[:, 64:72]
        HGW = cols[:, 104:105]

        def rmsnorm_hT(gcols):
            XT = [f"xt{tb}" for tb in range(4)]
            op("dve", lambda e: e.memset(ss[:], 0.0), writes=["ss"])
            for tb in range(4):
                op("act", lambda e, tb=tb: e.activation(out=xn[:, tb, :], in_=xt[:, tb, :], func=AF.Square,
                                                        accum_out=ss[:, tb:tb + 1]),
                   reads=[XT[tb], "ss"], writes=[f"xn{tb}", "ss"])
            op("act", lambda e: e.activation(out=rstd[:], in_=ss[:], func=AF.Sqrt, scale=1.0 / D, bias=epsc[:, 0:1]),
               reads=["ss", "epsc"], writes=["rstd"])
            op("dve", lambda e: e.reciprocal(out=rstd[:], in_=rstd[:]), reads=["rstd"], writes=["rstd"])
            for tb in range(4):
                scale_copy(alt(), xn[:, tb, :], xt[:, tb, :], rstd[:, tb:tb + 1], [XT[tb], "rstd"], [f"xn{tb}"])
            for kc in range(16):
                bk = kc % 4
                for tb in range(4):
                    op("pe", lambda e, kc=kc, tb=tb, bk=bk: e.transpose(
                        out=Bt[bk][:, tb * 128:(tb + 1) * 128], in_=xn[:, tb, kc * 128:(kc + 1) * 128], identity=identB[:]),
                       reads=[f"xn{tb}", "identB"], writes=[f"B{bk}"])
                scale_copy(alt(), hT[:, kc, :], Bt[bk][:, :], gcols[:, kc:kc + 1], [f"B{bk}", "cols"], [f"hT{kc}"])

        HT = [f"hT{kc}" for kc in range(16)]

        def ffn(pre):
            w1 = W[pre + "_w1"][0].rearrange("(kc p) f -> p kc f", p=128)
            w3 = W[pre + "_w3"][0].rearrange("(kc p) f -> p kc f", p=128)
            w2 = W[pre + "_w2"][0].rearrange("(kc p) f -> p kc f", p=128)
            P.handoff(ARENA_XN + ARENA_SCAN + ARENA_CONV, ARENA_HID)
            for blk in range(11):
                s1, n1 = Wr.get((pre, "w1", blk), w1[:, :, blk * 512:(blk + 1) * 512], [128, 16, 512])
                s3, n3 = Wr.get((pre, "w3", blk), w3[:, :, blk * 512:(blk + 1) * 512], [128, 16, 512])
                for j in range(4):
                    jj = blk * 4 + j
                    gb = 2 * (jj % 2)
                    ub = gb + 1
                    for kc in range(16):
                        op("pe", lambda e, kc=kc, j=j, gb=gb, s1=s1: e.matmul(
                            B[gb][:], s1[:, kc, j * 128:(j + 1) * 128], hT[:, kc, :], start=(kc == 0), stop=(kc == 15)),
                           reads=[n1, HT[kc]], writes=[f"B{gb}"])
                    for kc in range(16):
                        op("pe", lambda e, kc=kc, j=j, ub=ub, s3=s3: e.matmul(
                            B[ub][:], s3[:, kc, j * 128:(j + 1) * 128], hT[:, kc, :], start=(kc == 0), stop=(kc == 15)),
                           reads=[n3, HT[kc]], writes=[f"B{ub}"])
                    sgi = jj % 2
                    op("act", lambda e, gb=gb, sgi=sgi: e.activation(out=sg[sgi][:], in_=B[gb][:], func=AF.Silu),
                       reads=[f"B{gb}"], writes=[f"sg{sgi}"])
                    op("dve", lambda e, ub=ub, sgi=sgi, jj=jj: e.tensor_tensor(
                        out=hid[:, jj, :], in0=sg[sgi][:], in1=B[ub][:], op=ALU.mult),
                       reads=[f"sg{sgi}", f"B{ub}"], writes=[f"hid{jj}"])
            for n in range(4):
                for q in range(4):
                    s2, n2 = Wr.get((pre, "w2", n, q), w2[:, q * 11:(q + 1) * 11, n * 512:(n + 1) * 512], [128, 11, 512])
                    for tb in range(4):
                        for kc in range(11):
                            op("pe", lambda e, tb=tb, kc=kc, q=q, s2=s2: e.matmul(
                                B[4 + tb][:], hid[:, q * 11 + kc, tb * 128:(tb + 1) * 128], s2[:, kc, :],
                                start=(q == 0 and kc == 0), stop=(q == 3 and kc == 10)),
                               reads=[n2, f"hid{q * 11 + kc}"], writes=[f"B{4 + tb}"])
                for tb in range(4):
                    op("dve", lambda e, tb=tb, n=n: e.scalar_tensor_tensor(
                        out=xt[:, tb, n * 512:(n + 1) * 512], in0=B[4 + tb][:], scalar=0.5,
                        in1=xt[:, tb, n * 512:(n + 1) * 512], op0=ALU.mult, op1=ALU.add),
                       reads=[f"B{4 + tb}", f"xt{tb}"], writes=[f"xt{tb}"])

        def scan_group(d, hgp, zsrc, qsrc, finish, pidx=None, state_only=False):
            p = d if pidx is None else pidx
            Md, Mn = (Mf, "Mf") if d == 0 else (Mb, "Mb")
            mk, mkn = (maskf, "maskf") if d == 0 else (maskb, "maskb")
            srcs = {}
            for hi in range(4):
                h = hgp * 4 + hi
                st = hi % 2
                s_, g_, b_, eb_ = TS[st][0], TS[st][1], TS[st][2], TS[st][3]
                sN, gN, bN, ebN = (f"TS{st}_{i}" for i in range(4))
                if hi == 0:
                    srcs[0] = (zsrc(0), None if state_only else qsrc(0))
                (z, zN), qq = srcs[hi]
                if not state_only:
                    q_, qN = qq
                lc = p * 8 + h
                op("act", lambda e, s_=s_, z=z: e.activation(out=s_, in_=z, func=AF.Sigmoid), reads=[zN], writes=[sN])
                if hi + 1 < 4:
                    znext = zsrc(hi + 1)
                op("act", lambda e, s_=s_, g_=g_, lc=lc: e.activation(
                    out=g_, in_=s_, func=AF.Ln, scale=oml[:, lc:lc + 1], bias=lbv[:, lc:lc + 1]),
                   reads=[sN, "oml", "lbv"], writes=[gN])
                op("dve", lambda e, s_=s_, lc=lc: e.tensor_scalar(
                    out=s_, in0=s_, scalar1=noml[:, lc:lc + 1], scalar2=oml[:, lc:lc + 1], op0=ALU.mult, op1=ALU.add),
                   reads=[sN, "noml", "oml"], writes=[sN])
                if d == 0:
                    op("dve", lambda e, b_=b_, g_=g_: e.tensor_tensor_scan(
                        out=b_, data0=mk[:], data1=g_, initial=0.0, op0=ALU.mult, op1=ALU.add),
                       reads=[gN, mkn], writes=[bN])
                else:
                    op("dve", lambda e, b_=b_, g_=g_: e.tensor_tensor_scan(
                        out=b_[:, ::-1], data0=mk[:, ::-1], data1=g_[:, ::-1], initial=0.0, op0=ALU.mult, op1=ALU.add),
                       reads=[gN, mkn], writes=[bN])
                op("act", lambda e, b_=b_, eb_=eb_: e.activation(out=eb_, in_=b_, func=AF.Exp), reads=[bN], writes=[ebN])
                op("act", lambda e, b_=b_: e.activation(out=b_, in_=b_, func=AF.Exp, scale=-1.0), reads=[bN], writes=[bN])
                if hi + 1 < 4:
                    srcs[hi + 1] = (znext, None if state_only else qsrc(hi + 1))
                ecol = 63 if d == 0 else 0
                op("dve", lambda e, eb_=eb_, h=h, ecol=ecol: e.tensor_copy(
                    out=ebe[:, h, :], in_=eb_.rearrange("p (c t) -> p c t", t=64)[:, :, ecol]),
                   reads=[ebN], writes=[f"ebe{h}"])
                if not state_only:
                    op("dve", lambda e, hi=hi, q_=q_, eb_=eb_: e.tensor_tensor(out=qt[hi], in0=q_, in1=eb_, op=ALU.mult),
                       reads=[qN, ebN], writes=[f"qt{hi}"])
                op("dve", lambda e, st=st, s_=s_, b_=b_: e.tensor_tensor(out=ktb[st], in0=s_, in1=b_, op=ALU.mult),
                   reads=[sN, bN], writes=[f"ktb{st}"])
                if not state_only:
                    for g in range(4):
                        op("pe", lambda e, g=g, st=st, hi=hi: e.matmul(
                            B[2][:, g * 128:(g + 1) * 128], ktb[st][:, g * 128:(g + 1) * 128],
                            qt[hi][:, g * 128:(g + 1) * 128], start=True, stop=True),
                           reads=[f"ktb{st}", f"qt{hi}"], writes=["B2"])
                    op("dve", lambda e, st=st: e.tensor_tensor(out=scT[st], in0=B[2][:], in1=Md[:], op=ALU.mult),
                       reads=["B2", Mn], writes=[f"scT{st}"])
                for g in range(4):
                    op("pe", lambda e, g=g, st=st: e.transpose(
                        out=Bt[1][:, g * 128:(g + 1) * 128], in_=ktb[st][:, g * 128:(g + 1) * 128], identity=identB[:]),
                       reads=[f"ktb{st}", "identB"], writes=["B1"])
                op("act", lambda e, hi=hi: e.activation(out=ktok[hi].rearrange("p g k -> p (g k)"), in_=Bt[1][:, :],
                                                        func=AF.Copy), reads=["B1"], writes=[f"ktok{hi}"])
                if not state_only:
                    for g in range(4):
                        op("pe", lambda e, g=g, st=st, hi=hi, h=h: e.matmul(
                            B[4 + hi][:, g * 128:(g + 1) * 128], vtok[:, g, h * 128:(h + 1) * 128],
                            scT[st][:, g * 128:(g + 1) * 128], start=(g == 0), stop=False),
                           reads=["vtok", f"scT{st}"], writes=[f"B{4 + hi}"])
            corder = list(range(8)) if d == 0 else list(range(7, -1, -1))
            for c in corder:
                g = c // 2
                r0 = (c % 2) * 64
                for hi in range(4):
                    h = hgp * 4 + hi
                    op("pe", lambda e, hi=hi, h=h, g=g, r0=r0: e.matmul(
                        B[3][:, hi * 128:(hi + 1) * 128], ktok[hi][r0:r0 + 64, g, :], vtok[r0:r0 + 64, g, h * 128:(h + 1) * 128],
                        start=True, stop=True), reads=[f"ktok{hi}", "vtok"], writes=["B3"])
                for hi in range(4):
                    h = hgp * 4 + hi
                    if state_only:
                        break
                    op("pe", lambda e, c=c, hi=hi, h=h: e.matmul(
                        B[4 + hi][:, c * 64:(c + 1) * 64], Sbf[p][:, h, :], qt[hi][:, c * 64:(c + 1) * 64],
                        start=False, stop=True), reads=[f"Sbf_{p}_{h}", f"qt{hi}"], writes=[f"B{4 + hi}"])
                h0 = hgp * 4
                SN = [f"S32_{p}_{h0 + k}" for k in range(4)]
                SBN = [f"Sbf_{p}_{h0 + k}" for k in range(4)]
                EN = [f"ebe{h0 + k}" for k in range(4)]
                op("dve", lambda e, h0=h0: e.tensor_tensor(
                    out=Ssc[:], in0=B[3][:].rearrange("p (a b) -> p a b", b=128), in1=S32[p][:, h0:h0 + 4, :], op=ALU.add),
                   reads=["B3"] + SN, writes=["Ssc"])
                op("dve", lambda e, h0=h0, c=c: e.tensor_tensor(
                    out=S32[p][:, h0:h0 + 4, :], in0=Ssc[:], in1=ebe[:, h0:h0 + 4, c:c + 1].to_broadcast([128, 4, 128]),
                    op=ALU.mult), reads=["Ssc"] + EN, writes=SN)
                op("act", lambda e, h0=h0: e.activation(out=Sbf[p][:, h0:h0 + 4, :], in_=S32[p][:, h0:h0 + 4, :], func=AF.Copy),
                   reads=SN, writes=SBN)
            if not state_only:
                for hi in range(4):
                    finish(hi, hgp * 4 + hi, B[4 + hi], f"B{4 + hi}")

        def proj_chunk(bank, sv, sn, off):
            for kc in range(16):
                op("pe", lambda e, kc=kc: e.matmul(B[bank][:], sv[:, kc, off:off + 128], hT[:, kc, :],
                                                   start=(kc == 0), stop=(kc == 15)),
                   reads=[sn, HT[kc]], writes=[f"B{bank}"])

        win = W["w_in"][0].rearrange("(kc p) f -> p kc f", p=128)

        def wpiece(i):
            return Wr.get(("w_in", i), win[:, :, i * 512:(i + 1) * 512], [128, 16, 512])

        def fm(dram, h, tile):
            return dram[h * 128:(h + 1) * 128, tile * T:(tile + 1) * T]

        def phase_a(i):
            rows = slice(i * T, (i + 1) * T)
            for tb in range(4):
                op("sp", lambda e, tb=tb: e.dma_start(out=xt[:, tb, :], in_=x_d[i * T + tb * 128:i * T + (tb + 1) * 128, :]),
                   writes=[f"xt{tb}"], dma=True)
            P.handoff(ARENA_HID + ARENA_SCAN + ARENA_CONV, ARENA_XN)
            rmsnorm_hT(G1)
            ffn("ffn1")
            for tb in range(4):
                op("sp", lambda e, tb=tb: e.dma_start(out=x1_d[i * T + tb * 128:i * T + (tb + 1) * 128, :], in_=xt[:, tb, :]),
                   reads=[f"xt{tb}"], writes=[f"x1_d{i}_{tb}"], dma=True)
            P.handoff(ARENA_HID + ARENA_SCAN + ARENA_CONV, ARENA_XN)
            rmsnorm_hT(G2)
            P.handoff(ARENA_XN + ARENA_HID + ARENA_CONV, ARENA_SCAN)
            for pc in range(2):
                sv, sn = wpiece(10 + pc)
                for tb in range(4):
                    bank = tb % 2
                    for kc in range(16):
                        op("pe", lambda e, kc=kc, tb=tb, bank=bank, sv=sv: e.matmul(
                            B[bank][:], hT[:, kc, tb * 128:(tb + 1) * 128], sv[:, kc, :], start=(kc == 0), stop=(kc == 15)),
                           reads=[sn, HT[kc]], writes=[f"B{bank}"])
                    copy(alt(), vtok[:, tb, pc * 512:(pc + 1) * 512], B[bank][:], [f"B{bank}"], ["vtok"])
            op("sp", lambda e: e.dma_start(out=v_d[rows, :].rearrange("(t p) f -> p t f", p=128), in_=vtok[:]),
               reads=["vtok"], writes=[f"v_d{i}"], dma=True)
            for half in range(2):
                sa, na = wpiece(0 + half)
                sga, nga = wpiece(2 + half)
                for cc in range(4):
                    c = half * 4 + cc
                    st = c % 2
                    proj_chunk(0, sa, na, cc * 128)
                    proj_chunk(1, sga, nga, cc * 128)
                    t0 = TS[st][0]
                    op("act", lambda e, t0=t0: e.activation(out=t0, in_=B[1][:], func=AF.Sigmoid),
                       reads=["B1"], writes=[f"TS{st}_0"])
                    op("dve", lambda e, t0=t0, st=st: e.tensor_tensor(out=ktb[st], in0=t0, in1=B[0][:], op=ALU.mult),
                       reads=[f"TS{st}_0", "B0"], writes=[f"ktb{st}"])
                    op("sp", lambda e, st=st, c=c: e.dma_start(
                        out=upre_d[c * 128:(c + 1) * 128, 16 + i * T:16 + (i + 1) * T], in_=ktb[st]),
                       reads=[f"ktb{st}"], writes=[f"upre_d{i}_{c}"], dma=True)
            for grp, dram, fn, nm in ((12, og_d, AF.Silu, "og"), (8, zb_d, AF.Copy, "zb")):
                for half in range(2):
                    sv, sn = wpiece(grp + half)
                    for cc in range(4):
                        h = half * 4 + cc
                        st = h % 2
                        proj_chunk(h % 2, sv, sn, cc * 128)
                        t2 = TS[st][2]
                        op("act", lambda e, t2=t2, bk=h % 2, fn=fn: e.activation(out=t2, in_=B[bk][:], func=fn),
                           reads=[f"B{h % 2}"], writes=[f"TS{st}_2"])
                        op("sp", lambda e, t2=t2, h=h, dram=dram: e.dma_start(out=fm(dram, h, i), in_=t2),
                           reads=[f"TS{st}_2"], writes=[f"{nm}_d{i}_{h}"], dma=True)
            for hgp in range(2):
                sq_, nq = wpiece(4 + hgp)
                sz, nz = wpiece(6 + hgp)
                qbufs = {}

                def zsrc(hi, sz=sz, nz=nz):
                    proj_chunk(0, sz, nz, hi * 128)
                    return B[0][:], "B0"

                def qsrc(hi, sq_=sq_, nq=nq, hgp=hgp):
                    h = hgp * 4 + hi
                    st = hi % 2
                    proj_chunk(1, sq_, nq, hi * 128)
                    t4 = TS[st][4]
                    op("act", lambda e, t4=t4: e.activation(out=t4, in_=B[1][:], func=AF.Silu),
                       reads=["B1"], writes=[f"TS{st}_4"])
                    op("sp", lambda e, t4=t4, h=h: e.dma_start(out=fm(q_d, h, i), in_=t4),
                       reads=[f"TS{st}_4"], writes=[f"q_d{i}_{h}"], dma=True)
                    return t4, f"TS{st}_4"

                def finish(hi, h, ps, psn):
                    st = hi % 2
                    copy(alt(), osb[st], ps[:], [psn], [f"osb{st}"])
                    op("sp", lambda e, st=st, h=h: e.dma_start(out=fm(of_d, h, i), in_=osb[st]),
                       reads=[f"osb{st}"], writes=[f"of_d{i}_{h}"], dma=True)

                scan_group(0, hgp, zsrc, qsrc, finish)

        def upstream(t):
            rows = slice(t * T, (t + 1) * T)
            for tb in range(4):
                op("sp", lambda e, tb=tb: e.dma_start(out=xt[:, tb, :], in_=xu_d[t * T + tb * 128:t * T + (tb + 1) * 128, :]),
                   writes=[f"xt{tb}"], dma=True)
            P.handoff(ARENA_HID + ARENA_SCAN + ARENA_CONV, ARENA_XN)
            rmsnorm_hT(G1)
            ffn("ffn1")
            P.handoff(ARENA_HID + ARENA_SCAN + ARENA_CONV, ARENA_XN)
            rmsnorm_hT(G2)
            P.handoff(ARENA_XN + ARENA_HID + ARENA_CONV, ARENA_SCAN)
            for pc in range(2):
                sv, sn = wpiece(10 + pc)
                for tb in range(4):
                    bank = tb % 2
                    for kc in range(16):
                        op("pe", lambda e, kc=kc, tb=tb, bank=bank, sv=sv: e.matmul(
                            B[bank][:], hT[:, kc, tb * 128:(tb + 1) * 128], sv[:, kc, :], start=(kc == 0), stop=(kc == 15)),
                           reads=[sn, HT[kc]], writes=[f"B{bank}"])
                    copy(alt(), vtok[:, tb, pc * 512:(pc + 1) * 512], B[bank][:], [f"B{bank}"], ["vtok"])
            if t == NTu - 1:
                for half in range(2):
                    sa, na = wpiece(0 + half)
                    sga, nga = wpiece(2 + half)
                    for cc in range(4):
                        c = half * 4 + cc
                        st = c % 2
                        proj_chunk(0, sa, na, cc * 128)
                        proj_chunk(1, sga, nga, cc * 128)
                        t0 = TS[st][0]
                        op("act", lambda e, t0=t0: e.activation(out=t0, in_=B[1][:], func=AF.Sigmoid),
                           reads=["B1"], writes=[f"TS{st}_0"])
                        t1 = TS[st][1]
                        op("dve", lambda e, t0=t0, t1=t1: e.tensor_tensor(out=t1, in0=t0, in1=B[0][:], op=ALU.mult),
                           reads=[f"TS{st}_0", "B0"], writes=[f"TS{st}_1"])
                        op("dve", lambda e, t1=t1, st=st: e.tensor_copy(out=ktb[st][:, 0:16], in_=t1[:, ::-1][:, 0:16]),
                           reads=[f"TS{st}_1"], writes=[f"ktb{st}"])
                        op("sp", lambda e, st=st, c=c: e.dma_start(
                            out=upre_d[c * 128:(c + 1) * 128, 16 + Lc:32 + Lc], in_=ktb[st][:, 0:16]),
                           reads=[f"ktb{st}"], writes=["upre_pad1"], dma=True)
            for hgp in range(2):
                sz, nz = wpiece(8 + hgp)

                def zsrc(hi, sz=sz, nz=nz):
                    proj_chunk(0, sz, nz, hi * 128)
                    return B[0][:], "B0"

                scan_group(0, hgp, zsrc, None, None, pidx=1, state_only=True)

        def phase_b(i, outs):
            rows = slice(i * T, (i + 1) * T)
            P.handoff(ARENA_XN + ARENA_HID + ARENA_SCAN, ARENA_CONV)
            ur = upre_d.rearrange("(c p) l -> p c l", p=128)
            deps = [f"upre_d{j}_{c}" for j in (i - 1, i, i + 1) if 0 <= j < NT for c in range(8)] + ["upre_pad0", "upre_pad1"]
            op("sp", lambda e: e.dma_start(out=upad, in_=ur[:, :, 1 + i * T:1 + i * T + 542]),
               reads=deps, writes=["upad"], dma=True)
            def ln_stats(c):
                sq = cst[c % 2]
                op("pe", lambda e, c=c: e.matmul(B[0][:], onesF[:], uc[:, c, :], start=(c == 0), stop=(c == 7)),
                   reads=["onesF", f"uc{c}"], writes=["B0"])
                op("act", lambda e, c=c, sq=sq: e.activation(out=sq, in_=uc[:, c, :], func=AF.Square),
                   reads=[f"uc{c}"], writes=[f"cst{c % 2}"])
                op("pe", lambda e, c=c, sq=sq: e.matmul(B[1][:], onesF[:], sq, start=(c == 0), stop=(c == 7)),
                   reads=["onesF", f"cst{c % 2}"], writes=["B1"])

            for c in range(8):
                bank = 2 + c % 2
                for j in range(31):
                    k = (c * 31 + j) % 4
                    scale_copy("act" if (c * 31 + j) % 2 else "dve", dg[k], identB[:], cw[:, c, j:j + 1],
                               ["identB", "cw"], [f"dg{k}"])
                    op("pe", lambda e, c=c, j=j, k=k, bank=bank: e.matmul(
                        B[bank][:], dg[k], upad[:, c, j:j + T], start=(j == 0), stop=(j == 30)),
                       reads=[f"dg{k}", "upad"], writes=[f"B{bank}"])
                    if j == 8 and c > 0:
                        ln_stats(c - 1)
                op("act", lambda e, c=c, bank=bank: e.activation(out=uc[:, c, :], in_=B[bank][:], func=AF.Identity,
                                                                 scale=1.0, bias=CB[:, c:c + 1]),
                   reads=[f"B{bank}", "cols"], writes=[f"uc{c}"])
            ln_stats(7)
            mean, var, rs = cst[2], cst[3], cst[4]
            op("dve", lambda e: e.tensor_scalar(out=mean, in0=B[0][:], scalar1=1.0 / 1024, scalar2=None, op0=ALU.mult),
               reads=["B0"], writes=["cst2"])
            op("dve", lambda e: e.tensor_tensor(out=var, in0=mean, in1=mean, op=ALU.mult), reads=["cst2"], writes=["cst3"])
            op("dve", lambda e: e.scalar_tensor_tensor(out=var, in0=B[1][:], scalar=1.0 / 1024, in1=var,
                                                       op0=ALU.mult, op1=ALU.subtract),
               reads=["B1", "cst3"], writes=["cst3"])
            op("act", lambda e: e.activation(out=rs, in_=var, func=AF.Sqrt, scale=1.0, bias=epsc[:, 0:1]),
               reads=["cst3", "epsc"], writes=["cst4"])
            op("dve", lambda e: e.reciprocal(out=rs, in_=rs), reads=["cst4"], writes=["cst4"])
            for c in range(8):
                op("dve", lambda e, c=c: e.tensor_tensor(out=uc[:, c, :], in0=uc[:, c, :], in1=mean, op=ALU.subtract),
                   reads=[f"uc{c}", "cst2"], writes=[f"uc{c}"])
                op("dve", lambda e, c=c: e.tensor_tensor(out=uc[:, c, :], in0=uc[:, c, :], in1=rs, op=ALU.mult),
                   reads=[f"uc{c}", "cst4"], writes=[f"uc{c}"])
                op("act", lambda e, c=c: e.activation(out=hT[:, c, :], in_=uc[:, c, :], func=AF.Silu,
                                                      scale=LNW[:, c:c + 1], bias=LNB[:, c:c + 1]),
                   reads=[f"uc{c}", "cols"], writes=[HT[c]])
            for tb in range(4):
                op("sp", lambda e, tb=tb: e.dma_start(out=xt[:, tb, :], in_=x1_d[i * T + tb * 128:i * T + (tb + 1) * 128, :]),
                   reads=[f"x1_d{i}_{tb}"], writes=[f"xt{tb}"], dma=True)
            P.handoff(ARENA_XN + ARENA_HID + ARENA_CONV, ARENA_SCAN)
            op("sp", lambda e: e.dma_start(out=vtok[:], in_=v_d[rows, :].rearrange("(t p) f -> p t f", p=128)),
               reads=[f"v_d{i}"], writes=["vtok"], dma=True)
            for hgp in range(2):
                def zsrc(hi, hgp=hgp):
                    h = hgp * 4 + hi
                    st = hi % 2
                    t0 = TS[st][0]
                    op("sp", lambda e, t0=t0, h=h: e.dma_start(out=t0, in_=fm(zb_d, h, i)),
                       reads=[f"zb_d{i}_{h}"], writes=[f"TS{st}_0"], dma=True)
                    return t0, f"TS{st}_0"

                def qsrc(hi, hgp=hgp):
                    h = hgp * 4 + hi
                    st = hi % 2
                    t4 = TS[st][4]
                    op("sp", lambda e, t4=t4, h=h: e.dma_start(out=t4, in_=fm(q_d, h, i)),
                       reads=[f"q_d{i}_{h}"], writes=[f"TS{st}_4"], dma=True)
                    return t4, f"TS{st}_4"

                def finish(hi, h, ps, psn):
                    st = hi % 2
                    t5, t6 = TS[st][5], TS[st][6]
                    n5, n6 = f"TS{st}_5", f"TS{st}_6"
                    o_ = osb[st]
                    on = f"osb{st}"
                    op("sp", lambda e, t5=t5, h=h: e.dma_start(out=t5, in_=fm(of_d, h, i)),
                       reads=[f"of_d{i}_{h}"], writes=[n5], dma=True)
                    op("sp", lambda e, t6=t6, h=h: e.dma_start(out=t6, in_=fm(og_d, h, i)),
                       reads=[f"og_d{i}_{h}"], writes=[n6], dma=True)
                    op("dve", lambda e, o_=o_, t5=t5: e.tensor_tensor(out=o_, in0=ps[:], in1=t5, op=ALU.add),
                       reads=[psn, n5], writes=[on])
                    op("act", lambda e, o_=o_, t5=t5: e.activation(out=t5, in_=o_, func=AF.Square), reads=[on], writes=[n5])
                    op("pe", lambda e, t5=t5: e.matmul(B[0][:], onesF[:], t5, start=True, stop=True),
                       reads=["onesF", n5], writes=["B0"])
                    op("act", lambda e, t5=t5: e.activation(out=t5, in_=B[0][:], func=AF.Sqrt, scale=1.0 / 128,
                                                            bias=epsc[:, 0:1]), reads=["B0", "epsc"], writes=[n5])
                    op("dve", lambda e, t5=t5: e.reciprocal(out=t5, in_=t5), reads=[n5], writes=[n5])
                    op("dve", lambda e, o_=o_, t5=t5: e.tensor_tensor(out=o_, in0=o_, in1=t5, op=ALU.mult),
                       reads=[on, n5], writes=[on])
                    op("dve", lambda e, o_=o_, t6=t6, h=h: e.scalar_tensor_tensor(
                        out=hT[:, 8 + h, :], in0=o_, scalar=HGW[:, 0:1], in1=t6, op0=ALU.mult, op1=ALU.mult),
                       reads=[on, n6, "cols"], writes=[HT[8 + h]])

                scan_group(1, hgp, zsrc, qsrc, finish)
            wo = W["w_out"][0].rearrange("(kc p) f -> p kc f", p=128)
            for n in range(4):
                sv, sn = Wr.get(("w_out", n), wo[:, :, n * 512:(n + 1) * 512], [128, 16, 512])
                for tb in range(4):
                    bank = 4 + tb
                    for kc in range(16):
                        op("pe", lambda e, kc=kc, tb=tb, bank=bank, sv=sv: e.matmul(
                            B[bank][:], hT[:, kc, tb * 128:(tb + 1) * 128], sv[:, kc, :], start=(kc == 0), stop=(kc == 15)),
                           reads=[sn, HT[kc]], writes=[f"B{bank}"])
                    op("dve", lambda e, tb=tb, n=n, bank=bank: e.tensor_tensor(
                        out=xt[:, tb, n * 512:(n + 1) * 512], in0=B[bank][:], in1=xt[:, tb, n * 512:(n + 1) * 512], op=ALU.add),
                       reads=[f"B{bank}", f"xt{tb}"], writes=[f"xt{tb}"])
            P.handoff(ARENA_HID + ARENA_SCAN + ARENA_CONV, ARENA_XN)
            rmsnorm_hT(G3)
            ffn("ffn2")
            op("dve", lambda e: e.memset(ss[:], 0.0), writes=["ss"])
            for tb in range(4):
                op("act", lambda e, tb=tb: e.activation(
                    out=vtok[:, (tb % 2) * 2:(tb % 2) * 2 + 2, :].rearrange("p a b -> p (a b)"), in_=xt[:, tb, :],
                    func=AF.Square, accum_out=ss[:, tb:tb + 1]),
                   reads=[f"xt{tb}", "ss"], writes=["ss", "vtok"])
            op("act", lambda e: e.activation(out=rstd[:], in_=ss[:], func=AF.Sqrt, scale=1.0 / D, bias=epsc[:, 0:1]),
               reads=["ss", "epsc"], writes=["rstd"])
            op("dve", lambda e: e.reciprocal(out=rstd[:], in_=rstd[:]), reads=["rstd"], writes=["rstd"])
            for tb in range(4):
                op("dve", lambda e, tb=tb: e.scalar_tensor_tensor(
                    out=xt[:, tb, :], in0=xt[:, tb, :], scalar=rstd[:, tb:tb + 1], in1=fwb[:], op0=ALU.mult, op1=ALU.mult),
                   reads=[f"xt{tb}", "rstd", "fwb"], writes=[f"xt{tb}"])
            for tb in range(4):
                o = op("sp", lambda e, tb=tb: e.dma_start(out=y_d[i * T + tb * 128:i * T + (tb + 1) * 128, :], in_=xt[:, tb, :]),
                       reads=[f"xt{tb}"], writes=[f"y_d{i}_{tb}"], dma=True)
                outs.append(o)

        outs = []
        setup()
        for t in range(NTu):
            upstream(t)
        for i in range(NT):
            phase_a(i)
        for i in range(NT - 1, -1, -1):
            phase_b(i, outs)
        return Wr, outs

    Pd = Prog(nc, dry=True)
    Wr0, _ = make(Pd, None)
    order = Wr0.rec
    P = Prog(nc)
    Wr, outs = make(P, order)
    assert Wr.cur == len(order)
    P.finalize(final_waits=outs)
    return nc, P


_CACHE = {}


def run_cores(seqs, weights, NT, ups=None, wvariant=None):
    NTu = 0 if ups is None else ups[0].shape[0] // T
    key = (NT, NTu)
    if key not in _CACHE:
        _CACHE[key] = build(NT, NTu)[0]
    nc = _CACHE[key]
    wmap = {n: np.ascontiguousarray(np.asarray(weights[n], dtype=np.float32)) for n, _ in WNAMES}
    in_maps = []
    for c, s in enumerate(seqs):
        m = dict(wmap)
        if wvariant is not None and wvariant[c]:
            m.update(wvariant[c])
        m["x"] = np.ascontiguousarray(s, dtype=np.float32)
        if NTu:
            m["xu"] = np.ascontiguousarray(ups[c], dtype=np.float32)
        in_maps.append(m)
    res = run_bass_kernel_spmd(nc, in_maps, core_ids=list(range(8)))
    return [r["y"] for r in res.results]


def mirror_weights(weights):
    w_in = np.asarray(weights["w_in"], dtype=np.float32)
    wm = w_in.copy()
    wm[:, :, 3072:4096] = w_in[:, :, 4096:5120]
    wm[:, :, 4096:5120] = w_in[:, :, 3072:4096]
    return {
        "w_in": wm,
        "lb_fwd": np.ascontiguousarray(np.asarray(weights["lb_bwd"], dtype=np.float32)),
        "lb_bwd": np.ascontiguousarray(np.asarray(weights["lb_fwd"], dtype=np.float32)),
        "conv_w": np.ascontiguousarray(np.asarray(weights["conv_w"], dtype=np.float32)[:, ::-1, :]),
    }


def split_layout(xseqs, NT):
    No = NT * T
    owns, ups, mir = [], [], []
    for x in xseqs:
        L = x.shape[0]
        H = L // 2
        assert H <= No
        a = np.zeros((No, D), np.float32)
        a[No - H:] = x[:H]
        ua = np.zeros((No, D), np.float32)
        ua[No - H:] = x[H:][::-1]
        b = np.zeros((No, D), np.float32)
        b[No - H:] = x[H:][::-1]
        ub = np.zeros((No, D), np.float32)
        ub[No - H:] = x[:H]
        owns += [a, b]
        ups += [ua, ub]
        mir += [False, True]
    return owns, ups, mir


def run_split(xseqs, weights, NT):
    owns, ups, mir = split_layout(xseqs, NT)
    mw = mirror_weights(weights)
    ys = run_cores(owns, weights, NT, ups=ups, wvariant=[mw if m else None for m in mir])
    No = NT * T
    outs = []
    for k, x in enumerate(xseqs):
        H = x.shape[0] // 2
        ya = ys[2 * k][No - H:]
        yb = ys[2 * k + 1][No - H:][::-1]
        outs.append(np.concatenate([ya, yb], 0).astype(np.float32))
    return outs


def kernel(**inputs):
    xp = np.asarray(inputs["x_prompt"], dtype=np.float32)
    xs = np.asarray(inputs["x_sample"], dtype=np.float32)
    outs = run_split([xp[0], xp[1], xs[0], xs[1]], inputs, 8)
    y_prompt = np.stack([outs[0], outs[1]], 0)
    y_sample = np.stack([outs[2], outs[3]], 0)
    return (y_prompt, y_sample)
```

```python
import numpy as np
import concourse.bass as bass
import concourse.mybir as mybir
from concourse.bass_utils import run_bass_kernel_spmd

F32 = mybir.dt.float32
BF16 = mybir.dt.bfloat16
AF = mybir.ActivationFunctionType
ALU = mybir.AluOpType

SAME_ENGINE_SYNC = True
NDMASEM = 14
EPS = 1e-6
D = 2048
DFF = 5632
T = 512


class Buf:
    __slots__ = ("name", "w", "r")

    def __init__(self, name):
        self.name = name
        self.w = None
        self.r = []


class Op:
    __slots__ = ("eng", "emit", "deps", "dma", "idx", "sig", "cnt", "sem", "semval", "prevdma", "inc")


class Prog:
    ENGS = ("pe", "act", "dve", "pool", "sp")

    def __init__(self, nc, dry=False):
        self.nc = nc
        self.dry = dry
        self.ops = []
        self.by_eng = {e: [] for e in self.ENGS}
        self.ndma = {e: 0 for e in self.ENGS}
        self.dma_ops = {e: [] for e in self.ENGS}
        self.bufs = {}

    def buf(self, name):
        b = self.bufs.get(name)
        if b is None:
            b = self.bufs[name] = Buf(name)
        return b

    def handoff(self, old, new):
        if self.dry:
            return
        ops = []
        for n in old:
            b = self.buf(n)
            if b.w is not None:
                ops.append(b.w)
            ops.extend(b.r)
            b.w = None
            b.r = []
        best = {}
        keep = []
        for o in ops:
            if o.dma:
                keep.append(o)
            else:
                c = best.get(o.eng)
                if c is None or o.idx > c.idx:
                    best[o.eng] = o
        keep.extend(best.values())
        for n in new:
            self.buf(n).r.extend(keep)

    def op(self, eng, emit, reads=(), writes=(), dma=False, inc=16):
        if self.dry:
            return None
        o = Op()
        o.eng = eng
        o.emit = emit
        o.dma = dma
        o.idx = len(self.ops)
        o.sig = False
        o.cnt = None
        o.sem = None
        o.semval = None
        o.prevdma = None
        o.inc = inc
        deps = set()
        for b in reads:
            b = self.buf(b)
            if b.w is not None:
                deps.add(b.w)
            b.r.append(o)
        for b in writes:
            b = self.buf(b)
            if b.w is not None:
                deps.add(b.w)
            for r in b.r:
                if r is not o:
                    deps.add(r)
            b.w = o
            b.r = []
        deps.discard(o)
        best = {}
        keep = []
        for d in deps:
            if d.dma:
                keep.append(d)
            else:
                if d.eng == eng and not dma and (eng == "pe" or not SAME_ENGINE_SYNC):
                    continue
                cur = best.get(d.eng)
                if cur is None or d.idx > cur.idx:
                    best[d.eng] = d
        keep.extend(best.values())
        o.deps = keep
        for d in keep:
            d.sig = True
        if dma:
            k = self.ndma[eng]
            self.ndma[eng] += 1
            lst = self.dma_ops[eng]
            o.sem = k % NDMASEM
            prev = lst[k - NDMASEM].semval if k >= NDMASEM else 0
            o.semval = prev + inc
            if k >= NDMASEM:
                o.prevdma = lst[k - NDMASEM]
            lst.append(o)
        self.ops.append(o)
        self.by_eng[eng].append(o)
        return o

    def finalize(self, final_waits=()):
        nc = self.nc
        for e in self.ENGS:
            c = 0
            for o in self.by_eng[e]:
                if o.sig and not o.dma:
                    c += 1
                    o.cnt = c
        esem = {e: nc.alloc_semaphore(name=f"es_{e}") for e in self.ENGS}
        dsem = {e: [nc.alloc_semaphore(name=f"ds_{e}_{i}") for i in range(NDMASEM)]
                for e in ("sp", "pool", "act") if self.ndma[e] > 0}

        def run(e, engobj):
            waited = {}
            dwaited = {}
            for o in self.by_eng[e]:
                deps = list(o.deps)
                if o.prevdma is not None:
                    deps.append(o.prevdma)
                for d in deps:
                    if d.dma:
                        key = (d.eng, d.sem)
                        if dwaited.get(key, 0) >= d.semval:
                            continue
                        dwaited[key] = d.semval
                        engobj.wait_ge(dsem[d.eng][d.sem], d.semval)
                    else:
                        if waited.get(d.eng, 0) >= d.cnt:
                            continue
                        waited[d.eng] = d.cnt
                        engobj.wait_ge(esem[d.eng], d.cnt)
                ins = o.emit(engobj)
                if o.dma:
                    ins.then_inc(dsem[e][o.sem], o.inc)
                elif o.sig:
                    ins.then_inc(esem[e], 1)
            if e == "sp":
                for d in final_waits:
                    key = (d.eng, d.sem)
                    if dwaited.get(key, 0) >= d.semval:
                        continue
                    dwaited[key] = d.semval
                    engobj.wait_ge(dsem[d.eng][d.sem], d.semval)

        with nc.Block() as block:
            @block.tensor
            def _(eng):
                run("pe", eng)

            @block.scalar
            def _(eng):
                run("act", eng)

            @block.vector
            def _(eng):
                run("dve", eng)

            @block.gpsimd
            def _(eng):
                run("pool", eng)

            @block.sync
            def _(eng):
                run("sp", eng)


class WeightRing:
    NSLOT = 4
    LOOKAHEAD = 2

    def __init__(self, P, slots, order):
        self.P = P
        self.slots = slots
        self.order = order
        self.rec = []
        self.next_load = 0
        self.cur = 0

    def get(self, key, src, shape):
        if self.order is None:
            self.rec.append((key, src, shape))
            k = len(self.rec) - 1
        else:
            k = self.cur
            assert self.order[k][0] == key, (self.order[k][0], key)
            self.cur += 1
            lim = min(len(self.order), k + self.LOOKAHEAD + 1)
            while self.next_load < lim:
                self._load(self.next_load)
                self.next_load += 1
        s = k % self.NSLOT
        n = 1
        for d in shape[1:]:
            n *= d
        v = self.slots[s][:, 0:n]
        if len(shape) == 3:
            v = v.rearrange("p (a b) -> p a b", b=shape[2])
        return v, f"wslot{s}"

    def _load(self, k):
        key, src, shape = self.order[k]
        s = k % self.NSLOT
        n = 1
        for d in shape[1:]:
            n *= d
        v = self.slots[s][:, 0:n]
        if len(shape) == 3:
            v = v.rearrange("p (a b) -> p a b", b=shape[2])
        self.P.op("pool", lambda e, v=v, src=src: e.dma_start(out=v, in_=src), writes=[f"wslot{s}"], dma=True)


WNAMES = [
    ("ffn1_norm", [1, D]), ("ffn1_w1", [1, D, DFF]), ("ffn1_w3", [1, D, DFF]), ("ffn1_w2", [1, DFF, D]),
    ("mix_norm", [1, D]), ("w_in", [1, D, 7168]), ("conv_w", [1, 31, 1024]), ("conv_b", [1, 1024]),
    ("conv_ln_w", [1, 1024]), ("conv_ln_b", [1, 1024]), ("lb_fwd", [2, 1024]), ("lb_bwd", [2, 1024]),
    ("hg_norm", [1, 128]), ("w_out", [1, D, D]), ("ffn2_norm", [1, D]), ("ffn2_w1", [1, D, DFF]),
    ("ffn2_w3", [1, D, DFF]), ("ffn2_w2", [1, DFF, D]), ("final_norm", [D]),
]


def build(NT, NTu=0):
    nc = bass.Bass("TRN2", target_bir_lowering=False)
    Lc = NT * T
    x_d = nc.dram_tensor("x", [Lc, D], F32, kind="ExternalInput").ap()
    xu_d = nc.dram_tensor("xu", [NTu * T, D], F32, kind="ExternalInput").ap() if NTu else None
    W = {n: nc.dram_tensor(n, s, F32, kind="ExternalInput").ap() for n, s in WNAMES}
    y_d = nc.dram_tensor("y", [Lc, D], F32, kind="ExternalOutput").ap()
    x1_d = nc.dram_tensor("x1_d", [Lc, D], F32).ap()
    upre_d = nc.dram_tensor("upre_d", [1024, Lc + 32], BF16).ap()
    q_d = nc.dram_tensor("q_d", [1024, Lc], F32).ap()
    zb_d = nc.dram_tensor("zb_d", [1024, Lc], F32).ap()
    og_d = nc.dram_tensor("og_d", [1024, Lc], F32).ap()
    of_d = nc.dram_tensor("of_d", [1024, Lc], F32).ap()
    v_d = nc.dram_tensor("v_d", [Lc, 1024], BF16).ap()

    sb = nc.alloc_sbuf_tensor
    xt = sb("xt", [128, 4, D], F32)
    hT = sb("hT", [128, 16, T], BF16)
    hid = sb("hid", [128, 44, T], BF16)
    slots = [sb(f"wslot{i}", [128, 8192], BF16) for i in range(WeightRing.NSLOT)]
    sg = [sb(f"sg{i}", [128, T], F32) for i in range(2)]
    vtok = sb("vtok", [128, 4, 1024], BF16)
    stage = sb("stage", [128, 128], F32)
    cols = sb("cols", [128, 128], F32)
    identF = sb("identF", [128, 128], F32)
    identB = sb("identB", [128, 128], BF16)
    onesF = sb("onesF", [128, 128], F32)
    cwrow = sb("cwrow", [31, 1024], F32)
    cw = sb("cw", [128, 8, 31], F32)
    lbv = sb("lbv", [128, 16], F32)
    oml = sb("oml", [128, 16], F32)
    noml = sb("noml", [128, 16], F32)
    lbd = sb("lbd", [128, 16], F32)
    maskf = sb("maskf", [128, T], F32)
    maskb = sb("maskb", [128, T], F32)
    Mf = sb("Mf", [128, T], F32)
    Mb = sb("Mb", [128, T], F32)
    fwb = sb("fwb", [128, D], F32)
    zpad = sb("zpad", [128, 8, 16], BF16)
    ss = sb("ss", [128, 4], F32)
    epsc = sb("epsc", [128, 1], F32)
    rstd = sb("rstd", [128, 4], F32)
    S32 = [sb(f"S32_{d}", [128, 8, 128], F32) for d in range(2)]
    Sbf = [sb(f"Sbf_{d}", [128, 8, 128], BF16) for d in range(2)]
    Ssc = sb("Ssc", [128, 4, 128], F32)
    ebe = sb("ebe", [128, 8, 8], F32)
    B = [nc.alloc_psum_tensor(f"B{i}", [128, T], F32) for i in range(8)]

    hidflat = hid[:].rearrange("p a b -> p (a b)")

    def arena_f32(off_bytes, n):
        return hidflat[:, off_bytes // 2: off_bytes // 2 + 2 * n].bitcast(F32)

    def arena_bf16(off_bytes, n):
        return hidflat[:, off_bytes // 2: off_bytes // 2 + n]

    xn = hid[:, 0:16, :].rearrange("p (t a) f -> p t (a f)", t=4)
    KB = 1024
    TS = [[arena_f32((s * 7 + i) * 2 * KB, T) for i in range(7)] for s in range(2)]
    qt = [arena_bf16(28 * KB + i * KB, T) for i in range(4)]
    ktok = [arena_bf16(32 * KB + i * KB, T).rearrange("p (g k) -> p g k", k=128) for i in range(4)]
    scT = [arena_bf16(36 * KB + i * KB, T) for i in range(2)]
    ktb = [arena_bf16(38 * KB + i * KB, T) for i in range(2)]
    osb = [arena_f32(40 * KB + i * 2 * KB, T) for i in range(2)]
    upad = arena_bf16(0, 8 * 542).rearrange("p (c t) -> p c t", t=542)
    uc = arena_f32(8704, 8 * T).rearrange("p (c t) -> p c t", t=T)
    cst = [arena_f32(25088 + i * 2048, T) for i in range(5)]
    dg = [arena_bf16(35328 + i * 256, 128) for i in range(4)]
    ARENA_XN = ["xn0", "xn1", "xn2", "xn3"]
    ARENA_HID = [f"hid{j}" for j in range(44)]
    ARENA_SCAN = [f"TS{s}_{i}" for s in range(2) for i in range(7)] + [f"qt{i}" for i in range(4)] + \
                 [f"ktok{i}" for i in range(4)] + ["scT0", "scT1", "ktb0", "ktb1", "osb0", "osb1"]
    ARENA_CONV = ["upad"] + [f"uc{c}" for c in range(8)] + [f"cst{i}" for i in range(5)] + [f"dg{i}" for i in range(4)]

    Bt = [b[:, 0:256].bitcast(BF16) for b in B]

    def make(P, order):
        Wr = WeightRing(P, slots, order)
        op = P.op
        cnt = [0]

        def alt():
            cnt[0] += 1
            return "act" if cnt[0] % 2 else "dve"

        def scale_copy(eng, out, in_, scal, reads, writes):
            if eng == "act":
                op("act", lambda e: e.activation(out=out, in_=in_, func=AF.Copy, scale=scal), reads, writes)
            else:
                op("dve", lambda e: e.tensor_scalar(out=out, in0=in_, scalar1=scal, scalar2=None, op0=ALU.mult),
                   reads, writes)

        def copy(eng, out, in_, reads, writes):
            if eng == "act":
                op("act", lambda e: e.activation(out=out, in_=in_, func=AF.Copy), reads, writes)
            else:
                op("dve", lambda e: e.tensor_copy(out=out, in_=in_), reads, writes)

        def setup():
            op("dve", lambda e: e.memset(stage[:], 0.0), writes=["stage"])
            op("dve", lambda e: e.memset(epsc[:], EPS), writes=["epsc"])
            rows = [("ffn1_norm", 0, 16), ("mix_norm", 16, 16), ("ffn2_norm", 32, 16), ("conv_b", 48, 8),
                    ("conv_ln_w", 56, 8), ("conv_ln_b", 64, 8)]
            for n, r0, nr in rows:
                src = W[n][0].rearrange("(k p) -> k p", p=128)
                op("sp", lambda e, src=src, r0=r0, nr=nr: e.dma_start(out=stage[r0:r0 + nr, :], in_=src),
                   reads=[], writes=["stage"], dma=True)
            for n, r0 in (("lb_fwd", 72), ("lb_bwd", 88)):
                for s_ in range(2):
                    src = W[n][s_].rearrange("(k p) -> k p", p=128)
                    op("sp", lambda e, src=src, r=r0 + 8 * s_: e.dma_start(out=stage[r:r + 8, :], in_=src),
                       writes=["stage"], dma=True)
            op("sp", lambda e: e.dma_start(out=stage[104:105, :], in_=W["hg_norm"]), writes=["stage"], dma=True)
            op("sp", lambda e: e.dma_start(out=cwrow[:], in_=W["conv_w"][0]), writes=["cwrow"], dma=True)
            op("sp", lambda e: e.dma_start(out=fwb[:], in_=W["final_norm"].partition_broadcast(128)),
               writes=["fwb"], dma=True)
            op("pool", lambda e: e.memset(identF[:], 0.0), writes=["identF"])
            op("pool", lambda e: e.affine_select(out=identF[:], in_=identF[:], pattern=[[-1, 128]],
                                                 compare_op=ALU.not_equal, fill=1.0, base=0, channel_multiplier=1),
               reads=["identF"], writes=["identF"])
            op("dve", lambda e: e.tensor_copy(out=identB[:], in_=identF[:]), reads=["identF"], writes=["identB"])
            op("dve", lambda e: e.memset(onesF[:], 1.0), writes=["onesF"])
            op("dve", lambda e: e.memset(maskf[:], 1.0), writes=["maskf"])
            op("dve", lambda e: e.memset(maskf[:].rearrange("p (c t) -> p c t", t=64)[:, :, 0:1], 0.0),
               writes=["maskf"])
            op("dve", lambda e: e.memset(maskb[:], 1.0), writes=["maskb"])
            op("dve", lambda e: e.memset(maskb[:].rearrange("p (c t) -> p c t", t=64)[:, :, 63:64], 0.0),
               writes=["maskb"])
            op("pool", lambda e: e.memset(Mf[:], 1.0), writes=["Mf"])
            op("pool", lambda e: e.affine_select(out=Mf[:, 0:128], in_=Mf[:, 0:128], pattern=[[1, 128]],
                                                 compare_op=ALU.is_ge, fill=0.0, base=0, channel_multiplier=-1),
               reads=["Mf"], writes=["Mf"])
            op("pool", lambda e: e.memset(Mf[0:64, 64:128], 0.0), reads=["Mf"], writes=["Mf"])
            op("pool", lambda e: e.memset(Mb[:], 1.0), writes=["Mb"])
            op("pool", lambda e: e.affine_select(out=Mb[:, 0:128], in_=Mb[:, 0:128], pattern=[[-1, 128]],
                                                 compare_op=ALU.is_ge, fill=0.0, base=0, channel_multiplier=1),
               reads=["Mb"], writes=["Mb"])
            op("pool", lambda e: e.memset(Mb[64:128, 0:64], 0.0), reads=["Mb"], writes=["Mb"])
            for M_, nm in ((Mf, "Mf"), (Mb, "Mb")):
                for g in range(1, 4):
                    op("pool", lambda e, M_=M_, g=g: e.tensor_copy(out=M_[:, g * 128:(g + 1) * 128], in_=M_[:, 0:128]),
                       reads=[nm], writes=[nm])
            op("pe", lambda e: e.transpose(out=B[0][:, 0:128], in_=stage[:], identity=identF[:]),
               reads=["stage", "identF"], writes=["B0"])
            op("act", lambda e: e.activation(out=cols[:], in_=B[0][:, 0:128], func=AF.Copy), reads=["B0"], writes=["cols"])
            for c in range(8):
                op("pe", lambda e, c=c: e.transpose(out=B[1][:, 0:31], in_=cwrow[0:31, c * 128:(c + 1) * 128],
                                                    identity=identF[0:31, 0:31]),
                   reads=["cwrow", "identF"], writes=["B1"])
                op("dve", lambda e, c=c: e.tensor_copy(out=cw[:, c, :], in_=B[1][:, 0:31]), reads=["B1"], writes=["cw"])
            op("dve", lambda e: e.tensor_tensor(out=lbd[:, 0:8], in0=cols[:, 72:80], in1=cols[:, 80:88], op=ALU.subtract),
               reads=["cols"], writes=["lbd"])
            op("dve", lambda e: e.tensor_tensor(out=lbd[:, 8:16], in0=cols[:, 88:96], in1=cols[:, 96:104], op=ALU.subtract),
               reads=["cols"], writes=["lbd"])
            op("act", lambda e: e.activation(out=lbv[:], in_=lbd[:], func=AF.Sigmoid), reads=["lbd"], writes=["lbv"])
            op("dve", lambda e: e.tensor_scalar(out=oml[:], in0=lbv[:], scalar1=-1.0, scalar2=1.0, op0=ALU.mult, op1=ALU.add),
               reads=["lbv"], writes=["oml"])
            op("dve", lambda e: e.tensor_scalar(out=noml[:], in0=lbv[:], scalar1=-1.0, scalar2=None, op0=ALU.add),
               reads=["lbv"], writes=["noml"])
            op("dve", lambda e: e.memset(zpad[:], 0.0), writes=["zpad"])
            ur = upre_d.rearrange("(c p) l -> p c l", p=128)
            op("sp", lambda e: e.dma_start(out=ur[:, :, 0:16], in_=zpad[:]), reads=["zpad"], writes=["upre_pad0"], dma=True)
            if not NTu:
                op("sp", lambda e: e.dma_start(out=ur[:, :, 16 + Lc:32 + Lc], in_=zpad[:]), reads=["zpad"],
                   writes=["upre_pad1"], dma=True)
            for d in range(2):
                op("dve", lambda e, d=d: e.memset(S32[d][:], 0.0), writes=[f"S32_{d}_{h}" for h in range(8)])
                op("dve", lambda e, d=d: e.memset(Sbf[d][:], 0.0), writes=[f"Sbf_{d}_{h}" for h in range(8)])

        G1 = cols[:, 0:16]
        G2 = cols[:, 16:32]
        G3 = cols[:, 32:48]
        CB = cols[:, 48:56]
        LNW = cols[:, 56:64]
        LNB = cols[:, 64:72]
        HGW = cols[:, 104:105]

        def rmsnorm_hT(gcols):
            XT = [f"xt{tb}" for tb in range(4)]
            op("dve", lambda e: e.memset(ss[:], 0.0), writes=["ss"])
            for tb in range(4):
                op("act", lambda e, tb=tb: e.activation(out=xn[:, tb, :], in_=xt[:, tb, :], func=AF.Square,
                                                        accum_out=ss[:, tb:tb + 1]),
                   reads=[XT[tb], "ss"], writes=[f"xn{tb}", "ss"])
            op("act", lambda e: e.activation(out=rstd[:], in_=ss[:], func=AF.Sqrt, scale=1.0 / D, bias=epsc[:, 0:1]),
               reads=["ss", "epsc"], writes=["rstd"])
            op("dve", lambda e: e.reciprocal(out=rstd[:], in_=rstd[:]), reads=["rstd"], writes=["rstd"])
            for tb in range(4):
                scale_copy(alt(), xn[:, tb, :], xt[:, tb, :], rstd[:, tb:tb + 1], [XT[tb], "rstd"], [f"xn{tb}"])
            for kc in range(16):
                bk = kc % 4
                for tb in range(4):
                    op("pe", lambda e, kc=kc, tb=tb, bk=bk: e.transpose(
                        out=Bt[bk][:, tb * 128:(tb + 1) * 128], in_=xn[:, tb, kc * 128:(kc + 1) * 128], identity=identB[:]),
                       reads=[f"xn{tb}", "identB"], writes=[f"B{bk}"])
                scale_copy(alt(), hT[:, kc, :], Bt[bk][:, :], gcols[:, kc:kc + 1], [f"B{bk}", "cols"], [f"hT{kc}"])

        HT = [f"hT{kc}" for kc in range(16)]

        def ffn(pre):
            w1 = W[pre + "_w1"][0].rearrange("(kc p) f -> p kc f", p=128)
            w3 = W[pre + "_w3"][0].rearrange("(kc p) f -> p kc f", p=128)
            w2 = W[pre + "_w2"][0].rearrange("(kc p) f -> p kc f", p=128)
            P.handoff(ARENA_XN + ARENA_SCAN + ARENA_CONV, ARENA_HID)
            for blk in range(11):
                s1, n1 = Wr.get((pre, "w1", blk), w1[:, :, blk * 512:(blk + 1) * 512], [128, 16, 512])
                s3, n3 = Wr.get((pre, "w3", blk), w3[:, :, blk * 512:(blk + 1) * 512], [128, 16, 512])
                for j in range(4):
                    jj = blk * 4 + j
                    gb = 2 * (jj % 2)
                    ub = gb + 1
                    for kc in range(16):
                        op("pe", lambda e, kc=kc, j=j, gb=gb, s1=s1: e.matmul(
                            B[gb][:], s1[:, kc, j * 128:(j + 1) * 128], hT[:, kc, :], start=(kc == 0), stop=(kc == 15)),
                           reads=[n1, HT[kc]], writes=[f"B{gb}"])
                    for kc in range(16):
                        op("pe", lambda e, kc=kc, j=j, ub=ub, s3=s3: e.matmul(
                            B[ub][:], s3[:, kc, j * 128:(j + 1) * 128], hT[:, kc, :], start=(kc == 0), stop=(kc == 15)),
                           reads=[n3, HT[kc]], writes=[f"B{ub}"])
                    sgi = jj % 2
                    op("act", lambda e, gb=gb, sgi=sgi: e.activation(out=sg[sgi][:], in_=B[gb][:], func=AF.Silu),
                       reads=[f"B{gb}"], writes=[f"sg{sgi}"])
                    op("dve", lambda e, ub=ub, sgi=sgi, jj=jj: e.tensor_tensor(
                        out=hid[:, jj, :], in0=sg[sgi][:], in1=B[ub][:], op=ALU.mult),
                       reads=[f"sg{sgi}", f"B{ub}"], writes=[f"hid{jj}"])
            for n in range(4):
                for q in range(4):
                    s2, n2 = Wr.get((pre, "w2", n, q), w2[:, q * 11:(q + 1) * 11, n * 512:(n + 1) * 512], [128, 11, 512])
                    for tb in range(4):
                        for kc in range(11):
                            op("pe", lambda e, tb=tb, kc=kc, q=q, s2=s2: e.matmul(
                                B[4 + tb][:], hid[:, q * 11 + kc, tb * 128:(tb + 1) * 128], s2[:, kc, :],
                                start=(q == 0 and kc == 0), stop=(q == 3 and kc == 10)),
                               reads=[n2, f"hid{q * 11 + kc}"], writes=[f"B{4 + tb}"])
                for tb in range(4):
                    op("dve", lambda e, tb=tb, n=n: e.scalar_tensor_tensor(
                        out=xt[:, tb, n * 512:(n + 1) * 512], in0=B[4 + tb][:], scalar=0.5,
                        in1=xt[:, tb, n * 512:(n + 1) * 512], op0=ALU.mult, op1=ALU.add),
                       reads=[f"B{4 + tb}", f"xt{tb}"], writes=[f"xt{tb}"])

        def scan_group(d, hgp, zsrc, qsrc, finish, pidx=None, state_only=False):
            p = d if pidx is None else pidx
            Md, Mn = (Mf, "Mf") if d == 0 else (Mb, "Mb")
            mk, mkn = (maskf, "maskf") if d == 0 else (maskb, "maskb")
            srcs = {}
            for hi in range(4):
                h = hgp * 4 + hi
                st = hi % 2
                s_, g_, b_, eb_ = TS[st][0], TS[st][1], TS[st][2], TS[st][3]
                sN, gN, bN, ebN = (f"TS{st}_{i}" for i in range(4))
                if hi == 0:
                    srcs[0] = (zsrc(0), None if state_only else qsrc(0))
                (z, zN), qq = srcs[hi]
                if not state_only:
                    q_, qN = qq
                lc = p * 8 + h
                op("act", lambda e, s_=s_, z=z: e.activation(out=s_, in_=z, func=AF.Sigmoid), reads=[zN], writes=[sN])
                if hi + 1 < 4:
                    znext = zsrc(hi + 1)
                op("act", lambda e, s_=s_, g_=g_, lc=lc: e.activation(
                    out=g_, in_=s_, func=AF.Ln, scale=oml[:, lc:lc + 1], bias=lbv[:, lc:lc + 1]),
                   reads=[sN, "oml", "lbv"], writes=[gN])
                op("dve", lambda e, s_=s_, lc=lc: e.tensor_scalar(
                    out=s_, in0=s_, scalar1=noml[:, lc:lc + 1], scalar2=oml[:, lc:lc + 1], op0=ALU.mult, op1=ALU.add),
                   reads=[sN, "noml", "oml"], writes=[sN])
                if d == 0:
                    op("dve", lambda e, b_=b_, g_=g_: e.tensor_tensor_scan(
                        out=b_, data0=mk[:], data1=g_, initial=0.0, op0=ALU.mult, op1=ALU.add),
                       reads=[gN, mkn], writes=[bN])
                else:
                    op("dve", lambda e, b_=b_, g_=g_: e.tensor_tensor_scan(
                        out=b_[:, ::-1], data0=mk[:, ::-1], data1=g_[:, ::-1], initial=0.0, op0=ALU.mult, op1=ALU.add),
                       reads=[gN, mkn], writes=[bN])
                op("act", lambda e, b_=b_, eb_=eb_: e.activation(out=eb_, in_=b_, func=AF.Exp), reads=[bN], writes=[ebN])
                op("act", lambda e, b_=b_: e.activation(out=b_, in_=b_, func=AF.Exp, scale=-1.0), reads=[bN], writes=[bN])
                if hi + 1 < 4:
                    srcs[hi + 1] = (znext, None if state_only else qsrc(hi + 1))
                ecol = 63 if d == 0 else 0
                op("dve", lambda e, eb_=eb_, h=h, ecol=ecol: e.tensor_copy(
                    out=ebe[:, h, :], in_=eb_.rearrange("p (c t) -> p c t", t=64)[:, :, ecol]),
                   reads=[ebN], writes=[f"ebe{h}"])
                if not state_only:
                    op("dve", lambda e, hi=hi, q_=q_, eb_=eb_: e.tensor_tensor(out=qt[hi], in0=q_, in1=eb_, op=ALU.mult),
                       reads=[qN, ebN], writes=[f"qt{hi}"])
                op("dve", lambda e, st=st, s_=s_, b_=b_: e.tensor_tensor(out=ktb[st], in0=s_, in1=b_, op=ALU.mult),
                   reads=[sN, bN], writes=[f"ktb{st}"])
                if not state_only:
                    for g in range(4):
                        op("pe", lambda e, g=g, st=st, hi=hi: e.matmul(
                            B[2][:, g * 128:(g + 1) * 128], ktb[st][:, g * 128:(g + 1) * 128],
                            qt[hi][:, g * 128:(g + 1) * 128], start=True, stop=True),
                           reads=[f"ktb{st}", f"qt{hi}"], writes=["B2"])
                    op("dve", lambda e, st=st: e.tensor_tensor(out=scT[st], in0=B[2][:], in1=Md[:], op=ALU.mult),
                       reads=["B2", Mn], writes=[f"scT{st}"])
                for g in range(4):
                    op("pe", lambda e, g=g, st=st: e.transpose(
                        out=Bt[1][:, g * 128:(g + 1) * 128], in_=ktb[st][:, g * 128:(g + 1) * 128], identity=identB[:]),
                       reads=[f"ktb{st}", "identB"], writes=["B1"])
                op("act", lambda e, hi=hi: e.activation(out=ktok[hi].rearrange("p g k -> p (g k)"), in_=Bt[1][:, :],
                                                        func=AF.Copy), reads=["B1"], writes=[f"ktok{hi}"])
                if not state_only:
                    for g in range(4):
                        op("pe", lambda e, g=g, st=st, hi=hi, h=h: e.matmul(
                            B[4 + hi][:, g * 128:(g + 1) * 128], vtok[:, g, h * 128:(h + 1) * 128],
                            scT[st][:, g * 128:(g + 1) * 128], start=(g == 0), stop=False),
                           reads=["vtok", f"scT{st}"], writes=[f"B{4 + hi}"])
            corder = list(range(8)) if d == 0 else list(range(7, -1, -1))
            for c in corder:
                g = c // 2
                r0 = (c % 2) * 64
                for hi in range(4):
                    h = hgp * 4 + hi
                    op("pe", lambda e, hi=hi, h=h, g=g, r0=r0: e.matmul(
                        B[3][:, hi * 128:(hi + 1) * 128], ktok[hi][r0:r0 + 64, g, :], vtok[r0:r0 + 64, g, h * 128:(h + 1) * 128],
                        start=True, stop=True), reads=[f"ktok{hi}", "vtok"], writes=["B3"])
                for hi in range(4):
                    h = hgp * 4 + hi
                    if state_only:
                        break
                    op("pe", lambda e, c=c, hi=hi, h=h: e.matmul(
                        B[4 + hi][:, c * 64:(c + 1) * 64], Sbf[p][:, h, :], qt[hi][:, c * 64:(c + 1) * 64],
                        start=False, stop=True), reads=[f"Sbf_{p}_{h}", f"qt{hi}"], writes=[f"B{4 + hi}"])
                h0 = hgp * 4
                SN = [f"S32_{p}_{h0 + k}" for k in range(4)]
                SBN = [f"Sbf_{p}_{h0 + k}" for k in range(4)]
                EN = [f"ebe{h0 + k}" for k in range(4)]
                op("dve", lambda e, h0=h0: e.tensor_tensor(
                    out=Ssc[:], in0=B[3][:].rearrange("p (a b) -> p a b", b=128), in1=S32[p][:, h0:h0 + 4, :], op=ALU.add),
                   reads=["B3"] + SN, writes=["Ssc"])
                op("dve", lambda e, h0=h0, c=c: e.tensor_tensor(
                    out=S32[p][:, h0:h0 + 4, :], in0=Ssc[:], in1=ebe[:, h0:h0 + 4, c:c + 1].to_broadcast([128, 4, 128]),
                    op=ALU.mult), reads=["Ssc"] + EN, writes=SN)
                op("act", lambda e, h0=h0: e.activation(out=Sbf[p][:, h0:h0 + 4, :], in_=S32[p][:, h0:h0 + 4, :], func=AF.Copy),
                   reads=SN, writes=SBN)
            if not state_only:
                for hi in range(4):
                    finish(hi, hgp * 4 + hi, B[4 + hi], f"B{4 + hi}")

        def proj_chunk(bank, sv, sn, off):
            for kc in range(16):
                op("pe", lambda e, kc=kc: e.matmul(B[bank][:], sv[:, kc, off:off + 128], hT[:, kc, :],
                                                   start=(kc == 0), stop=(kc == 15)),
                   reads=[sn, HT[kc]], writes=[f"B{bank}"])

        win = W["w_in"][0].rearrange("(kc p) f -> p kc f", p=128)

        def wpiece(i):
            return Wr.get(("w_in", i), win[:, :, i * 512:(i + 1) * 512], [128, 16, 512])

        def fm(dram, h, tile):
            return dram[h * 128:(h + 1) * 128, tile * T:(tile + 1) * T]

        def phase_a(i):
            rows = slice(i * T, (i + 1) * T)
            for tb in range(4):
                op("sp", lambda e, tb=tb: e.dma_start(out=xt[:, tb, :], in_=x_d[i * T + tb * 128:i * T + (tb + 1) * 128, :]),
                   writes=[f"xt{tb}"], dma=True)
            P.handoff(ARENA_HID + ARENA_SCAN + ARENA_CONV, ARENA_XN)
            rmsnorm_hT(G1)
            ffn("ffn1")
            for tb in range(4):
                op("sp", lambda e, tb=tb: e.dma_start(out=x1_d[i * T + tb * 128:i * T + (tb + 1) * 128, :], in_=xt[:, tb, :]),
                   reads=[f"xt{tb}"], writes=[f"x1_d{i}_{tb}"], dma=True)
            P.handoff(ARENA_HID + ARENA_SCAN + ARENA_CONV, ARENA_XN)
            rmsnorm_hT(G2)
            P.handoff(ARENA_XN + ARENA_HID + ARENA_CONV, ARENA_SCAN)
            for pc in range(2):
                sv, sn = wpiece(10 + pc)
                for tb in range(4):
                    bank = tb % 4
                    for kc in range(16):
                        op("pe", lambda e, kc=kc, tb=tb, bank=bank, sv=sv: e.matmul(
                            B[bank][:], hT[:, kc, tb * 128:(tb + 1) * 128], sv[:, kc, :], start=(kc == 0), stop=(kc == 15)),
                           reads=[sn, HT[kc]], writes=[f"B{bank}"])
                    copy(alt(), vtok[:, tb, pc * 512:(pc + 1) * 512], B[bank][:], [f"B{bank}"], ["vtok"])
            op("sp", lambda e: e.dma_start(out=v_d[rows, :].rearrange("(t p) f -> p t f", p=128), in_=vtok[:]),
               reads=["vtok"], writes=[f"v_d{i}"], dma=True)
            for half in range(2):
                sa, na = wpiece(0 + half)
                sga, nga = wpiece(2 + half)
                for cc in range(4):
                    c = half * 4 + cc
                    st = c % 2
                    proj_chunk(0, sa, na, cc * 128)
                    proj_chunk(1, sga, nga, cc * 128)
                    t0 = TS[st][0]
                    op("act", lambda e, t0=t0: e.activation(out=t0, in_=B[1][:], func=AF.Sigmoid),
                       reads=["B1"], writes=[f"TS{st}_0"])
                    op("dve", lambda e, t0=t0, st=st: e.tensor_tensor(out=ktb[st], in0=t0, in1=B[0][:], op=ALU.mult),
                       reads=[f"TS{st}_0", "B0"], writes=[f"ktb{st}"])
                    op("sp", lambda e, st=st, c=c: e.dma_start(
                        out=upre_d[c * 128:(c + 1) * 128, 16 + i * T:16 + (i + 1) * T], in_=ktb[st]),
                       reads=[f"ktb{st}"], writes=[f"upre_d{i}_{c}"], dma=True)
            for grp, dram, fn, nm in ((12, og_d, AF.Silu, "og"), (8, zb_d, AF.Copy, "zb")):
                for half in range(2):
                    sv, sn = wpiece(grp + half)
                    for cc in range(4):
                        h = half * 4 + cc
                        st = h % 2
                        proj_chunk(h % 2, sv, sn, cc * 128)
                        t2 = TS[st][2]
                        op("act", lambda e, t2=t2, bk=h % 2, fn=fn: e.activation(out=t2, in_=B[bk][:], func=fn),
                           reads=[f"B{h % 2}"], writes=[f"TS{st}_2"])
                        op("sp", lambda e, t2=t2, h=h, dram=dram: e.dma_start(out=fm(dram, h, i), in_=t2),
                           reads=[f"TS{st}_2"], writes=[f"{nm}_d{i}_{h}"], dma=True)
            for hgp in range(2):
                sq_, nq = wpiece(4 + hgp)
                sz, nz = wpiece(6 + hgp)
                qbufs = {}

                def zsrc(hi, sz=sz, nz=nz):
                    proj_chunk(0, sz, nz, hi * 128)
                    return B[0][:], "B0"

                def qsrc(hi, sq_=sq_, nq=nq, hgp=hgp):
                    h = hgp * 4 + hi
                    st = hi % 2
                    proj_chunk(1, sq_, nq, hi * 128)
                    t4 = TS[st][4]
                    op("act", lambda e, t4=t4: e.activation(out=t4, in_=B[1][:], func=AF.Silu),
                       reads=["B1"], writes=[f"TS{st}_4"])
                    op("sp", lambda e, t4=t4, h=h: e.dma_start(out=fm(q_d, h, i), in_=t4),
                       reads=[f"TS{st}_4"], writes=[f"q_d{i}_{h}"], dma=True)
                    return t4, f"TS{st}_4"

                def finish(hi, h, ps, psn):
                    st = hi % 2
                    copy(alt(), osb[st], ps[:], [psn], [f"osb{st}"])
                    op("sp", lambda e, st=st, h=h: e.dma_start(out=fm(of_d, h, i), in_=osb[st]),
                       reads=[f"osb{st}"], writes=[f"of_d{i}_{h}"], dma=True)

                scan_group(0, hgp, zsrc, qsrc, finish)

        def upstream(t):
            rows = slice(t * T, (t + 1) * T)
            for tb in range(4):
                op("sp", lambda e, tb=tb: e.dma_start(out=xt[:, tb, :], in_=xu_d[t * T + tb * 128:t * T + (tb + 1) * 128, :]),
                   writes=[f"xt{tb}"], dma=True)
            P.handoff(ARENA_HID + ARENA_SCAN + ARENA_CONV, ARENA_XN)
            rmsnorm_hT(G1)
            ffn("ffn1")
            P.handoff(ARENA_HID + ARENA_SCAN + ARENA_CONV, ARENA_XN)
            rmsnorm_hT(G2)
            P.handoff(ARENA_XN + ARENA_HID + ARENA_CONV, ARENA_SCAN)
            for pc in range(2):
                sv, sn = wpiece(10 + pc)
                for tb in range(4):
                    bank = tb % 4
                    for kc in range(16):
                        op("pe", lambda e, kc=kc, tb=tb, bank=bank, sv=sv: e.matmul(
                            B[bank][:], hT[:, kc, tb * 128:(tb + 1) * 128], sv[:, kc, :], start=(kc == 0), stop=(kc == 15)),
                           reads=[sn, HT[kc]], writes=[f"B{bank}"])
                    copy(alt(), vtok[:, tb, pc * 512:(pc + 1) * 512], B[bank][:], [f"B{bank}"], ["vtok"])
            if t == NTu - 1:
                for half in range(2):
                    sa, na = wpiece(0 + half)
                    sga, nga = wpiece(2 + half)
                    for cc in range(4):
                        c = half * 4 + cc
                        st = c % 2
                        proj_chunk(0, sa, na, cc * 128)
                        proj_chunk(1, sga, nga, cc * 128)
                        t0 = TS[st][0]
                        op("act", lambda e, t0=t0: e.activation(out=t0, in_=B[1][:], func=AF.Sigmoid),
                           reads=["B1"], writes=[f"TS{st}_0"])
                        t1 = TS[st][1]
                        op("dve", lambda e, t0=t0, t1=t1: e.tensor_tensor(out=t1, in0=t0, in1=B[0][:], op=ALU.mult),
                           reads=[f"TS{st}_0", "B0"], writes=[f"TS{st}_1"])
                        op("dve", lambda e, t1=t1, st=st: e.tensor_copy(out=ktb[st][:, 0:16], in_=t1[:, ::-1][:, 0:16]),
                           reads=[f"TS{st}_1"], writes=[f"ktb{st}"])
                        op("sp", lambda e, st=st, c=c: e.dma_start(
                            out=upre_d[c * 128:(c + 1) * 128, 16 + Lc:32 + Lc], in_=ktb[st][:, 0:16]),
                           reads=[f"ktb{st}"], writes=["upre_pad1"], dma=True)
            for hgp in range(2):
                sz, nz = wpiece(8 + hgp)

                def zsrc(hi, sz=sz, nz=nz):
                    proj_chunk(0, sz, nz, hi * 128)
                    return B[0][:], "B0"

                scan_group(0, hgp, zsrc, None, None, pidx=1, state_only=True)

        def phase_b(i, outs):
            rows = slice(i * T, (i + 1) * T)
            P.handoff(ARENA_XN + ARENA_HID + ARENA_SCAN, ARENA_CONV)
            ur = upre_d.rearrange("(c p) l -> p c l", p=128)
            deps = [f"upre_d{j}_{c}" for j in (i - 1, i, i + 1) if 0 <= j < NT for c in range(8)] + ["upre_pad0", "upre_pad1"]
            op("sp", lambda e: e.dma_start(out=upad, in_=ur[:, :, 1 + i * T:1 + i * T + 542]),
               reads=deps, writes=["upad"], dma=True)
            def ln_stats(c):
                sq = cst[c % 2]
                op("pe", lambda e, c=c: e.matmul(B[0][:], onesF[:], uc[:, c, :], start=(c == 0), stop=(c == 7)),
                   reads=["onesF", f"uc{c}"], writes=["B0"])
                op("act", lambda e, c=c, sq=sq: e.activation(out=sq, in_=uc[:, c, :], func=AF.Square),
                   reads=[f"uc{c}"], writes=[f"cst{c % 2}"])
                op("pe", lambda e, c=c, sq=sq: e.matmul(B[1][:], onesF[:], sq, start=(c == 0), stop=(c == 7)),
                   reads=["onesF", f"cst{c % 2}"], writes=["B1"])

            for c in range(8):
                bank = 2 + c % 2
                for j in range(31):
                    k = (c * 31 + j) % 4
                    scale_copy("act" if (c * 31 + j) % 2 else "dve", dg[k], identB[:], cw[:, c, j:j + 1],
                               ["identB", "cw"], [f"dg{k}"])
                    op("pe", lambda e, c=c, j=j, k=k, bank=bank: e.matmul(
                        B[bank][:], dg[k], upad[:, c, j:j + T], start=(j == 0), stop=(j == 30)),
                       reads=[f"dg{k}", "upad"], writes=[f"B{bank}"])
                    if j == 8 and c > 0:
                        ln_stats(c - 1)
                op("act", lambda e, c=c, bank=bank: e.activation(out=uc[:, c, :], in_=B[bank][:], func=AF.Identity,
                                                                 scale=1.0, bias=CB[:, c:c + 1]),
                   reads=[f"B{bank}", "cols"], writes=[f"uc{c}"])
            ln_stats(7)
            mean, var, rs = cst[2], cst[3], cst[4]
            op("dve", lambda e: e.tensor_scalar(out=mean, in0=B[0][:], scalar1=1.0 / 1024, scalar2=None, op0=ALU.mult),
               reads=["B0"], writes=["cst2"])
            op("dve", lambda e: e.tensor_tensor(out=var, in0=mean, in1=mean, op=ALU.mult), reads=["cst2"], writes=["cst3"])
            op("dve", lambda e: e.scalar_tensor_tensor(out=var, in0=B[1][:], scalar=1.0 / 1024, in1=var,
                                                       op0=ALU.mult, op1=ALU.subtract),
               reads=["B1", "cst3"], writes=["cst3"])
            op("act", lambda e: e.activation(out=rs, in_=var, func=AF.Sqrt, scale=1.0, bias=epsc[:, 0:1]),
               reads=["cst3", "epsc"], writes=["cst4"])
            op("dve", lambda e: e.reciprocal(out=rs, in_=rs), reads=["cst4"], writes=["cst4"])
            for c in range(8):
                op("dve", lambda e, c=c: e.tensor_tensor(out=uc[:, c, :], in0=uc[:, c, :], in1=mean, op=ALU.subtract),
                   reads=[f"uc{c}", "cst2"], writes=[f"uc{c}"])
                op("dve", lambda e, c=c: e.tensor_tensor(out=uc[:, c, :], in0=uc[:, c, :], in1=rs, op=ALU.mult),
                   reads=[f"uc{c}", "cst4"], writes=[f"uc{c}"])
                op("act", lambda e, c=c: e.activation(out=hT[:, c, :], in_=uc[:, c, :], func=AF.Silu,
                                                      scale=LNW[:, c:c + 1], bias=LNB[:, c:c + 1]),
                   reads=[f"uc{c}", "cols"], writes=[HT[c]])
            for tb in range(4):
                op("sp", lambda e, tb=tb: e.dma_start(out=xt[:, tb, :], in_=x1_d[i * T + tb * 128:i * T + (tb + 1) * 128, :]),
                   reads=[f"x1_d{i}_{tb}"], writes=[f"xt{tb}"], dma=True)
            P.handoff(ARENA_XN + ARENA_HID + ARENA_CONV, ARENA_SCAN)
            op("sp", lambda e: e.dma_start(out=vtok[:], in_=v_d[rows, :].rearrange("(t p) f -> p t f", p=128)),
               reads=[f"v_d{i}"], writes=["vtok"], dma=True)
            for hgp in range(2):
                def zsrc(hi, hgp=hgp):
                    h = hgp * 4 + hi
                    st = hi % 2
                    t0 = TS[st][0]
                    op("sp", lambda e, t0=t0, h=h: e.dma_start(out=t0, in_=fm(zb_d, h, i)),
                       reads=[f"zb_d{i}_{h}"], writes=[f"TS{st}_0"], dma=True)
                    return t0, f"TS{st}_0"

                def qsrc(hi, hgp=hgp):
                    h = hgp * 4 + hi
                    st = hi % 2
                    t4 = TS[st][4]
                    op("sp", lambda e, t4=t4, h=h: e.dma_start(out=t4, in_=fm(q_d, h, i)),
                       reads=[f"q_d{i}_{h}"], writes=[f"TS{st}_4"], dma=True)
                    return t4, f"TS{st}_4"

                def finish(hi, h, ps, psn):
                    st = hi % 2
                    t5, t6 = TS[st][5], TS[st][6]
                    n5, n6 = f"TS{st}_5", f"TS{st}_6"
                    o_ = osb[st]
                    on = f"osb{st}"
                    op("sp", lambda e, t5=t5, h=h: e.dma_start(out=t5, in_=fm(of_d, h, i)),
                       reads=[f"of_d{i}_{h}"], writes=[n5], dma=True)
                    op("sp", lambda e, t6=t6, h=h: e.dma_start(out=t6, in_=fm(og_d, h, i)),
                       reads=[f"og_d{i}_{h}"], writes=[n6], dma=True)
                    op("dve", lambda e, o_=o_, t5=t5: e.tensor_tensor(out=o_, in0=ps[:], in1=t5, op=ALU.add),
                       reads=[psn, n5], writes=[on])
                    op("act", lambda e, o_=o_, t5=t5: e.activation(out=t5, in_=o_, func=AF.Square), reads=[on], writes=[n5])
                    op("pe", lambda e, t5=t5: e.matmul(B[0][:], onesF[:], t5, start=True, stop=True),
                       reads=["onesF", n5], writes=["B0"])
                    op("act", lambda e, t5=t5: e.activation(out=t5, in_=B[0][:], func=AF.Sqrt, scale=1.0 / 128,
                                                            bias=epsc[:, 0:1]), reads=["B0", "epsc"], writes=[n5])
                    op("dve", lambda e, t5=t5: e.reciprocal(out=t5, in_=t5), reads=[n5], writes=[n5])
                    op("dve", lambda e, o_=o_, t5=t5: e.tensor_tensor(out=o_, in0=o_, in1=t5, op=ALU.mult),
                       reads=[on, n5], writes=[on])
                    op("dve", lambda e, o_=o_, t6=t6, h=h: e.scalar_tensor_tensor(
                        out=hT[:, 8 + h, :], in0=o_, scalar=HGW[:, 0:1], in1=t6, op0=ALU.mult, op1=ALU.mult),
                       reads=[on, n6, "cols"], writes=[HT[8 + h]])

                scan_group(1, hgp, zsrc, qsrc, finish)
            wo = W["w_out"][0].rearrange("(kc p) f -> p kc f", p=128)
            for n in range(4):
                sv, sn = Wr.get(("w_out", n), wo[:, :, n * 512:(n + 1) * 512], [128, 16, 512])
                for tb in range(4):
                    bank = 4 + tb
                    for kc in range(16):
                        op("pe", lambda e, kc=kc, tb=tb, bank=bank, sv=sv: e.matmul(
                            B[bank][:], hT[:, kc, tb * 128:(tb + 1) * 128], sv[:, kc, :], start=(kc == 0), stop=(kc == 15)),
                           reads=[sn, HT[kc]], writes=[f"B{bank}"])
                    op("dve", lambda e, tb=tb, n=n, bank=bank: e.tensor_tensor(
                        out=xt[:, tb, n * 512:(n + 1) * 512], in0=B[bank][:], in1=xt[:, tb, n * 512:(n + 1) * 512], op=ALU.add),
                       reads=[f"B{bank}", f"xt{tb}"], writes=[f"xt{tb}"])
            P.handoff(ARENA_HID + ARENA_SCAN + ARENA_CONV, ARENA_XN)
            rmsnorm_hT(G3)
            ffn("ffn2")
            op("dve", lambda e: e.memset(ss[:], 0.0), writes=["ss"])
            for tb in range(4):
                op("act", lambda e, tb=tb: e.activation(
                    out=vtok[:, (tb % 2) * 2:(tb % 2) * 2 + 2, :].rearrange("p a b -> p (a b)"), in_=xt[:, tb, :],
                    func=AF.Square, accum_out=ss[:, tb:tb + 1]),
                   reads=[f"xt{tb}", "ss"], writes=["ss", "vtok"])
            op("act", lambda e: e.activation(out=rstd[:], in_=ss[:], func=AF.Sqrt, scale=1.0 / D, bias=epsc[:, 0:1]),
               reads=["ss", "epsc"], writes=["rstd"])
            op("dve", lambda e: e.reciprocal(out=rstd[:], in_=rstd[:]), reads=["rstd"], writes=["rstd"])
            for tb in range(4):
                op("dve", lambda e, tb=tb: e.scalar_tensor_tensor(
                    out=xt[:, tb, :], in0=xt[:, tb, :], scalar=rstd[:, tb:tb + 1], in1=fwb[:], op0=ALU.mult, op1=ALU.mult),
                   reads=[f"xt{tb}", "rstd", "fwb"], writes=[f"xt{tb}"])
            for tb in range(4):
                o = op("sp", lambda e, tb=tb: e.dma_start(out=y_d[i * T + tb * 128:i * T + (tb + 1) * 128, :], in_=xt[:, tb, :]),
                       reads=[f"xt{tb}"], writes=[f"y_d{i}_{tb}"], dma=True)
                outs.append(o)

        outs = []
        setup()
        for t in range(NTu):
            upstream(t)
        for i in range(NT):
            phase_a(i)
        for i in range(NT - 1, -1, -1):
            phase_b(i, outs)
        return Wr, outs

    Pd = Prog(nc, dry=True)
    Wr0, _ = make(Pd, None)
    order = Wr0.rec
    P = Prog(nc)
    Wr, outs = make(P, order)
    assert Wr.cur == len(order)
    P.finalize(final_waits=outs)
    return nc, P


_CACHE = {}


def run_cores(seqs, weights, NT, ups=None, wvariant=None):
    NTu = 0 if ups is None else ups[0].shape[0] // T
    key = (NT, NTu)
    if key not in _CACHE:
        _CACHE[key] = build(NT, NTu)[0]
    nc = _CACHE[key]
    wmap = {n: np.ascontiguousarray(np.asarray(weights[n], dtype=np.float32)) for n, _ in WNAMES}
    in_maps = []
    for c, s in enumerate(seqs):
        m = dict(wmap)
        if wvariant is not None and wvariant[c]:
            m.update(wvariant[c])
        m["x"] = np.ascontiguousarray(s, dtype=np.float32)
        if NTu:
            m["xu"] = np.ascontiguousarray(ups[c], dtype=np.float32)
        in_maps.append(m)
    res = run_bass_kernel_spmd(nc, in_maps, core_ids=list(range(8)))
    return [r["y"] for r in res.results]


def mirror_weights(weights):
    w_in = np.asarray(weights["w_in"], dtype=np.float32)
    wm = w_in.copy()
    wm[:, :, 3072:4096] = w_in[:, :, 4096:5120]
    wm[:, :, 4096:5120] = w_in[:, :, 3072:4096]
    return {
        "w_in": wm,
        "lb_fwd": np.ascontiguousarray(np.asarray(weights["lb_bwd"], dtype=np.float32)),
        "lb_bwd": np.ascontiguousarray(np.asarray(weights["lb_fwd"], dtype=np.float32)),
        "conv_w": np.ascontiguousarray(np.asarray(weights["conv_w"], dtype=np.float32)[:, ::-1, :]),
    }


def split_layout(xseqs, NT):
    No = NT * T
    owns, ups, mir = [], [], []
    for x in xseqs:
        L = x.shape[0]
        H = L // 2
        assert H <= No
        a = np.zeros((No, D), np.float32)
        a[No - H:] = x[:H]
        ua = np.zeros((No, D), np.float32)
        ua[No - H:] = x[H:][::-1]
        b = np.zeros((No, D), np.float32)
        b[No - H:] = x[H:][::-1]
        ub = np.zeros((No, D), np.float32)
        ub[No - H:] = x[:H]
        owns += [a, b]
        ups += [ua, ub]
        mir += [False, True]
    return owns, ups, mir


def run_split(xseqs, weights, NT):
    owns, ups, mir = split_layout(xseqs, NT)
    mw = mirror_weights(weights)
    ys = run_cores(owns, weights, NT, ups=ups, wvariant=[mw if m else None for m in mir])
    No = NT * T
    outs = []
    for k, x in enumerate(xseqs):
        H = x.shape[0] // 2
        ya = ys[2 * k][No - H:]
        yb = ys[2 * k + 1][No - H:][::-1]
        outs.append(np.concatenate([ya, yb], 0).astype(np.float32))
    return outs


def kernel(**inputs):
    xp = np.asarray(inputs["x_prompt"], dtype=np.float32)
    xs = np.asarray(inputs["x_sample"], dtype=np.float32)
    outs = run_split([xp[0], xp[1], xs[0], xs[1]], inputs, 8)
    y_prompt = np.stack([outs[0], outs[1]], 0)
    y_sample = np.stack([outs[2], outs[3]], 0)
    return (y_prompt, y_sample)
```
